# Optimizing a Trainium2 kernel written in Bass

```python
import math
import jax
import jax.numpy as jnp
from jax import lax
import numpy as np

D_MODEL = 1024
BATCH = 2
SEQ = 16384
DEPTH = 2

GRID_W = 64
CTX_LEN = 256
HEAD_DIM = 64
A_HEADS = 6
A_KV = 2
B_HEADS = 4
B_QK_DIM = HEAD_DIM // 2
C_HEADS = 6
C_KV = 2
MIX_WIDTH = (A_HEADS + B_HEADS + C_HEADS) * HEAD_DIM
IN_WIDTHS = (A_HEADS * HEAD_DIM, A_KV * HEAD_DIM, A_KV * HEAD_DIM,
             B_HEADS * HEAD_DIM, B_HEADS * HEAD_DIM, B_HEADS * HEAD_DIM,
             C_HEADS * HEAD_DIM, C_KV * HEAD_DIM, C_KV * HEAD_DIM)
IN_WIDTH = sum(IN_WIDTHS)
D_FF = -(-(8 * D_MODEL) // (3 * 256)) * 256
WINDOW = 128
Q_BLOCK = 128
ROPE_THETA = 10000.0
EPS = 1e-6
NEG_INF = -1e30

kernel_name = 'hybrid_parallel_heads_dit_block'


def rmsnorm(x, g):
    xf = x.astype(jnp.float32)
    y = xf * lax.rsqrt(jnp.mean(xf * xf, axis=-1, keepdims=True) + EPS)
    return (y * g.astype(jnp.float32)).astype(x.dtype)


def modulate(h, shift, scale):
    return h * (1 + scale) + shift


def split_columns(p):
    offsets = [int(o) for o in np.cumsum(IN_WIDTHS)[:-1]]
    return jnp.split(p, offsets, axis=-1)


def to_heads(t, n):
    b, s, _ = t.shape
    return t.reshape(b, s, n, -1).transpose(0, 2, 1, 3)


def merge_heads(o):
    b, h, s, d = o.shape
    return o.transpose(0, 2, 1, 3).reshape(b, s, h * d)


def group(q, g):
    b, h, s, d = q.shape
    return q.reshape(b, g, h // g, s, d)


def ungroup(o):
    b, g, r, s, d = o.shape
    return o.reshape(b, g * r, s, d)


def rope_1d(x, pos):
    half = x.shape[-1] // 2
    freqs = ROPE_THETA ** (-jnp.arange(half, dtype=jnp.float32) / half)
    ang = pos.astype(jnp.float32)[:, None] * freqs[None, :]
    cos = jnp.cos(ang).astype(x.dtype)
    sin = jnp.sin(ang).astype(x.dtype)
    x1, x2 = x[..., :half], x[..., half:]
    return jnp.concatenate([x1 * cos - x2 * sin, x1 * sin + x2 * cos], axis=-1)


def rope_2d(x, rows, cols):
    h = x.shape[-1] // 2
    return jnp.concatenate([rope_1d(x[..., :h], rows), rope_1d(x[..., h:], cols)], axis=-1)


def scores(q, k, scale):
    return jnp.einsum('bgrqd,bgkd->bgrqk', q, k, preferred_element_type=jnp.float32) * scale


def weigh(p, v):
    return jnp.einsum('bgrqk,bgkd->bgrqd', p.astype(v.dtype), v)


def sweep_blocks(fn, *qs):
    def split(q):
        b, g, r, s, d = q.shape
        return jnp.moveaxis(q.reshape(b, g, r, s // Q_BLOCK, Q_BLOCK, d), 3, 0)
    out = lax.map(lambda blk: fn(*blk), tuple(split(q) for q in qs))
    nb, b, g, r, qb, dv = out.shape
    return jnp.moveaxis(out, 0, 3).reshape(b, g, r, nb * qb, dv)


def window_attention(q, k, v, kc, vc, sink, scale):
    b, g, r, s, d = q.shape
    nb = s // Q_BLOCK

    def band(t):
        tp = jnp.pad(t, ((0, 0), (0, 0), (Q_BLOCK, Q_BLOCK), (0, 0)))
        tp = tp.reshape(b, g, nb + 2, Q_BLOCK, t.shape[-1])
        blocks = jnp.concatenate([tp[:, :, :-2], tp[:, :, 1:-1], tp[:, :, 2:]], axis=3)
        return jnp.moveaxis(blocks, 2, 0)

    kb, vb = band(k), band(v)
    qb = jnp.moveaxis(q.reshape(b, g, r, nb, Q_BLOCK, d), 3, 0)
    qpos = jnp.arange(nb)[:, None] * Q_BLOCK + jnp.arange(Q_BLOCK)[None, :]
    kpos = jnp.arange(nb)[:, None] * Q_BLOCK - Q_BLOCK + jnp.arange(3 * Q_BLOCK)[None, :]
    valid = ((jnp.abs(qpos[:, :, None] - kpos[:, None, :]) <= WINDOW)
             & (kpos >= 0)[:, None, :] & (kpos < s)[:, None, :])
    sink_col = jnp.broadcast_to(sink.astype(jnp.float32)[None, :, :, None, None], (b, g, r, Q_BLOCK, 1))

    def one(args):
        qblk, kblk, vblk, vmask = args
        s_loc = jnp.where(vmask, scores(qblk, kblk, scale), NEG_INF)
        s_all = jnp.concatenate([s_loc, scores(qblk, kc, scale), sink_col], axis=-1)
        p = jax.nn.softmax(s_all, axis=-1)
        return weigh(p[..., :3 * Q_BLOCK], vblk) + weigh(p[..., 3 * Q_BLOCK:-1], vc)

    out = lax.map(one, (qb, kb, vb, valid))
    return jnp.moveaxis(out, 0, 3).reshape(b, g, r, s, -1)


def diff_lambda(lam_q1, lam_k1, lam_q2, lam_k2, lam_init):
    f = lambda a, c: jnp.exp(jnp.sum(a.astype(jnp.float32) * c.astype(jnp.float32)))
    return f(lam_q1, lam_k1) - f(lam_q2, lam_k2) + lam_init


def token_mixers(hx, hc, w_in, g_q, g_k, lam_q1, lam_k1, lam_q2, lam_k2, lam_init, g_subln, sink,
                 rows, cols, with_ctx_out):
    axq, axk, axv, bxq, bxk, bxv, cxq, cxk, cxv = split_columns(hx @ w_in)
    acq, ack, acv, bcq, bck, bcv, ccq, cck, ccv = split_columns(hc @ w_in)
    pos = lambda t: rope_2d(t, rows, cols)

    sa = HEAD_DIM ** -0.5
    qa = pos(rmsnorm(to_heads(axq, A_HEADS), g_q))
    ka = pos(rmsnorm(to_heads(axk, A_KV), g_k))
    qa_c = rmsnorm(to_heads(acq, A_HEADS), g_q)
    ka_c = rmsnorm(to_heads(ack, A_KV), g_k)
    va_c = to_heads(acv, A_KV)
    ka_all = jnp.concatenate([ka, ka_c], axis=2)
    va_all = jnp.concatenate([to_heads(axv, A_KV), va_c], axis=2)
    oa = ungroup(sweep_blocks(
        lambda qblk: weigh(jax.nn.softmax(scores(qblk, ka_all, sa), axis=-1), va_all),
        group(qa, A_KV)))

    sb = B_QK_DIM ** -0.5
    lam = diff_lambda(lam_q1, lam_k1, lam_q2, lam_k2, lam_init)
    qb_x = to_heads(bxq, B_HEADS)
    kb_x = to_heads(bxk, B_HEADS)
    qb_c = to_heads(bcq, B_HEADS)
    kb_c = to_heads(bck, B_HEADS)
    vb_c = to_heads(bcv, B_HEADS)
    q1, q2 = pos(qb_x[..., :B_QK_DIM]), pos(qb_x[..., B_QK_DIM:])
    k1_all = jnp.concatenate([pos(kb_x[..., :B_QK_DIM]), kb_c[..., :B_QK_DIM]], axis=2)
    k2_all = jnp.concatenate([pos(kb_x[..., B_QK_DIM:]), kb_c[..., B_QK_DIM:]], axis=2)
    vb_all = jnp.concatenate([to_heads(bxv, B_HEADS), vb_c], axis=2)

    def diff_block(qb1, qb2):
        p = (jax.nn.softmax(scores(qb1, k1_all, sb), axis=-1)
             - lam * jax.nn.softmax(scores(qb2, k2_all, sb), axis=-1))
        return weigh(p, vb_all)

    ob = sweep_blocks(diff_block, group(q1, B_HEADS), group(q2, B_HEADS))
    ob = rmsnorm(ungroup(ob), g_subln) * (1.0 - lam_init)

    sc = HEAD_DIM ** -0.5
    sink_gr = sink.reshape(C_KV, C_HEADS // C_KV)
    qc_x = pos(to_heads(cxq, C_HEADS))
    kc_x = pos(to_heads(cxk, C_KV))
    vc_x = to_heads(cxv, C_KV)
    qc_c = to_heads(ccq, C_HEADS)
    kc_c = to_heads(cck, C_KV)
    vc_c = to_heads(ccv, C_KV)
    oc = ungroup(window_attention(group(qc_x, C_KV), kc_x, vc_x, kc_c, vc_c, sink_gr, sc))

    ox = merge_heads(jnp.concatenate([oa, ob, oc], axis=1))
    if not with_ctx_out:
        return ox, None

    oa_c = ungroup(weigh(jax.nn.softmax(scores(group(qa_c, A_KV), ka_c, sa), axis=-1), va_c))
    p_b = (jax.nn.softmax(scores(group(qb_c[..., :B_QK_DIM], B_HEADS), kb_c[..., :B_QK_DIM], sb), axis=-1)
           - lam * jax.nn.softmax(scores(group(qb_c[..., B_QK_DIM:], B_HEADS), kb_c[..., B_QK_DIM:], sb), axis=-1))
    ob_c = rmsnorm(ungroup(weigh(p_b, vb_c)), g_subln) * (1.0 - lam_init)
    s_cc = scores(group(qc_c, C_KV), kc_c, sc)
    sink_col = jnp.broadcast_to(sink_gr.astype(jnp.float32)[None, :, :, None, None], s_cc.shape[:-1] + (1,))
    p_c = jax.nn.softmax(jnp.concatenate([s_cc, sink_col], axis=-1), axis=-1)
    oc_c = ungroup(weigh(p_c[..., :-1], vc_c))
    return ox, merge_heads(jnp.concatenate([oa_c, ob_c, oc_c], axis=1))


def swiglu(h, w1, w3, w2):
    return (jax.nn.silu(h @ w1) * (h @ w3)) @ w2


def setup_inputs(seed: int = 0) -> dict:
    key = jax.random.key(seed)
    ks = jax.random.split(key, 24)
    D = D_MODEL

    def nrm(k, shape, scale):
        return jax.random.normal(k, shape, jnp.float32) * scale

    return {
        'x': nrm(ks[0], (BATCH, SEQ, D), 1.0),
        'c': nrm(ks[1], (BATCH, D), 1.0),
        'ctx': nrm(ks[2], (BATCH, CTX_LEN, D), 1.0),
        'c_ctx': nrm(ks[3], (D,), 1.0),
        'w_ada': nrm(ks[4], (DEPTH, D, 6 * D), 0.5 * D ** -0.5),
        'b_ada': nrm(ks[5], (DEPTH, 6 * D), 0.02),
        'g_attn': 1.0 + nrm(ks[6], (DEPTH, D), 0.05),
        'g_ffn': 1.0 + nrm(ks[7], (DEPTH, D), 0.05),
        'w_in': nrm(ks[8], (DEPTH, D, IN_WIDTH), D ** -0.5),
        'g_q': 1.0 + nrm(ks[9], (DEPTH, HEAD_DIM), 0.05),
        'g_k': 1.0 + nrm(ks[10], (DEPTH, HEAD_DIM), 0.05),
        'lam_q1': nrm(ks[11], (DEPTH, B_QK_DIM), 0.1),
        'lam_k1': nrm(ks[12], (DEPTH, B_QK_DIM), 0.1),
        'lam_q2': nrm(ks[13], (DEPTH, B_QK_DIM), 0.1),
        'lam_k2': nrm(ks[14], (DEPTH, B_QK_DIM), 0.1),
        'g_subln': 1.0 + nrm(ks[15], (DEPTH, HEAD_DIM), 0.05),
        'sink_logit': nrm(ks[16], (DEPTH, C_HEADS), 0.5),
        'w_out': nrm(ks[17], (DEPTH, MIX_WIDTH, D), MIX_WIDTH ** -0.5),
        'w_ff1': nrm(ks[18], (DEPTH, D, D_FF), D ** -0.5),
        'w_ff3': nrm(ks[19], (DEPTH, D, D_FF), D ** -0.5),
        'w_ff2': nrm(ks[20], (DEPTH, D_FF, D), D_FF ** -0.5),
        'g_final': 1.0 + nrm(ks[21], (D,), 0.05),
    }


def reference(x, c, ctx, c_ctx, w_ada, b_ada, g_attn, g_ffn, w_in, g_q, g_k, lam_q1, lam_k1, lam_q2, lam_k2,
              g_subln, sink_logit, w_out, w_ff1, w_ff3, w_ff2, g_final):
    n_tok = x.shape[1]
    ROWS = n_tok // GRID_W
    rows = jnp.repeat(jnp.arange(ROWS, dtype=jnp.int32), GRID_W)
    cols = jnp.tile(jnp.arange(GRID_W, dtype=jnp.int32), ROWS)
    silu_c = jax.nn.silu(c)
    silu_cc = jax.nn.silu(c_ctx)
    for layer in range(DEPTH):
        last = layer == DEPTH - 1
        lam_init = 0.8 - 0.6 * math.exp(-0.3 * layer)
        mx = jnp.split((silu_c @ w_ada[layer] + b_ada[layer])[:, None, :], 6, axis=-1)
        mc = jnp.split(silu_cc @ w_ada[layer] + b_ada[layer], 6, axis=-1)
        hx = modulate(rmsnorm(x, g_attn[layer]), mx[0], mx[1])
        hc = modulate(rmsnorm(ctx, g_attn[layer]), mc[0], mc[1])
        ox, octx = token_mixers(hx, hc, w_in[layer], g_q[layer], g_k[layer], lam_q1[layer], lam_k1[layer],
                                lam_q2[layer], lam_k2[layer], lam_init, g_subln[layer], sink_logit[layer],
                                rows, cols, not last)
        x = x + mx[2] * (ox @ w_out[layer])
        x = x + mx[5] * swiglu(modulate(rmsnorm(x, g_ffn[layer]), mx[3], mx[4]),
                               w_ff1[layer], w_ff3[layer], w_ff2[layer])
        if not last:
            ctx = ctx + mc[2] * (octx @ w_out[layer])
            ctx = ctx + mc[5] * swiglu(modulate(rmsnorm(ctx, g_ffn[layer]), mc[3], mc[4]),
                                       w_ff1[layer], w_ff3[layer], w_ff2[layer])
    return rmsnorm(x, g_final)
```

```python
import contextlib
import math
import numpy as np
import ml_dtypes
import concourse.bass as bass
import concourse.mybir as mybir
from concourse.bass_utils import run_bass_kernel_spmd

F32 = mybir.dt.float32
BF16 = mybir.dt.bfloat16
AF = mybir.ActivationFunctionType
ALU = mybir.AluOpType
AX = mybir.AxisListType

D = 1024
SEQ = 16384
NLAT = 4096
NCTX = 256
NTOK = NLAT + NCTX
NTILE = NTOK // 128
DFF = 2816
NFF = DFF // 128
DEPTH = 2
EPS = 1e-6
AQ, AK, AV, BQ, BK, BV, CQ, CK, CV = [(0, 384), (384, 512), (512, 640), (640, 896), (896, 1152), (1152, 1408),
                                      (1408, 1792), (1792, 1920), (1920, 2048)]
WIN_ORDER = [AQ, AK, BQ, BK, CQ, CK, AV, BV, CV]


class Sched:
    def __init__(self, nc):
        self.nc = nc
        self.eng = {"pe": nc.tensor, "act": nc.scalar, "dve": nc.vector, "pool": nc.gpsimd, "sp": nc.sync}
        self.stack = contextlib.ExitStack()
        self.scopes = []
        self.sems = {}
        self.cnt = {}
        self.waited = {e: {} for e in self.eng}
        self.last_w = {}
        self.readers = {}
        for e in self.eng:
            self._sem("E_" + e)
        self.n_inst = 0
        self.n_wait = 0
        self.uid = 0

    def _sem(self, name):
        if name not in self.sems:
            self.sems[name] = self.stack.enter_context(self.nc.semaphore(name))
            self.cnt[name] = 0
        return self.sems[name]

    def push(self):
        st = contextlib.ExitStack()
        self.scopes.append(st)
        return st

    def pop(self):
        self.barrier()
        self.scopes.pop().close()

    def _ctx(self):
        return self.scopes[-1] if self.scopes else self.stack

    def sb(self, name, shape, dtype):
        self.uid += 1
        return self._ctx().enter_context(self.nc.sbuf_tensor("%s_%d" % (name, self.uid), list(shape), dtype))

    def ps(self, name, shape, dtype):
        self.uid += 1
        return self._ctx().enter_context(self.nc.psum_tensor("%s_%d" % (name, self.uid), list(shape), dtype))

    @staticmethod
    def _key(r):
        if isinstance(r, tuple):
            return tuple(Sched._key(x) for x in r)
        if isinstance(r, (str, int)):
            return r
        return id(r)

    def _deps(self, reads, writes):
        d = {}
        for r in reads:
            t = self.last_w.get(self._key(r))
            if t:
                d[t[0]] = max(d.get(t[0], 0), t[1])
        for w in writes:
            k = self._key(w)
            t = self.last_w.get(k)
            if t:
                d[t[0]] = max(d.get(t[0], 0), t[1])
            for t in self.readers.get(k, ()):
                d[t[0]] = max(d.get(t[0], 0), t[1])
        return d

    def _emit_waits(self, en, deps, skip_self=False):
        e = self.eng[en]
        wd = self.waited[en]
        for sn, v in deps.items():
            if skip_self and sn == "E_" + en:
                continue
            if wd.get(sn, 0) < v:
                e.wait_ge(self.sems[sn], v)
                wd[sn] = v
                self.n_wait += 1

    def _record(self, tok, reads, writes):
        for r in reads:
            self.readers.setdefault(self._key(r), []).append(tok)
        for w in writes:
            k = self._key(w)
            self.last_w[k] = tok
            self.readers[k] = []

    def op(self, en, fn, reads=(), writes=()):
        deps = self._deps(reads, writes)
        self._emit_waits(en, deps, skip_self=(en == "pe"))
        ins = fn(self.eng[en])
        sn = "E_" + en
        self.cnt[sn] += 1
        ins.then_inc(self.sems[sn], 1)
        self._record((sn, self.cnt[sn]), reads, writes)
        self.n_inst += 1
        return ins

    def dma(self, q, out, in_, reads=(), writes=(), sem=None, **kw):
        if sem is None:
            sem = "D_%x" % (hash(self._key(writes[0])) & 0xFFFFFFF)
        self._sem(sem)
        deps = self._deps(reads, writes)
        self._emit_waits(q, deps)
        ins = self.eng[q].dma_start(out=out, in_=in_, **kw)
        self.cnt[sem] += 16
        ins.then_inc(self.sems[sem], 16)
        self._record((sem, self.cnt[sem]), reads, writes)
        self.n_inst += 1
        return ins

    def collective(self, in_ap, out_ap, reads=(), writes=(), sem="CC"):
        self._sem(sem)
        deps = self._deps(reads, writes)
        self._emit_waits("pool", deps)
        ins = self.nc.gpsimd.collective_compute("AllGather", ALU.bypass, replica_groups=[[0, 1, 2, 3], [4, 5, 6, 7]],
                                                ins=[in_ap], outs=[out_ap])
        self.cnt[sem] += 1
        ins.then_inc(self.sems[sem], 1)
        self._record((sem, self.cnt[sem]), reads, writes)
        self.n_inst += 1
        return ins

    def barrier(self):
        allv = {sn: c for sn, c in self.cnt.items() if c > 0}
        for en in self.eng:
            self._emit_waits(en, allv)

    def finish(self):
        self.barrier()
        while self.scopes:
            self.scopes.pop().close()
        self.stack.close()


def emit_consts(s):
    identf = s.sb("identf", [128, 128], F32)
    identb = s.sb("identb", [128, 128], BF16)
    onesf = s.sb("onesf", [128, 128], F32)
    s.op("pool", lambda e: e.memset(identf[:], 0.0), writes=[identf])
    s.op("pool", lambda e: e.affine_select(out=identf[:], in_=identf[:], pattern=[[-1, 128]],
                                           compare_op=ALU.not_equal, fill=1.0, base=0, channel_multiplier=1),
         reads=[identf], writes=[identf])
    s.op("pool", lambda e: e.tensor_copy(out=identb[:], in_=identf[:]), reads=[identf], writes=[identb])
    s.op("pool", lambda e: e.memset(onesf[:], 1.0), writes=[onesf])
    return identf, identb, onesf


def emit_mod(s, cvec, stream, w_ada, b_ada, c0, ncols, mod, identf):
    s.push()
    cb = s.sb("cb", [128, D], F32)
    sc = s.sb("sc", [128, D], F32)
    scT = s.sb("scT", [128, 8, 128], F32)
    bb = s.sb("bb", [128, ncols], F32)
    wa = [s.sb("wa%d" % i, [128, 8, 512], F32) for i in range(2)]
    ptr = s.ps("ptr", [128, 4, 128], F32)
    pm = s.ps("pm", [128, 512], F32)
    s.dma("sp", cb[:], cvec[stream].partition_broadcast(128), writes=[cb])
    s.dma("sp", bb[:], b_ada[c0:c0 + ncols].partition_broadcast(128), writes=[bb])
    s.op("act", lambda e: e.activation(out=sc[:], in_=cb[:], func=AF.Silu), reads=[cb], writes=[sc])
    for kk in range(0, 8, 4):
        for j in range(4):
            s.op("pe", lambda e: e.transpose(ptr[:, j, :], sc[:, (kk + j) * 128:(kk + j + 1) * 128], identf[:]),
                 reads=[sc, identf], writes=[ptr])
        s.op("dve", lambda e: e.tensor_copy(out=scT[:, kk:kk + 4, :], in_=ptr[:]), reads=[], writes=[ptr, (scT, kk)])
    for cbk in range(ncols // 512):
        w = wa[cbk % 2]
        s.dma("sp", w[:], w_ada[:, c0 + cbk * 512:c0 + (cbk + 1) * 512].rearrange("(k p) n -> p k n", p=128), writes=[w])
        for k in range(8):
            s.op("pe", lambda e: e.matmul(pm[:], lhsT=scT[:, k, :], rhs=w[:, k, :], start=(k == 0), stop=(k == 7)),
                 reads=[(scT, 0), (scT, 4), w], writes=[pm])
        s.op("dve", lambda e: e.tensor_tensor(out=mod[:, cbk * 512:(cbk + 1) * 512], in0=pm[:],
                                              in1=bb[:, cbk * 512:(cbk + 1) * 512], op=ALU.add),
             reads=[bb], writes=[pm, (mod, cbk)])
    s.pop()


def emit_rstd(s, ss, rstd, n, reads, writes):
    s.op("act", lambda e: e.activation(out=rstd, in_=ss, func=AF.Ln, scale=1.0 / n, bias=EPS), reads=reads, writes=writes)
    s.op("act", lambda e: e.activation(out=rstd, in_=rstd, func=AF.Exp, scale=-0.5), reads=writes, writes=writes)


def emit_rope(s, src, src_res, C, S, tabres, w, out, out_res, t, u, src_is_psum):
    G = 512 // w
    d = w // 4
    srcg = src.rearrange("p (g c) -> p g c", g=G)
    src5 = src.rearrange("p (g t a d) -> p g t a d", g=G, t=2, a=2, d=d)
    u5 = u[:].rearrange("p (g t a d) -> p g t a d", g=G, t=2, a=2, d=d)
    S4 = S.rearrange("p (t a d) -> p t a d", t=2, a=2, d=d)
    rd = [] if src_is_psum else [src_res]
    wr = [src_res] if src_is_psum else []
    s.op("dve", lambda e: e.tensor_tensor(out=t[:].rearrange("p (g c) -> p g c", g=G), in0=srcg,
                                          in1=C.unsqueeze(1).broadcast_to([128, G, w]), op=ALU.mult),
         reads=rd + [tabres], writes=wr + [t])
    for a in range(2):
        s.op("dve", lambda e: e.tensor_tensor(out=u5[:, :, :, a, :], in0=src5[:, :, :, 1 - a, :],
                                              in1=S4[:, :, a, :].unsqueeze(1).broadcast_to([128, G, 2, d]), op=ALU.mult),
             reads=rd + [tabres], writes=wr + [(u, a)])
    s.op("pool", lambda e: e.tensor_tensor(out=out, in0=t[:], in1=u[:], op=ALU.add),
         reads=[t, (u, 0), (u, 1)], writes=[out_res])


def emit_pre(s, consts, xs, cvec, w_ada, b_ada, g_attn, w_in, g_qk, rope, qT, kT, vv, ckT, cvv):
    identf, identb, onesf = consts
    s.push()
    mods = []
    for stream in range(2):
        m = s.sb("mod%d" % stream, [128, 2 * D], F32)
        emit_mod(s, cvec, stream, w_ada, b_ada, 0, 2 * D, m, identf)
        mods.append(m)
    gat = s.sb("gat", [128, D], F32)
    s.dma("sp", gat[:], g_attn.partition_broadcast(128), writes=[gat])
    for m in mods:
        s.op("dve", lambda e: e.scalar_tensor_tensor(out=m[:, D:2 * D], in0=m[:, D:2 * D], scalar=1.0, in1=gat[:],
                                                     op0=ALU.add, op1=ALU.mult),
             reads=[gat, (m, 2), (m, 3)], writes=[(m, 2), (m, 3)])
    win = s.sb("win", [128, 8, 2048], BF16)
    c = 0
    for (a, b) in WIN_ORDER:
        s.dma("pool", win[:, :, c:c + (b - a)], w_in[:, a:b].rearrange("(k p) n -> p k n", p=128), writes=[(win, c)], sem="D_win")
        c += b - a
    win_res = [(win, cc) for cc in np.cumsum([0] + [b - a for (a, b) in WIN_ORDER[:-1]]).tolist()]
    gqk = s.sb("gqk", [128, 512], F32)
    for h in range(8):
        s.dma("sp", gqk[:, h * 64:(h + 1) * 64], g_qk[0 if h < 6 else 1].partition_broadcast(128), writes=[(gqk, h)], sem="D_gqk")
    gqk_res = [(gqk, h) for h in range(8)]
    ktacc = s.sb("ktacc", [128, 4, NLAT], BF16)
    vacc = s.sb("vacc", [128, 8, 32, 64], BF16)
    ktc = s.sb("ktc", [128, 4, NCTX], BF16)
    vc = s.sb("vc", [128, 8, 2, 64], BF16)
    xt = [s.sb("xt%d" % i, [128, D], F32) for i in range(2)]
    tb = [s.sb("tb%d" % i, [128, 192], F32) for i in range(2)]
    junk = s.sb("junk", [128, D], BF16)
    ss = [s.sb("ss%d" % i, [128, 1], F32) for i in range(2)]
    rstd = [s.sb("rstd%d" % i, [128, 1], F32) for i in range(2)]
    hb = [s.sb("hb%d" % i, [128, D], BF16) for i in range(2)]
    hT = [s.sb("hT%d" % i, [128, 8, 128], BF16) for i in range(2)]
    sqa = s.sb("sqa", [128, 512], F32)
    ssa = s.sb("ssa", [128, 8], F32)
    ra = s.sb("ra", [128, 8], F32)
    qa = s.sb("qa", [128, 512], F32)
    tt = s.sb("tt", [128, 512], F32)
    uu = s.sb("uu", [128, 512], F32)
    qkb = [s.sb("qkb%d" % i, [128, 1536], BF16) for i in range(2)]
    qst = [s.sb("qst%d" % i, [128, 8, 512], BF16) for i in range(2)]
    pT = s.ps("pT", [128, 8, 128], BF16)
    pq = s.ps("pq", [128, 4, 512], F32)
    ptq = [s.ps("ptq%d" % i, [128, 8, 128], BF16) for i in range(2)]

    for i in range(NTILE):
        p = i % 2
        isctx = i >= 32
        m = mods[1] if isctx else mods[0]
        s.dma("sp", xt[p][:], xs[i * 128:(i + 1) * 128, :], writes=[xt[p]])
        s.dma("sp", tb[p][:], rope[i * 128:(i + 1) * 128, :], writes=[tb[p]])
        s.op("act", lambda e: e.activation(out=junk[:], in_=xt[p][:], func=AF.Square, accum_out=ss[p][:]),
             reads=[xt[p]], writes=[junk, ss[p]])
        emit_rstd(s, ss[p][:], rstd[p][:], D, [ss[p]], [rstd[p]])
        s.op("dve", lambda e: e.scalar_tensor_tensor(out=xt[p][:], in0=xt[p][:], scalar=rstd[p][:, 0:1], in1=m[:, D:2 * D],
                                                     op0=ALU.mult, op1=ALU.mult),
             reads=[rstd[p], (m, 2), (m, 3)], writes=[xt[p]])
        s.op("pool", lambda e: e.tensor_tensor(out=hb[p][:], in0=xt[p][:], in1=m[:, 0:D], op=ALU.add),
             reads=[xt[p], (m, 0), (m, 1)], writes=[hb[p]])
        for k in range(8):
            s.op("pe", lambda e: e.transpose(pT[:, k, :], hb[p][:, k * 128:(k + 1) * 128], identb[:]),
                 reads=[hb[p], identb], writes=[pT])
        s.op("act", lambda e: e.copy(out=hT[p][:], in_=pT[:]), reads=[], writes=[pT, hT[p]])
        for cb in range(4):
            for k in range(8):
                s.op("pe", lambda e: e.matmul(pq[:, cb, :], lhsT=hT[p][:, k, :], rhs=win[:, k, cb * 512:(cb + 1) * 512],
                                              start=(k == 0), stop=(k == 7)),
                     reads=[hT[p]] + win_res, writes=[(pq, cb)])
        vdst = vc[:, :, i - 32, :] if isctx else vacc[:, :, i, :]
        s.op("act", lambda e: e.copy(out=vdst, in_=pq[:, 3, :].rearrange("p (u e) -> p u e", u=8)),
             reads=[], writes=[(pq, 3), ("vacc", i)])
        s.op("act", lambda e: e.activation(out=sqa[:], in_=pq[:, 0, :], func=AF.Square), reads=[], writes=[(pq, 0), sqa])
        s.op("dve", lambda e: e.reduce_sum(out=ssa[:], in_=sqa[:].rearrange("p (h c) -> p h c", h=8), axis=AX.X),
             reads=[sqa], writes=[ssa])
        emit_rstd(s, ssa[:], ra[:], 64, [ssa], [ra])
        s.op("dve", lambda e: e.tensor_tensor(out=qa[:].rearrange("p (h c) -> p h c", h=8),
                                              in0=pq[:, 0, :].rearrange("p (h c) -> p h c", h=8),
                                              in1=ra[:].unsqueeze(2).broadcast_to([128, 8, 64]), op=ALU.mult),
             reads=[ra], writes=[(pq, 0), qa])
        s.op("pool", lambda e: e.tensor_tensor(out=qa[:], in0=qa[:], in1=gqk[:], op=ALU.mult),
             reads=gqk_res, writes=[qa])
        emit_rope(s, qa[:], qa, tb[p][:, 0:64], tb[p][:, 64:128], tb[p], 64, qkb[p][:, 0:512], (qkb[p], 0), tt, uu, False)
        emit_rope(s, pq[:, 1, :], (pq, 1), tb[p][:, 128:160], tb[p][:, 160:192], tb[p], 32, qkb[p][:, 512:1024], (qkb[p], 1), tt, uu, True)
        emit_rope(s, pq[:, 2, :], (pq, 2), tb[p][:, 0:64], tb[p][:, 64:128], tb[p], 64, qkb[p][:, 1024:1536], (qkb[p], 2), tt, uu, True)
        gi = i // 4
        gp = gi % 2
        tl = i % 4
        for c12 in range(12):
            pt_, j = ptq[c12 // 8], c12 % 8
            s.op("pe", lambda e: e.transpose(pt_[:, j, :], qkb[p][:, c12 * 128:(c12 + 1) * 128], identb[:]),
                 reads=[(qkb[p], c12 // 4), identb], writes=[pt_])
        kdst = (lambda j0, n: ktc[:, j0:j0 + n, (i - 32) * 128:(i - 31) * 128]) if isctx else \
               (lambda j0, n: ktacc[:, j0:j0 + n, i * 128:(i + 1) * 128])
        qd = lambda j0, n: qst[gp][:, j0:j0 + n, tl * 128:(tl + 1) * 128]
        s.op("act", lambda e: e.copy(out=qd(0, 3), in_=ptq[0][:, 0:3, :]), reads=[], writes=[ptq[0], (qst[gp], tl, 0)])
        s.op("dve", lambda e: e.tensor_copy(out=kdst(0, 1), in_=ptq[0][:, 3:4, :]), reads=[], writes=[ptq[0], ("ktacc", i, 0)])
        s.op("act", lambda e: e.copy(out=qd(3, 2), in_=ptq[0][:, 4:6, :]), reads=[], writes=[ptq[0], (qst[gp], tl, 1)])
        s.op("dve", lambda e: e.tensor_copy(out=kdst(1, 2), in_=ptq[0][:, 6:8, :]), reads=[], writes=[ptq[0], ("ktacc", i, 1)])
        s.op("act", lambda e: e.copy(out=qd(5, 3), in_=ptq[1][:, 0:3, :]), reads=[], writes=[ptq[1], (qst[gp], tl, 2)])
        s.op("dve", lambda e: e.tensor_copy(out=kdst(3, 1), in_=ptq[1][:, 3:4, :]), reads=[], writes=[ptq[1], ("ktacc", i, 2)])
        ntl = 4 if gi < 8 else 2
        if tl == ntl - 1:
            t0 = gi * 512
            s.dma("sp", qT.rearrange("(c r) t -> r c t", r=128)[:, :, t0:t0 + ntl * 128], qst[gp][:, :, 0:ntl * 128],
                  reads=[(qst[gp], a, b) for a in range(ntl) for b in range(3)],
                  writes=[("qT", gi)] + [(qst[gp], a, b) for a in range(ntl) for b in range(3)], sem="D_qT")
    allk = [("ktacc", i, j) for i in range(NTILE) for j in range(3)]
    allv = [("vacc", i) for i in range(NTILE)]
    s.dma("sp", kT.rearrange("(c r) t -> r c t", r=128), ktacc[:], reads=allk, writes=["kT"], sem="D_kvout")
    s.dma("sp", ckT.rearrange("(c r) t -> r c t", r=128), ktc[:], reads=allk, writes=["ckT"], sem="D_kvout")
    for j in range(4):
        s.dma("sp", vv[j], vacc[:, 2 * j:2 * j + 2, :, :].rearrange("p u k e -> p (u k e)"), reads=allv, writes=["vv"], sem="D_kvout")
    s.dma("sp", cvv, vc[:].rearrange("p u k e -> p (u k e)"), reads=allv, writes=["cvv"], sem="D_kvout")
    s.pop()


def rope_tables(core):
    t = (core % 4) * NLAT + np.arange(NLAT)
    rows = (t // 64).astype(np.float32)
    cols = (t % 64).astype(np.float32)

    def tab(half):
        fr = (10000.0 ** (-np.arange(half, dtype=np.float32) / half)).astype(np.float32)
        ar = rows[:, None] * fr[None, :]
        ac = cols[:, None] * fr[None, :]
        cr, sr, cc, sc = np.cos(ar), np.sin(ar), np.cos(ac), np.sin(ac)
        C = np.concatenate([cr, cr, cc, cc], axis=1)
        S = np.concatenate([-sr, sr, -sc, sc], axis=1)
        return C.astype(np.float32), S.astype(np.float32)

    C64, S64 = tab(16)
    C32, S32 = tab(8)
    lat = np.concatenate([C64, S64, C32, S32], axis=1)
    ctx = np.zeros((NCTX, 192), np.float32)
    ctx[:, 0:64] = 1.0
    ctx[:, 128:160] = 1.0
    return np.ascontiguousarray(np.concatenate([lat, ctx], axis=0))


def cmask_table(core):
    j = np.arange(128)[:, None]
    i = np.arange(128)[None, :]
    mp = (j >= i).astype(np.float32)
    mn = (j <= i).astype(np.float32)
    one = np.ones((128, 128), np.float32)
    pats = np.zeros((8, 128, 512), np.float32)
    for t in range(6):
        for b in range(4):
            blk = mp if t == b else one if t == b + 1 else mn if t == b + 2 else None
            if blk is not None:
                pats[t, :, b * 128:(b + 1) * 128] = blk
    q = core % 4
    pats[6] = pats[0] if q > 0 else 0.0
    pats[7] = pats[5] if q < 3 else 0.0
    return np.ascontiguousarray(pats.transpose(1, 0, 2).reshape(128, 8 * 512)).astype(ml_dtypes.bfloat16)


def emit_attn(s, consts, layer, do_ctx, qT, kg, vgf, kT_own, v_own, ckT, cvv, sel, cmask, lamv, sinkv, gsub, oT):
    identf, identb, onesf = consts
    lam_init = 0.8 - 0.6 * math.exp(-0.3 * layer)
    s.push()
    lq = s.sb("lq", [128, 4, 32], F32)
    for i in range(4):
        s.dma("sp", lq[:, i, :], lamv[i].partition_broadcast(128), writes=[(lq, i)], sem="D_small")
    lp = s.sb("lp", [128, 2, 32], F32)
    ld = s.sb("ld", [128, 2], F32)
    neglam = s.sb("neglam", [128, 1], F32)
    s.op("dve", lambda e: e.tensor_tensor(out=lp[:], in0=lq[:, 0:4:2, :], in1=lq[:, 1:4:2, :], op=ALU.mult),
         reads=[(lq, i) for i in range(4)], writes=[lp])
    s.op("dve", lambda e: e.reduce_sum(out=ld[:], in_=lp[:], axis=AX.X), reads=[lp], writes=[ld])
    s.op("act", lambda e: e.activation(out=ld[:], in_=ld[:], func=AF.Exp), reads=[ld], writes=[ld])
    s.op("dve", lambda e: e.tensor_tensor(out=neglam[:], in0=ld[:, 1:2], in1=ld[:, 0:1], op=ALU.subtract), reads=[ld], writes=[neglam])
    s.op("dve", lambda e: e.tensor_scalar_add(out=neglam[:], in0=neglam[:], scalar1=-lam_init), reads=[neglam], writes=[neglam])
    sinkexp = s.sb("sinkexp", [128, 6], F32)
    s.dma("sp", sinkexp[:], sinkv.partition_broadcast(128), writes=[sinkexp], sem="D_small")
    s.op("act", lambda e: e.activation(out=sinkexp[:], in_=sinkexp[:], func=AF.Exp), reads=[sinkexp], writes=[sinkexp])
    gsubs = s.sb("gsubs", [64, 1], F32)
    s.dma("sp", gsubs[:], gsub.rearrange("(p o) -> p o", o=1), writes=[gsubs], sem="D_small")
    s.op("dve", lambda e: e.tensor_scalar_mul(out=gsubs[:], in0=gsubs[:], scalar1=1.0 - lam_init), reads=[gsubs], writes=[gsubs])
    cm = s.sb("cm", [128, 8, 512], BF16)
    s.dma("sp", cm[:], cmask.rearrange("p (m q) -> p m q", m=8), writes=[cm], sem="D_small")

    NKT = 130
    KT = [s.sb("KT%d" % i, [128, NKT * 128], BF16) for i in range(2)]
    VA = [s.sb("VA%d" % i, [128, NKT, 128], BF16) for i in range(2)]
    QB = [s.sb("QB%d" % i, [128, 3, 512], BF16) for i in range(2)]
    QBB = [s.sb("QBB%d" % i, [128, 2, 512], BF16) for i in range(2)]
    for i in range(2):
        s.op("pool", lambda e: e.memset(VA[i][:, :, 64:128], 1.0), writes=[("VAones", i)])
        s.op("pool", lambda e: e.memset(KT[i][64:128, :], 0.0), writes=[("KTz", i)])
        s.op("pool", lambda e: e.memset(QB[i][64:128, :, :], 0.0), writes=[("QBz", i)])
        s.op("pool", lambda e: e.memset(QBB[i][:], 0.0), writes=[("QB", "B", i)])
    PT = [s.sb("PT%d" % i, [128, 2, 512], BF16) for i in range(2)]
    rs = [s.sb("rs%d" % i, [128, 512], F32) for i in range(2)]
    bcs = [s.sb("bcs%d" % i, [64, 512], F32) for i in range(2)]
    t1 = s.sb("t1", [64, 512], F32)
    t2 = s.sb("t2", [64, 512], F32)
    od = s.sb("od", [64, 512], F32)
    sq = s.sb("sq", [64, 512], F32)
    rst = s.sb("rst", [64, 512], F32)
    OTs = [s.sb("OTs%d" % i, [64, 512], BF16) for i in range(2)]
    Sb = [s.ps("Sb%d" % i, [128, 2, 512], F32) for i in range(2)]
    Ob = [s.ps("Ob%d" % i, [128, 512], F32) for i in range(2)]
    Mb = [s.ps("Mb%d" % i, [128, 512], F32) for i in range(2)]

    def load_kv(buf, krow0, unit):
        for r in range(4):
            s.dma("sp", KT[buf][0:64, r * NLAT:(r + 1) * NLAT], kg(r, krow0), reads=[("kTg", krow0 // 128)], writes=[("KT", buf)],
                  sem="D_kv%d" % buf)
            s.dma("sp", VA[buf][:, r * 32:(r + 1) * 32, 0:64],
                  vgf(r, unit).rearrange("p (k e) -> p k e", e=64), reads=[("vg", unit // 2)], writes=[("VA", buf)], sem="D_kv%d" % buf)
        s.dma("sp", KT[buf][0:64, 4 * NLAT:4 * NLAT + NCTX], ckT[krow0:krow0 + 64, :], writes=[("KT", buf)], sem="D_kv%d" % buf)
        s.dma("sp", VA[buf][:, 128:130, 0:64], cvv[:, unit * 128:(unit + 1) * 128].rearrange("p (k e) -> p k e", e=64),
              writes=[("VA", buf)], sem="D_kv%d" % buf)

    candk = s.sb("candk", [64, 4, 128], BF16)
    candv = s.sb("candv", [128, 4, 64], BF16)
    acck = s.sb("acck", [64, 128], F32)
    accv = s.sb("accv", [128, 64], F32)
    selt = s.sb("selt", [128, 8], F32)
    s.dma("sp", selt[:], sel, writes=[selt], sem="D_small")

    def load_kv_c(buf, g):
        KC, VC = KT[buf], VA[buf]
        sem = "D_kv%d" % buf
        s.dma("sp", KC[0:64, 128:128 + NLAT], kT_own[384 + g * 64:384 + (g + 1) * 64, :], writes=[("KT", buf)], sem=sem)
        s.dma("sp", KC[0:64, 34 * 128:36 * 128], ckT[384 + g * 64:384 + (g + 1) * 64, :], writes=[("KT", buf)], sem=sem)
        s.dma("sp", VC[:, 1:33, 0:64], v_own(6 + g).rearrange("p (k e) -> p k e", e=64), writes=[("VA", buf)], sem=sem)
        s.dma("sp", VC[:, 34:36, 0:64], cvv[:, (6 + g) * 128:(7 + g) * 128].rearrange("p (k e) -> p k e", e=64), writes=[("VA", buf)], sem=sem)
        for side in range(2):
            for r in range(4):
                kcols = (NLAT - 128, NLAT) if side == 0 else (0, 128)
                s.dma("sp", candk[:, r, :], kg(r, 384 + g * 64)[:, kcols[0]:kcols[1]], reads=[("kTg", 3)], writes=[candk], sem="D_cand")
                vt = 31 if side == 0 else 0
                s.dma("sp", candv[:, r, :], vgf(r, 6 + g)[:, vt * 64:(vt + 1) * 64], reads=[("vg", 3)], writes=[candv], sem="D_cand")
            for (cand, acc, np_) in ((candk, acck, 64), (candv, accv, 128)):
                s.op("dve", lambda e: e.tensor_scalar(out=acc[:], in0=cand[:, 0, :], scalar1=selt[0:np_, 4 * side:4 * side + 1], scalar2=None,
                                                      op0=ALU.mult), reads=[cand, selt], writes=[acc])
                for r in range(1, 4):
                    s.op("dve", lambda e: e.scalar_tensor_tensor(out=acc[:], in0=cand[:, r, :], scalar=selt[0:np_, 4 * side + r:4 * side + r + 1],
                                                                 in1=acc[:], op0=ALU.mult, op1=ALU.add), reads=[cand, selt], writes=[acc])
            kc0 = 0 if side == 0 else 33 * 128
            s.op("dve", lambda e: e.tensor_copy(out=KC[0:64, kc0:kc0 + 128], in_=acck[:]), reads=[acck], writes=[("KT", buf)])
            s.op("dve", lambda e: e.tensor_copy(out=VC[:, 0 if side == 0 else 33, 0:64], in_=accv[:]), reads=[accv], writes=[("VA", buf)])

    def load_group(gi):
        kind, g = groups[gi]
        if kind == "C":
            load_kv_c(gi % 2, g)
        else:
            kk, uu = kv_spec(kind, g)
            load_kv(gi % 2, kk, uu)

    def bcast_row(dst_ps, row_ap, nq, rres):
        s.op("pe", lambda e: e.matmul(dst_ps[0:64, 0:nq], lhsT=onesf[64:65, 0:64], rhs=row_ap, start=True, stop=True),
             reads=[rres, onesf], writes=[dst_ps])

    groups = [("A", g) for g in range(2)] + [("B", h) for h in range(4)] + [("C", g) for g in range(2)]
    qblocks = [(j * 512, 512, list(range(128)) + [128, 129]) for j in range(8)]
    if do_ctx:
        qblocks.append((NLAT, NCTX, [128, 129]))

    def kv_spec(kind, g):
        if kind == "A":
            return g * 64, g
        return 128 + g * 64, 2 + g

    def q_rows(kind, g):
        return (g * 192, 3) if kind == "A" else (640 + g * 192, 3) if kind == "C" else (384 + g * 64, 1)

    cblocks = [(j * 512, 512, [4 * j + t for t in range(6)] + [34, 35]) for j in range(8)]
    if do_ctx:
        cblocks.append((NLAT, NCTX, [34, 35]))
    jobs = []
    for gi, (kind, g) in enumerate(groups):
        for bi, (q0, nq, tiles) in enumerate(cblocks if kind == "C" else qblocks):
            jb = dict(gi=gi, kind=kind, g=g, q0=q0, nq=nq, tiles=tiles, masks=None)
            if kind == "C" and nq == 512:
                pats = list(range(6))
                if bi == 0:
                    pats[0] = 6
                if bi == 7:
                    pats[5] = 7
                jb["masks"] = pats
            jobs.append(jb)

    def load_q(jn):
        jb = jobs[jn]
        r0, nh = q_rows(jb["kind"], jb["g"])
        q0, nq = jb["q0"], jb["nq"]
        if jb["kind"] in ("A", "C"):
            s.dma("sp", QB[jn % 2][0:64, 0:nh, 0:nq], qT[r0:r0 + nh * 64, q0:q0 + nq].rearrange("(h r) t -> r h t", r=64),
                  writes=[("QB", jn % 2)], sem="D_q%d" % (jn % 2))
        else:
            for k in range(2):
                s.dma("sp", QBB[jn % 2][32 * k:32 * k + 32, k, 0:nq], qT[r0 + 32 * k:r0 + 32 * k + 32, q0:q0 + nq],
                      writes=[("QB", "B", jn % 2)], sem="D_qb%d" % (jn % 2))

    steps = []
    ocnt = 0
    for jn, jb in enumerate(jobs):
        kind, nq, tiles = jb["kind"], jb["nq"], jb["tiles"]
        if kind in ("A", "C"):
            for r in range(3):
                ob = ocnt % 2
                ocnt += 1
                npair = len(tiles) // 2
                for pi in range(npair):
                    mk = None
                    if jb["masks"] is not None and pi < 3:
                        mk = (jb["masks"][2 * pi], jb["masks"][2 * pi + 1])
                    steps.append(dict(jn=jn, kind=kind, r=r, ob=ob, t=(tiles[2 * pi], tiles[2 * pi + 1]),
                                      first=(pi == 0), last=(pi == npair - 1), mk=mk))
        else:
            for ti, t in enumerate(tiles):
                steps.append(dict(jn=jn, kind="B", t=(t,), first=(ti == 0), last=(ti == len(tiles) - 1)))

    def emit_S(i):
        st = steps[i]
        jb = jobs[st["jn"]]
        buf, qb, nq, sb = jb["gi"] % 2, st["jn"] % 2, jb["nq"], Sb[i % 2]
        if st["kind"] in ("A", "C"):
            rd = [("KT", buf), ("KTz", buf), ("QB", qb), ("QBz", qb)]
            for k, t in enumerate(st["t"]):
                s.op("pe", lambda e: e.matmul(sb[:, k, 0:nq], lhsT=KT[buf][:, t * 128:(t + 1) * 128],
                                              rhs=QB[qb][:, st["r"], 0:nq], start=True, stop=True), reads=rd, writes=[sb])
        else:
            rd = [("KT", buf), ("KTz", buf), ("QB", "B", qb)]
            t = st["t"][0]
            for k in range(2):
                s.op("pe", lambda e: e.matmul(sb[:, k, 0:nq], lhsT=KT[buf][:, t * 128:(t + 1) * 128],
                                              rhs=QBB[qb][:, k, 0:nq], start=True, stop=True), reads=rd, writes=[sb])

    def emit_rest(i):
        st = steps[i]
        jb = jobs[st["jn"]]
        buf, nq, sb, pt = jb["gi"] % 2, jb["nq"], Sb[i % 2], PT[i % 2]
        kind = st["kind"]
        scale = 32 ** -0.5 if kind == "B" else 0.125
        s.op("act", lambda e: e.activation(out=pt[:, :, 0:nq], in_=sb[:, :, 0:nq], func=AF.Exp, scale=scale),
             reads=[], writes=[sb, pt])
        if st.get("mk") is not None:
            m0, m1 = st["mk"]
            if m1 == m0 + 1:
                s.op("dve", lambda e: e.tensor_tensor(out=pt[:], in0=pt[:], in1=cm[:, m0:m0 + 2, :], op=ALU.mult), reads=[cm], writes=[pt])
            else:
                for k, mm in enumerate((m0, m1)):
                    s.op("dve", lambda e: e.tensor_tensor(out=pt[:, k, :], in0=pt[:, k, :], in1=cm[:, mm, :], op=ALU.mult), reads=[cm], writes=[pt])
        rdv = [("VA", buf), ("VAones", buf), pt]
        if kind in ("A", "C"):
            o = Ob[st["ob"]]
            for k, t in enumerate(st["t"]):
                s.op("pe", lambda e: e.matmul(o[:, 0:nq], lhsT=VA[buf][:, t, :], rhs=pt[:, k, 0:nq],
                                              start=(st["first"] and k == 0), stop=(st["last"] and k == 1)), reads=rdv, writes=[o])
        else:
            t = st["t"][0]
            for k in range(2):
                s.op("pe", lambda e: e.matmul(Ob[k][:, 0:nq], lhsT=VA[buf][:, t, :], rhs=pt[:, k, 0:nq],
                                              start=st["first"], stop=st["last"]), reads=rdv, writes=[Ob[k]])
        if not st["last"]:
            return
        q0 = jb["q0"]
        if kind in ("A", "C"):
            o, ob = Ob[st["ob"]], st["ob"]
            if kind == "C":
                hh = 3 * jb["g"] + st["r"]
                s.op("dve", lambda e: e.tensor_scalar(out=rs[ob][64:65, 0:nq], in0=o[64:65, 0:nq], scalar1=sinkexp[64:65, hh:hh + 1],
                                                      scalar2=None, op0=ALU.add), reads=[sinkexp], writes=[o, rs[ob]])
                s.op("dve", lambda e: e.reciprocal(out=rs[ob][64:65, 0:nq], in_=rs[ob][64:65, 0:nq]), reads=[], writes=[rs[ob]])
            else:
                s.op("dve", lambda e: e.reciprocal(out=rs[ob][64:65, 0:nq], in_=o[64:65, 0:nq]), reads=[], writes=[o, rs[ob]])
            bcast_row(Mb[ob], rs[ob][64:65, 0:nq], nq, rs[ob])
            s.op("act", lambda e: e.copy(out=bcs[ob][:, 0:nq], in_=Mb[ob][0:64, 0:nq]), reads=[], writes=[Mb[ob], bcs[ob]])
            s.op("dve", lambda e: e.tensor_tensor(out=OTs[ob][:, 0:nq], in0=o[0:64, 0:nq], in1=bcs[ob][:, 0:nq], op=ALU.mult),
                 reads=[bcs[ob]], writes=[o, OTs[ob]])
            row0 = ((0 if kind == "A" else 10) + jb["g"] * 3 + st["r"]) * 64
            s.dma("sp", oT[row0:row0 + 64, q0:q0 + nq], OTs[ob][:, 0:nq], reads=[OTs[ob]], writes=[("oT", row0, q0)], sem="D_oT")
        else:
            for k in range(2):
                s.op("dve", lambda e: e.reciprocal(out=rs[k][64:65, 0:nq], in_=Ob[k][64:65, 0:nq]), reads=[], writes=[Ob[k], rs[k]])
            s.op("dve", lambda e: e.tensor_scalar(out=rs[1][64:65, 0:nq], in0=rs[1][64:65, 0:nq], scalar1=neglam[64:65, 0:1],
                                                  scalar2=None, op0=ALU.mult), reads=[neglam], writes=[rs[1]])
            for k in range(2):
                bcast_row(Mb[k], rs[k][64:65, 0:nq], nq, rs[k])
                s.op("act", lambda e: e.copy(out=bcs[k][:, 0:nq], in_=Mb[k][0:64, 0:nq]), reads=[], writes=[Mb[k], bcs[k]])
            s.op("dve", lambda e: e.tensor_tensor(out=t1[:, 0:nq], in0=Ob[0][0:64, 0:nq], in1=bcs[0][:, 0:nq], op=ALU.mult),
                 reads=[bcs[0]], writes=[Ob[0], t1])
            s.op("dve", lambda e: e.tensor_tensor(out=t2[:, 0:nq], in0=Ob[1][0:64, 0:nq], in1=bcs[1][:, 0:nq], op=ALU.mult),
                 reads=[bcs[1]], writes=[Ob[1], t2])
            s.op("pool", lambda e: e.tensor_tensor(out=od[:, 0:nq], in0=t1[:, 0:nq], in1=t2[:, 0:nq], op=ALU.add), reads=[t1, t2], writes=[od])
            s.op("pool", lambda e: e.tensor_tensor(out=sq[:, 0:nq], in0=od[:, 0:nq], in1=od[:, 0:nq], op=ALU.mult), reads=[od], writes=[sq])
            s.op("pe", lambda e: e.matmul(Mb[0][0:64, 0:nq], lhsT=onesf[0:64, 0:64], rhs=sq[:, 0:nq], start=True, stop=True),
                 reads=[sq, onesf], writes=[Mb[0]])
            s.op("act", lambda e: e.activation(out=rst[:, 0:nq], in_=Mb[0][0:64, 0:nq], func=AF.Ln, scale=1.0 / 64, bias=EPS),
                 reads=[], writes=[Mb[0], rst])
            s.op("act", lambda e: e.activation(out=rst[:, 0:nq], in_=rst[:, 0:nq], func=AF.Exp, scale=-0.5), reads=[rst], writes=[rst])
            s.op("dve", lambda e: e.scalar_tensor_tensor(out=OTs[0][:, 0:nq], in0=od[:, 0:nq], scalar=gsubs[:, 0:1], in1=rst[:, 0:nq],
                                                         op0=ALU.mult, op1=ALU.mult), reads=[od, gsubs, rst], writes=[OTs[0]])
            row0 = (6 + jb["g"]) * 64
            s.dma("sp", oT[row0:row0 + 64, q0:q0 + nq], OTs[0][:, 0:nq], reads=[OTs[0]], writes=[("oT", row0, q0)], sem="D_oT")

    load_group(0)
    load_q(0)
    emit_S(0)
    cur_job = -1
    for i, st in enumerate(steps):
        if st["jn"] != cur_job:
            cur_job = st["jn"]
            jb = jobs[cur_job]
            if cur_job + 1 < len(jobs):
                load_q(cur_job + 1)
            if (cur_job == 0 or jobs[cur_job - 1]["gi"] != jb["gi"]) and jb["gi"] + 1 < len(groups):
                load_group(jb["gi"] + 1)
        if i + 1 < len(steps):
            emit_S(i + 1)
        emit_rest(i)

    s.pop()


def emit_ffn_mod(s, consts, do_ctx, cvec, w_ada, b_ada, g_ffn, modd):
    identf = consts[0]
    s.push()
    gff = s.sb("gff", [128, D], F32)
    s.dma("sp", gff[:], g_ffn.partition_broadcast(128), writes=[gff])
    mod = s.sb("modm", [128, 4 * D], F32)
    modres = [(mod, i) for i in range(8)]
    for stream in range(2 if do_ctx else 1):
        emit_mod(s, cvec, stream, w_ada, b_ada, 2 * D, 4 * D, mod, identf)
        s.op("dve", lambda e: e.scalar_tensor_tensor(out=mod[:, 2 * D:3 * D], in0=mod[:, 2 * D:3 * D], scalar=1.0, in1=gff[:],
                                                     op0=ALU.add, op1=ALU.mult), reads=[gff] + modres, writes=modres)
        s.dma("sp", modd[stream], mod[:], reads=modres, writes=[("modd", stream)] + modres, sem="D_modd")
    s.pop()


def emit_ffn(s, consts, last, do_ctx, xs, modd, w_out, w_ff1, w_ff3, w_ff2, g_final, oT, xout):
    identf, identb, onesf = consts
    s.push()
    wout = s.sb("wout", [128, 8, D], BF16)
    w1 = s.sb("w1", [128, 8, DFF], BF16)
    w3 = s.sb("w3", [128, 8, DFF], BF16)
    w2 = s.sb("w2", [128, NFF, D], BF16)
    wres = []
    for k in range(8):
        s.dma("pool", wout[:, k, :], w_out[k * 128:(k + 1) * 128, :], writes=[(wout, k)], sem="D_w")
        s.dma("pool", w1[:, k, :], w_ff1[k * 128:(k + 1) * 128, :], writes=[(w1, k)], sem="D_w")
        s.dma("pool", w3[:, k, :], w_ff3[k * 128:(k + 1) * 128, :], writes=[(w3, k)], sem="D_w")
        wres += [(wout, k), (w1, k), (w3, k)]
    for k in range(0, NFF, 2):
        s.dma("pool", w2[:, k:k + 2, :], w_ff2[k * 128:(k + 2) * 128, :].rearrange("(c p) n -> p c n", p=128), writes=[(w2, k)], sem="D_w")
        wres.append((w2, k))
    gfin = None
    if last:
        gfin = s.sb("gfin", [128, D], F32)
        s.dma("sp", gfin[:], g_final.partition_broadcast(128), writes=[gfin])
    mod = s.sb("modf", [128, 4 * D], F32)
    modres = [(mod, i) for i in range(8)]
    xt = [s.sb("fxt%d" % i, [128, D], F32) for i in range(2)]
    x1 = [s.sb("x1_%d" % i, [128, D], F32) for i in range(2)]
    ss = [s.sb("fss%d" % i, [128, 1], F32) for i in range(2)]
    rstd = [s.sb("frstd%d" % i, [128, 1], F32) for i in range(2)]
    hb = [s.sb("fhb%d" % i, [128, D], BF16) for i in range(2)]
    h2T = s.sb("h2T", [128, 8, 128], BF16)
    OTt = [s.sb("OTt%d" % i, [128, 8, 128], BF16) for i in range(2)]
    sa = s.sb("sa", [128, 512], F32)
    junk = sa[:].bitcast(BF16)
    uT = s.sb("uT", [128, NFF, 128], BF16)
    pya = s.ps("pya", [128, 2, 512], F32)
    py = s.ps("py", [128, 2, 512], F32)
    pa = [s.ps("pa%d" % i, [128, 512], F32) for i in range(2)]
    pb = [s.ps("pb%d" % i, [128, 512], F32) for i in range(2)]
    pT = pa[0][:].bitcast(BF16).rearrange("p (k t) -> p k t", k=8)
    grps = [list(range(c, min(c + 4, NFF))) for c in range(0, NFF, 4)]

    def stage_a(n, i):
        p = n % 2
        s.dma("sp", xt[p][:], xs[i * 128:(i + 1) * 128, :], writes=[xt[p]])
        s.dma("sp", OTt[p][:], oT.rearrange("(c r) t -> r c t", r=128)[:, :, i * 128:(i + 1) * 128], writes=[OTt[p]])
        for h in range(2):
            for c in range(8):
                s.op("pe", lambda e: e.matmul(pya[:, h, :], lhsT=OTt[p][:, c, :], rhs=wout[:, c, h * 512:(h + 1) * 512],
                                              start=(c == 0), stop=(c == 7)), reads=[OTt[p]] + wres, writes=[pya])
        s.op("dve", lambda e: e.tensor_tensor(out=x1[p][:], in0=pya[:].rearrange("p a b -> p (a b)"), in1=mod[:, 0:D], op=ALU.mult),
             reads=modres, writes=[pya, x1[p]])
        s.op("dve", lambda e: e.tensor_tensor(out=x1[p][:], in0=x1[p][:], in1=xt[p][:], op=ALU.add), reads=[xt[p]], writes=[x1[p]])
        s.op("act", lambda e: e.activation(out=junk, in_=x1[p][:], func=AF.Square, accum_out=ss[p][:]), reads=[x1[p]], writes=[sa, ss[p]])
        emit_rstd(s, ss[p][:], rstd[p][:], D, [ss[p]], [rstd[p]])
        s.op("dve", lambda e: e.scalar_tensor_tensor(out=xt[p][:], in0=x1[p][:], scalar=rstd[p][:, 0:1], in1=mod[:, 2 * D:3 * D],
                                                     op0=ALU.mult, op1=ALU.mult), reads=[x1[p], rstd[p]] + modres, writes=[xt[p]])
        s.op("pool", lambda e: e.tensor_tensor(out=hb[p][:], in0=xt[p][:], in1=mod[:, D:2 * D], op=ALU.add),
             reads=[xt[p]] + modres, writes=[hb[p]])

    def stage_b(n, i):
        p = n % 2
        for k in range(8):
            s.op("pe", lambda e: e.transpose(pT[:, k, :], hb[p][:, k * 128:(k + 1) * 128], identb[:]), reads=[hb[p], identb], writes=[pa[0]])
        s.op("act", lambda e: e.copy(out=h2T[:], in_=pT), reads=[], writes=[pa[0], h2T])
        for gi, grp in enumerate(grps):
            gp = gi % 2
            n_ = len(grp) * 128
            for (w_, pp) in ((w1, pa[gp]), (w3, pb[gp])):
                for j, c in enumerate(grp):
                    for k in range(8):
                        s.op("pe", lambda e: e.matmul(pp[:, j * 128:(j + 1) * 128], lhsT=w_[:, k, c * 128:(c + 1) * 128],
                                                      rhs=h2T[:, k, :], start=(k == 0), stop=(k == 7)), reads=[h2T] + wres, writes=[pp])
            s.op("act", lambda e: e.activation(out=sa[:, 0:n_], in_=pa[gp][:, 0:n_], func=AF.Silu), reads=[], writes=[pa[gp], sa])
            s.op("dve", lambda e: e.tensor_tensor(out=uT[:, grp[0]:grp[0] + len(grp), :].rearrange("p c t -> p (c t)"),
                                                  in0=sa[:, 0:n_], in1=pb[gp][:, 0:n_], op=ALU.mult),
                 reads=[sa], writes=[pb[gp], (uT, gi)])
        for h in range(2):
            for c in range(NFF):
                s.op("pe", lambda e: e.matmul(py[:, h, :], lhsT=uT[:, c, :], rhs=w2[:, c, h * 512:(h + 1) * 512],
                                              start=(c == 0), stop=(c == NFF - 1)),
                     reads=[(uT, g_) for g_ in range(len(grps))] + wres, writes=[py])
        s.op("dve", lambda e: e.tensor_tensor(out=xt[p][:], in0=py[:].rearrange("p a b -> p (a b)"), in1=mod[:, 3 * D:4 * D], op=ALU.mult),
             reads=modres, writes=[py, xt[p]])
        s.op("pool", lambda e: e.tensor_tensor(out=xt[p][:], in0=xt[p][:], in1=x1[p][:], op=ALU.add), reads=[x1[p]], writes=[xt[p]])
        if last:
            s.op("act", lambda e: e.activation(out=junk, in_=xt[p][:], func=AF.Square, accum_out=ss[p][:]), reads=[xt[p]], writes=[sa, ss[p]])
            emit_rstd(s, ss[p][:], rstd[p][:], D, [ss[p]], [rstd[p]])
            s.op("dve", lambda e: e.scalar_tensor_tensor(out=xt[p][:], in0=xt[p][:], scalar=rstd[p][:, 0:1], in1=gfin[:],
                                                         op0=ALU.mult, op1=ALU.mult), reads=[rstd[p], gfin], writes=[xt[p]])
        s.dma("sp", xout[i * 128:(i + 1) * 128, :], xt[p][:], reads=[xt[p]], writes=[("xout", i)], sem="D_xout")

    def run_tiles(tile_ids):
        stage_a(0, tile_ids[0])
        for n, i in enumerate(tile_ids):
            if n + 1 < len(tile_ids):
                stage_a(n + 1, tile_ids[n + 1])
            stage_b(n, i)

    for stream in range(2 if do_ctx else 1):
        s.dma("sp", mod[:], modd[stream], reads=[("modd", stream)], writes=modres)
        run_tiles(list(range(32)) if stream == 0 else [32, 33])
    s.pop()


def build_fused():
    nc = bass.Bass("TRN2", target_bir_lowering=False)
    di = lambda n, sh, dt_=F32: nc.dram_tensor(n, sh, dt_, kind="ExternalInput").ap()
    it = lambda n, sh, dt_=BF16: nc.dram_tensor(n, sh, dt_, kind="Internal").ap()
    xs_in = di("xs", [NTOK, D]); cvec = di("cvec", [2, D]); rope = di("rope", [NTOK, 192])
    cmask = di("cmask", [128, 8 * 512], BF16); sel = di("sel", [128, 8])
    w_ada = di("w_ada", [DEPTH, D, 6 * D]); b_ada = di("b_ada", [DEPTH, 6 * D]); g_attn = di("g_attn", [DEPTH, D])
    g_ffn = di("g_ffn", [DEPTH, D]); w_in = di("w_in", [DEPTH, D, 2048]); g_qk = di("g_qk", [DEPTH, 2, 64])
    lamv = di("lamv", [DEPTH, 4, 32]); sinkv = di("sinkv", [DEPTH, 6]); gsub = di("gsub", [DEPTH, 64])
    w_out = di("w_out", [DEPTH, D, D]); w_ff1 = di("w_ff1", [DEPTH, D, DFF]); w_ff3 = di("w_ff3", [DEPTH, D, DFF])
    w_ff2 = di("w_ff2", [DEPTH, DFF, D]); g_final = di("g_final", [D])
    xout = nc.dram_tensor("xout", [NLAT, D], F32, kind="ExternalOutput").ap()
    qT = it("qT_s", [1024, NTOK]); kT = it("kT_s", [512, NLAT]); vv = [it("vv_s%d" % j, [128, NLAT]) for j in range(4)]
    ckT = it("ckT_s", [512, NCTX]); cvv = it("cvv_s", [128, 1024]); oT = it("oT_s", [1024, NTOK])
    kTg = [it("kTg%d" % j, [512, NLAT]) for j in range(4)]
    vg = [it("vg%d" % j, [512, NLAT]) for j in range(4)]
    xs1 = it("xs1_s", [NTOK, D], F32)
    modd = it("modd_s", [2, 128, 4 * D], F32)
    kg = lambda r, krow0: kTg[krow0 // 128][r * 128 + krow0 % 128:r * 128 + krow0 % 128 + 64, :]
    vgf = lambda r, unit: vg[unit // 2][r * 128:(r + 1) * 128, (unit % 2) * 2048:(unit % 2 + 1) * 2048]
    s = Sched(nc)
    consts = emit_consts(s)
    xs = xs_in
    for l in range(DEPTH):
        last = l == DEPTH - 1
        emit_pre(s, consts, xs, cvec, w_ada[l], b_ada[l], g_attn[l], w_in[l], g_qk[l], rope, qT, kT, vv, ckT, cvv)
        for j in range(4):
            s.collective(kT[j * 128:(j + 1) * 128, :], kTg[j], reads=["kT"], writes=[("kTg", j)], sem="CCk%d" % j)
            s.collective(vv[j], vg[j], reads=["vv"], writes=[("vg", j)], sem="CCv%d" % j)
        s.barrier()
        vown = lambda unit: vv[unit // 2][:, (unit % 2) * 2048:(unit % 2 + 1) * 2048]
        emit_attn(s, consts, l, not last, qT, kg, vgf, kT, vown, ckT, cvv, sel, cmask, lamv[l], sinkv[l], gsub[l], oT)
        emit_ffn_mod(s, consts, not last, cvec, w_ada[l], b_ada[l], g_ffn[l], modd)
        emit_ffn(s, consts, last, not last, xs, modd, w_out[l], w_ff1[l], w_ff3[l], w_ff2[l], g_final, oT, xout if last else xs1)
        xs = xs1
    s.finish()
    return nc, s


_PROG = {}


def kernel(x, c, ctx, c_ctx, w_ada, b_ada, g_attn, g_ffn, w_in, g_q, g_k, lam_q1, lam_k1, lam_q2, lam_k2,
           g_subln, sink_logit, w_out, w_ff1, w_ff3, w_ff2, g_final):
    f = lambda a: np.ascontiguousarray(np.asarray(a, dtype=np.float32))
    x, c, ctx, c_ctx = f(x), f(c), f(ctx), f(c_ctx)
    cores = list(range(8))
    shared = dict(w_ada=f(w_ada), b_ada=f(b_ada), g_attn=f(g_attn), g_ffn=f(g_ffn), w_in=f(w_in),
                  g_qk=np.ascontiguousarray(np.stack([f(g_q), f(g_k)], axis=1)),
                  lamv=np.ascontiguousarray(np.stack([f(lam_q1), f(lam_k1), f(lam_q2), f(lam_k2)], axis=1)),
                  sinkv=f(sink_logit), gsub=f(g_subln), w_out=f(w_out), w_ff1=f(w_ff1), w_ff3=f(w_ff3), w_ff2=f(w_ff2),
                  g_final=f(g_final))
    maps = []
    for i in cores:
        b, q = i // 4, i % 4
        sel = np.zeros((128, 8), np.float32)
        if q > 0:
            sel[:, q - 1] = 1.0
        if q < 3:
            sel[:, 4 + q + 1] = 1.0
        maps.append(dict(xs=np.ascontiguousarray(np.concatenate([x[b, q * NLAT:(q + 1) * NLAT], ctx[b]], 0)),
                         cvec=np.ascontiguousarray(np.stack([c[b], c_ctx])), rope=rope_tables(i), cmask=cmask_table(i), sel=sel,
                         **shared))
    if "fused" not in _PROG:
        _PROG["fused"] = build_fused()[0]
    res = run_bass_kernel_spmd(_PROG["fused"], maps, core_ids=cores).results
    out = np.empty((2, SEQ, D), np.float32)
    for i in cores:
        out[i // 4, (i % 4) * NLAT:(i % 4 + 1) * NLAT] = np.asarray(res[i]["xout"])
    return out
```

```python
import contextlib
import math
import numpy as np
import ml_dtypes
import concourse.bass as bass
import concourse.mybir as mybir
from concourse.bass_utils import run_bass_kernel_spmd

F32 = mybir.dt.float32
BF16 = mybir.dt.bfloat16
AF = mybir.ActivationFunctionType
ALU = mybir.AluOpType
AX = mybir.AxisListType

D = 1024
SEQ = 16384
NLAT = 4096
NCTX = 256
NTOK = NLAT + NCTX
NTILE = NTOK // 128
DFF = 2816
NFF = DFF // 128
DEPTH = 2
EPS = 1e-6
AQ, AK, AV, BQ, BK, BV, CQ, CK, CV = [(0, 384), (384, 512), (512, 640), (640, 896), (896, 1152), (1152, 1408),
                                      (1408, 1792), (1792, 1920), (1920, 2048)]
WIN_ORDER = [AQ, AK, BQ, BK, CQ, CK, AV, BV, CV]


class Sched:
    def __init__(self, nc):
        self.nc = nc
        self.eng = {"pe": nc.tensor, "act": nc.scalar, "dve": nc.vector, "pool": nc.gpsimd, "sp": nc.sync}
        self.stack = contextlib.ExitStack()
        self.scopes = []
        self.sems = {}
        self.cnt = {}
        self.waited = {e: {} for e in self.eng}
        self.last_w = {}
        self.readers = {}
        for e in self.eng:
            self._sem("E_" + e)
        self.n_inst = 0
        self.n_wait = 0
        self.uid = 0

    def _sem(self, name):
        if name not in self.sems:
            self.sems[name] = self.stack.enter_context(self.nc.semaphore(name))
            self.cnt[name] = 0
        return self.sems[name]

    def push(self):
        st = contextlib.ExitStack()
        self.scopes.append(st)
        return st

    def pop(self):
        self.barrier()
        self.scopes.pop().close()

    def _ctx(self):
        return self.scopes[-1] if self.scopes else self.stack

    def sb(self, name, shape, dtype):
        self.uid += 1
        return self._ctx().enter_context(self.nc.sbuf_tensor("%s_%d" % (name, self.uid), list(shape), dtype))

    def ps(self, name, shape, dtype):
        self.uid += 1
        return self._ctx().enter_context(self.nc.psum_tensor("%s_%d" % (name, self.uid), list(shape), dtype))

    @staticmethod
    def _key(r):
        if isinstance(r, tuple):
            return tuple(Sched._key(x) for x in r)
        if isinstance(r, (str, int)):
            return r
        return id(r)

    def _deps(self, reads, writes):
        d = {}
        for r in reads:
            t = self.last_w.get(self._key(r))
            if t:
                d[t[0]] = max(d.get(t[0], 0), t[1])
        for w in writes:
            k = self._key(w)
            t = self.last_w.get(k)
            if t:
                d[t[0]] = max(d.get(t[0], 0), t[1])
            for t in self.readers.get(k, ()):
                d[t[0]] = max(d.get(t[0], 0), t[1])
        return d

    def _emit_waits(self, en, deps, skip_self=False):
        e = self.eng[en]
        wd = self.waited[en]
        for sn, v in deps.items():
            if skip_self and sn == "E_" + en:
                continue
            if wd.get(sn, 0) < v:
                e.wait_ge(self.sems[sn], v)
                wd[sn] = v
                self.n_wait += 1

    def _record(self, tok, reads, writes):
        for r in reads:
            self.readers.setdefault(self._key(r), []).append(tok)
        for w in writes:
            k = self._key(w)
            self.last_w[k] = tok
            self.readers[k] = []

    def op(self, en, fn, reads=(), writes=()):
        deps = self._deps(reads, writes)
        self._emit_waits(en, deps, skip_self=(en == "pe"))
        ins = fn(self.eng[en])
        sn = "E_" + en
        self.cnt[sn] += 1
        ins.then_inc(self.sems[sn], 1)
        self._record((sn, self.cnt[sn]), reads, writes)
        self.n_inst += 1
        return ins

    def dma(self, q, out, in_, reads=(), writes=(), sem=None, **kw):
        if sem is None:
            sem = "D_%x" % (hash(self._key(writes[0])) & 0xFFFFFFF)
        self._sem(sem)
        deps = self._deps(reads, writes)
        self._emit_waits(q, deps)
        ins = self.eng[q].dma_start(out=out, in_=in_, **kw)
        self.cnt[sem] += 16
        ins.then_inc(self.sems[sem], 16)
        self._record((sem, self.cnt[sem]), reads, writes)
        self.n_inst += 1
        return ins

    def collective(self, in_ap, out_ap, reads=(), writes=(), sem="CC"):
        self._sem(sem)
        deps = self._deps(reads, writes)
        self._emit_waits("pool", deps)
        ins = self.nc.gpsimd.collective_compute("AllGather", ALU.bypass, replica_groups=[[0, 1, 2, 3], [4, 5, 6, 7]],
                                                ins=[in_ap], outs=[out_ap])
        self.cnt[sem] += 1
        ins.then_inc(self.sems[sem], 1)
        self._record((sem, self.cnt[sem]), reads, writes)
        self.n_inst += 1
        return ins

    def barrier(self):
        allv = {sn: c for sn, c in self.cnt.items() if c > 0}
        for en in self.eng:
            self._emit_waits(en, allv)

    def finish(self):
        self.barrier()
        while self.scopes:
            self.scopes.pop().close()
        self.stack.close()


def emit_consts(s):
    identf = s.sb("identf", [128, 128], F32)
    identb = s.sb("identb", [128, 128], BF16)
    onesf = s.sb("onesf", [128, 128], F32)
    s.op("pool", lambda e: e.memset(identf[:], 0.0), writes=[identf])
    s.op("pool", lambda e: e.affine_select(out=identf[:], in_=identf[:], pattern=[[-1, 128]],
                                           compare_op=ALU.not_equal, fill=1.0, base=0, channel_multiplier=1),
         reads=[identf], writes=[identf])
    s.op("pool", lambda e: e.tensor_copy(out=identb[:], in_=identf[:]), reads=[identf], writes=[identb])
    s.op("pool", lambda e: e.memset(onesf[:], 1.0), writes=[onesf])
    return identf, identb, onesf


def emit_mod(s, cvec, stream, w_ada, b_ada, c0, ncols, mod, identf):
    s.push()
    cb = s.sb("cb", [128, D], F32)
    sc = s.sb("sc", [128, D], F32)
    scT = s.sb("scT", [128, 8, 128], F32)
    bb = s.sb("bb", [128, ncols], F32)
    wa = [s.sb("wa%d" % i, [128, 8, 512], F32) for i in range(2)]
    ptr = s.ps("ptr", [128, 4, 128], F32)
    pm = s.ps("pm", [128, 512], F32)
    s.dma("sp", cb[:], cvec[stream].partition_broadcast(128), writes=[cb])
    s.dma("sp", bb[:], b_ada[c0:c0 + ncols].partition_broadcast(128), writes=[bb])
    s.op("act", lambda e: e.activation(out=sc[:], in_=cb[:], func=AF.Silu), reads=[cb], writes=[sc])
    for kk in range(0, 8, 4):
        for j in range(4):
            s.op("pe", lambda e: e.transpose(ptr[:, j, :], sc[:, (kk + j) * 128:(kk + j + 1) * 128], identf[:]),
                 reads=[sc, identf], writes=[ptr])
        s.op("dve", lambda e: e.tensor_copy(out=scT[:, kk:kk + 4, :], in_=ptr[:]), reads=[], writes=[ptr, (scT, kk)])
    for cbk in range(ncols // 512):
        w = wa[cbk % 2]
        s.dma("sp", w[:], w_ada[:, c0 + cbk * 512:c0 + (cbk + 1) * 512].rearrange("(k p) n -> p k n", p=128), writes=[w])
        for k in range(8):
            s.op("pe", lambda e: e.matmul(pm[:], lhsT=scT[:, k, :], rhs=w[:, k, :], start=(k == 0), stop=(k == 7)),
                 reads=[(scT, 0), (scT, 4), w], writes=[pm])
        s.op("dve", lambda e: e.tensor_tensor(out=mod[:, cbk * 512:(cbk + 1) * 512], in0=pm[:],
                                              in1=bb[:, cbk * 512:(cbk + 1) * 512], op=ALU.add),
             reads=[bb], writes=[pm, (mod, cbk)])
    s.pop()


def emit_rstd(s, ss, rstd, n, reads, writes):
    s.op("act", lambda e: e.activation(out=rstd, in_=ss, func=AF.Ln, scale=1.0 / n, bias=EPS), reads=reads, writes=writes)
    s.op("act", lambda e: e.activation(out=rstd, in_=rstd, func=AF.Exp, scale=-0.5), reads=writes, writes=writes)


def emit_rope(s, src, src_res, C, S, tabres, w, out, out_res, t, u, src_is_psum):
    G = 512 // w
    d = w // 4
    srcg = src.rearrange("p (g c) -> p g c", g=G)
    src5 = src.rearrange("p (g t a d) -> p g t a d", g=G, t=2, a=2, d=d)
    u5 = u[:].rearrange("p (g t a d) -> p g t a d", g=G, t=2, a=2, d=d)
    S4 = S.rearrange("p (t a d) -> p t a d", t=2, a=2, d=d)
    rd = [] if src_is_psum else [src_res]
    wr = [src_res] if src_is_psum else []
    s.op("dve", lambda e: e.tensor_tensor(out=t[:].rearrange("p (g c) -> p g c", g=G), in0=srcg,
                                          in1=C.unsqueeze(1).broadcast_to([128, G, w]), op=ALU.mult),
         reads=rd + [tabres], writes=wr + [t])
    for a in range(2):
        s.op("dve", lambda e: e.tensor_tensor(out=u5[:, :, :, a, :], in0=src5[:, :, :, 1 - a, :],
                                              in1=S4[:, :, a, :].unsqueeze(1).broadcast_to([128, G, 2, d]), op=ALU.mult),
             reads=rd + [tabres], writes=wr + [(u, a)])
    s.op("pool", lambda e: e.tensor_tensor(out=out, in0=t[:], in1=u[:], op=ALU.add),
         reads=[t, (u, 0), (u, 1)], writes=[out_res])


def emit_pre(s, consts, xs, cvec, w_ada, b_ada, g_attn, w_in, g_qk, rope, qT, kT, vv, ckT, cvv):
    identf, identb, onesf = consts
    s.push()
    mods = []
    for stream in range(2):
        m = s.sb("mod%d" % stream, [128, 2 * D], F32)
        emit_mod(s, cvec, stream, w_ada, b_ada, 0, 2 * D, m, identf)
        mods.append(m)
    gat = s.sb("gat", [128, D], F32)
    s.dma("sp", gat[:], g_attn.partition_broadcast(128), writes=[gat])
    for m in mods:
        s.op("dve", lambda e: e.scalar_tensor_tensor(out=m[:, D:2 * D], in0=m[:, D:2 * D], scalar=1.0, in1=gat[:],
                                                     op0=ALU.add, op1=ALU.mult),
             reads=[gat, (m, 2), (m, 3)], writes=[(m, 2), (m, 3)])
    win = s.sb("win", [128, 8, 2048], BF16)
    c = 0
    for (a, b) in WIN_ORDER:
        s.dma("pool", win[:, :, c:c + (b - a)], w_in[:, a:b].rearrange("(k p) n -> p k n", p=128), writes=[(win, c)], sem="D_win")
        c += b - a
    win_res = [(win, cc) for cc in np.cumsum([0] + [b - a for (a, b) in WIN_ORDER[:-1]]).tolist()]
    gqk = s.sb("gqk", [128, 512], F32)
    for h in range(8):
        s.dma("sp", gqk[:, h * 64:(h + 1) * 64], g_qk[0 if h < 6 else 1].partition_broadcast(128), writes=[(gqk, h)], sem="D_gqk")
    gqk_res = [(gqk, h) for h in range(8)]
    ktacc = s.sb("ktacc", [128, 4, NLAT], BF16)
    vacc = s.sb("vacc", [128, 8, 32, 64], BF16)
    ktc = s.sb("ktc", [128, 4, NCTX], BF16)
    vc = s.sb("vc", [128, 8, 2, 64], BF16)
    xt = [s.sb("xt%d" % i, [128, D], F32) for i in range(2)]
    tb = [s.sb("tb%d" % i, [128, 192], F32) for i in range(2)]
    junk = s.sb("junk", [128, D], BF16)
    ss = [s.sb("ss%d" % i, [128, 1], F32) for i in range(2)]
    rstd = [s.sb("rstd%d" % i, [128, 1], F32) for i in range(2)]
    hb = [s.sb("hb%d" % i, [128, D], BF16) for i in range(2)]
    hT = [s.sb("hT%d" % i, [128, 8, 128], BF16) for i in range(2)]
    sqa = s.sb("sqa", [128, 512], F32)
    ssa = s.sb("ssa", [128, 8], F32)
    ra = s.sb("ra", [128, 8], F32)
    qa = s.sb("qa", [128, 512], F32)
    tt = s.sb("tt", [128, 512], F32)
    uu = s.sb("uu", [128, 512], F32)
    qkb = [s.sb("qkb%d" % i, [128, 1536], BF16) for i in range(2)]
    qst = [s.sb("qst%d" % i, [128, 8, 512], BF16) for i in range(2)]
    pT = s.ps("pT", [128, 8, 128], BF16)
    pq = s.ps("pq", [128, 4, 512], F32)
    ptq = [s.ps("ptq%d" % i, [128, 8, 128], BF16) for i in range(2)]

    for i in range(NTILE):
        p = i % 2
        isctx = i >= 32
        m = mods[1] if isctx else mods[0]
        s.dma("sp", xt[p][:], xs[i * 128:(i + 1) * 128, :], writes=[xt[p]])
        s.dma("sp", tb[p][:], rope[i * 128:(i + 1) * 128, :], writes=[tb[p]])
        s.op("act", lambda e: e.activation(out=junk[:], in_=xt[p][:], func=AF.Square, accum_out=ss[p][:]),
             reads=[xt[p]], writes=[junk, ss[p]])
        emit_rstd(s, ss[p][:], rstd[p][:], D, [ss[p]], [rstd[p]])
        s.op("dve", lambda e: e.scalar_tensor_tensor(out=xt[p][:], in0=xt[p][:], scalar=rstd[p][:, 0:1], in1=m[:, D:2 * D],
                                                     op0=ALU.mult, op1=ALU.mult),
             reads=[rstd[p], (m, 2), (m, 3)], writes=[xt[p]])
        s.op("pool", lambda e: e.tensor_tensor(out=hb[p][:], in0=xt[p][:], in1=m[:, 0:D], op=ALU.add),
             reads=[xt[p], (m, 0), (m, 1)], writes=[hb[p]])
        for k in range(8):
            s.op("pe", lambda e: e.transpose(pT[:, k, :], hb[p][:, k * 128:(k + 1) * 128], identb[:]),
                 reads=[hb[p], identb], writes=[pT])
        s.op("act", lambda e: e.copy(out=hT[p][:], in_=pT[:]), reads=[], writes=[pT, hT[p]])
        for cb in range(4):
            for k in range(8):
                s.op("pe", lambda e: e.matmul(pq[:, cb, :], lhsT=hT[p][:, k, :], rhs=win[:, k, cb * 512:(cb + 1) * 512],
                                              start=(k == 0), stop=(k == 7)),
                     reads=[hT[p]] + win_res, writes=[(pq, cb)])
        vdst = vc[:, :, i - 32, :] if isctx else vacc[:, :, i, :]
        s.op("act", lambda e: e.copy(out=vdst, in_=pq[:, 3, :].rearrange("p (u e) -> p u e", u=8)),
             reads=[], writes=[(pq, 3), ("vacc", i)])
        s.op("act", lambda e: e.activation(out=sqa[:], in_=pq[:, 0, :], func=AF.Square), reads=[], writes=[(pq, 0), sqa])
        s.op("dve", lambda e: e.reduce_sum(out=ssa[:], in_=sqa[:].rearrange("p (h c) -> p h c", h=8), axis=AX.X),
             reads=[sqa], writes=[ssa])
        emit_rstd(s, ssa[:], ra[:], 64, [ssa], [ra])
        s.op("dve", lambda e: e.tensor_tensor(out=qa[:].rearrange("p (h c) -> p h c", h=8),
                                              in0=pq[:, 0, :].rearrange("p (h c) -> p h c", h=8),
                                              in1=ra[:].unsqueeze(2).broadcast_to([128, 8, 64]), op=ALU.mult),
             reads=[ra], writes=[(pq, 0), qa])
        s.op("pool", lambda e: e.tensor_tensor(out=qa[:], in0=qa[:], in1=gqk[:], op=ALU.mult),
             reads=gqk_res, writes=[qa])
        emit_rope(s, qa[:], qa, tb[p][:, 0:64], tb[p][:, 64:128], tb[p], 64, qkb[p][:, 0:512], (qkb[p], 0), tt, uu, False)
        emit_rope(s, pq[:, 1, :], (pq, 1), tb[p][:, 128:160], tb[p][:, 160:192], tb[p], 32, qkb[p][:, 512:1024], (qkb[p], 1), tt, uu, True)
        emit_rope(s, pq[:, 2, :], (pq, 2), tb[p][:, 0:64], tb[p][:, 64:128], tb[p], 64, qkb[p][:, 1024:1536], (qkb[p], 2), tt, uu, True)
        gi = i // 4
        gp = gi % 2
        tl = i % 4
        for c12 in range(12):
            pt_, j = ptq[c12 // 8], c12 % 8
            s.op("pe", lambda e: e.transpose(pt_[:, j, :], qkb[p][:, c12 * 128:(c12 + 1) * 128], identb[:]),
                 reads=[(qkb[p], c12 // 4), identb], writes=[pt_])
        kdst = (lambda j0, n: ktc[:, j0:j0 + n, (i - 32) * 128:(i - 31) * 128]) if isctx else \
               (lambda j0, n: ktacc[:, j0:j0 + n, i * 128:(i + 1) * 128])
        qd = lambda j0, n: qst[gp][:, j0:j0 + n, tl * 128:(tl + 1) * 128]
        s.op("act", lambda e: e.copy(out=qd(0, 3), in_=ptq[0][:, 0:3, :]), reads=[], writes=[ptq[0], (qst[gp], tl, 0)])
        s.op("dve", lambda e: e.tensor_copy(out=kdst(0, 1), in_=ptq[0][:, 3:4, :]), reads=[], writes=[ptq[0], ("ktacc", i, 0)])
        s.op("act", lambda e: e.copy(out=qd(3, 2), in_=ptq[0][:, 4:6, :]), reads=[], writes=[ptq[0], (qst[gp], tl, 1)])
        s.op("dve", lambda e: e.tensor_copy(out=kdst(1, 2), in_=ptq[0][:, 6:8, :]), reads=[], writes=[ptq[0], ("ktacc", i, 1)])
        s.op("act", lambda e: e.copy(out=qd(5, 3), in_=ptq[1][:, 0:3, :]), reads=[], writes=[ptq[1], (qst[gp], tl, 2)])
        s.op("dve", lambda e: e.tensor_copy(out=kdst(3, 1), in_=ptq[1][:, 3:4, :]), reads=[], writes=[ptq[1], ("ktacc", i, 2)])
        ntl = 4 if gi < 8 else 2
        if tl == ntl - 1:
            t0 = gi * 512
            s.dma("sp", qT.rearrange("(c r) t -> r c t", r=128)[:, :, t0:t0 + ntl * 128], qst[gp][:, :, 0:ntl * 128],
                  reads=[(qst[gp], a, b) for a in range(ntl) for b in range(3)],
                  writes=[("qT", gi)] + [(qst[gp], a, b) for a in range(ntl) for b in range(3)], sem="D_qT")
    allk = [("ktacc", i, j) for i in range(NTILE) for j in range(3)]
    allv = [("vacc", i) for i in range(NTILE)]
    s.dma("sp", kT.rearrange("(c r) t -> r c t", r=128), ktacc[:], reads=allk, writes=["kT"], sem="D_kvout")
    s.dma("sp", ckT.rearrange("(c r) t -> r c t", r=128), ktc[:], reads=allk, writes=["ckT"], sem="D_kvout")
    for j in range(4):
        s.dma("sp", vv[j], vacc[:, 2 * j:2 * j + 2, :, :].rearrange("p u k e -> p (u k e)"), reads=allv, writes=["vv"], sem="D_kvout")
    s.dma("sp", cvv, vc[:].rearrange("p u k e -> p (u k e)"), reads=allv, writes=["cvv"], sem="D_kvout")
    s.pop()


def rope_tables(core):
    t = (core % 4) * NLAT + np.arange(NLAT)
    rows = (t // 64).astype(np.float32)
    cols = (t % 64).astype(np.float32)

    def tab(half):
        fr = (10000.0 ** (-np.arange(half, dtype=np.float32) / half)).astype(np.float32)
        ar = rows[:, None] * fr[None, :]
        ac = cols[:, None] * fr[None, :]
        cr, sr, cc, sc = np.cos(ar), np.sin(ar), np.cos(ac), np.sin(ac)
        C = np.concatenate([cr, cr, cc, cc], axis=1)
        S = np.concatenate([-sr, sr, -sc, sc], axis=1)
        return C.astype(np.float32), S.astype(np.float32)

    C64, S64 = tab(16)
    C32, S32 = tab(8)
    lat = np.concatenate([C64, S64, C32, S32], axis=1)
    ctx = np.zeros((NCTX, 192), np.float32)
    ctx[:, 0:64] = 1.0
    ctx[:, 128:160] = 1.0
    return np.ascontiguousarray(np.concatenate([lat, ctx], axis=0))


def cmask_table(core):
    j = np.arange(128)[:, None]
    i = np.arange(128)[None, :]
    mp = (j >= i).astype(np.float32)
    mn = (j <= i).astype(np.float32)
    one = np.ones((128, 128), np.float32)
    pats = np.zeros((8, 128, 512), np.float32)
    for t in range(6):
        for b in range(4):
            blk = mp if t == b else one if t == b + 1 else mn if t == b + 2 else None
            if blk is not None:
                pats[t, :, b * 128:(b + 1) * 128] = blk
    q = core % 4
    pats[6] = pats[0] if q > 0 else 0.0
    pats[7] = pats[5] if q < 3 else 0.0
    return np.ascontiguousarray(pats.transpose(1, 0, 2).reshape(128, 8 * 512)).astype(ml_dtypes.bfloat16)


def emit_attn(s, consts, layer, do_ctx, qT, kg, vgf, kT_own, v_own, ckT, cvv, sel, cmask, lamv, sinkv, gsub, oT):
    identf, identb, onesf = consts
    lam_init = 0.8 - 0.6 * math.exp(-0.3 * layer)
    s.push()
    lq = s.sb("lq", [128, 4, 32], F32)
    for i in range(4):
        s.dma("sp", lq[:, i, :], lamv[i].partition_broadcast(128), writes=[(lq, i)], sem="D_small")
    lp = s.sb("lp", [128, 2, 32], F32)
    ld = s.sb("ld", [128, 2], F32)
    neglam = s.sb("neglam", [128, 1], F32)
    s.op("dve", lambda e: e.tensor_tensor(out=lp[:], in0=lq[:, 0:4:2, :], in1=lq[:, 1:4:2, :], op=ALU.mult),
         reads=[(lq, i) for i in range(4)], writes=[lp])
    s.op("dve", lambda e: e.reduce_sum(out=ld[:], in_=lp[:], axis=AX.X), reads=[lp], writes=[ld])
    s.op("act", lambda e: e.activation(out=ld[:], in_=ld[:], func=AF.Exp), reads=[ld], writes=[ld])
    s.op("dve", lambda e: e.tensor_tensor(out=neglam[:], in0=ld[:, 1:2], in1=ld[:, 0:1], op=ALU.subtract), reads=[ld], writes=[neglam])
    s.op("dve", lambda e: e.tensor_scalar_add(out=neglam[:], in0=neglam[:], scalar1=-lam_init), reads=[neglam], writes=[neglam])
    sinkexp = s.sb("sinkexp", [128, 6], F32)
    s.dma("sp", sinkexp[:], sinkv.partition_broadcast(128), writes=[sinkexp], sem="D_small")
    s.op("act", lambda e: e.activation(out=sinkexp[:], in_=sinkexp[:], func=AF.Exp), reads=[sinkexp], writes=[sinkexp])
    gsubs = s.sb("gsubs", [64, 1], F32)
    s.dma("sp", gsubs[:], gsub.rearrange("(p o) -> p o", o=1), writes=[gsubs], sem="D_small")
    s.op("dve", lambda e: e.tensor_scalar_mul(out=gsubs[:], in0=gsubs[:], scalar1=1.0 - lam_init), reads=[gsubs], writes=[gsubs])
    cm = s.sb("cm", [128, 8, 512], BF16)
    s.dma("sp", cm[:], cmask.rearrange("p (m q) -> p m q", m=8), writes=[cm], sem="D_small")

    NKT = 130
    KT = [s.sb("KT%d" % i, [128, NKT * 128], BF16) for i in range(2)]
    VA = [s.sb("VA%d" % i, [128, NKT, 128], BF16) for i in range(2)]
    QB = [s.sb("QB%d" % i, [128, 3, 512], BF16) for i in range(2)]
    QBB = [s.sb("QBB%d" % i, [128, 2, 512], BF16) for i in range(2)]
    for i in range(2):
        s.op("pool", lambda e: e.memset(VA[i][:, :, 64:128], 1.0), writes=[("VAones", i)])
        s.op("pool", lambda e: e.memset(KT[i][64:128, :], 0.0), writes=[("KTz", i)])
        s.op("pool", lambda e: e.memset(QB[i][64:128, :, :], 0.0), writes=[("QBz", i)])
        s.op("pool", lambda e: e.memset(QBB[i][:], 0.0), writes=[("QB", "B", i)])
    PT = [s.sb("PT%d" % i, [128, 2, 512], BF16) for i in range(3)]
    rs = [s.sb("rs%d" % i, [128, 512], F32) for i in range(2)]
    bcs = [s.sb("bcs%d" % i, [64, 512], F32) for i in range(2)]
    t1 = s.sb("t1", [64, 512], F32)
    t2 = s.sb("t2", [64, 512], F32)
    od = s.sb("od", [64, 512], F32)
    sq = s.sb("sq", [64, 512], F32)
    rst = s.sb("rst", [64, 512], F32)
    OTs = [s.sb("OTs%d" % i, [64, 512], BF16) for i in range(2)]
    Sb = [s.ps("Sb%d" % i, [128, 2, 512], F32) for i in range(3)]
    Ob = [s.ps("Ob%d" % i, [128, 512], F32) for i in range(2)]

    def load_kv(buf, krow0, unit):
        for r in range(4):
            s.dma("sp", KT[buf][0:64, r * NLAT:(r + 1) * NLAT], kg(r, krow0), reads=[("kTg", krow0 // 128)], writes=[("KT", buf)],
                  sem="D_kv%d" % buf)
            s.dma("sp", VA[buf][:, r * 32:(r + 1) * 32, 0:64],
                  vgf(r, unit).rearrange("p (k e) -> p k e", e=64), reads=[("vg", unit // 2)], writes=[("VA", buf)], sem="D_kv%d" % buf)
        s.dma("sp", KT[buf][0:64, 4 * NLAT:4 * NLAT + NCTX], ckT[krow0:krow0 + 64, :], writes=[("KT", buf)], sem="D_kv%d" % buf)
        s.dma("sp", VA[buf][:, 128:130, 0:64], cvv[:, unit * 128:(unit + 1) * 128].rearrange("p (k e) -> p k e", e=64),
              writes=[("VA", buf)], sem="D_kv%d" % buf)

    candk = s.sb("candk", [64, 4, 128], BF16)
    candv = s.sb("candv", [128, 4, 64], BF16)
    acck = s.sb("acck", [64, 128], F32)
    accv = s.sb("accv", [128, 64], F32)
    selt = s.sb("selt", [128, 8], F32)
    s.dma("sp", selt[:], sel, writes=[selt], sem="D_small")

    def load_kv_c(buf, g):
        KC, VC = KT[buf], VA[buf]
        sem = "D_kv%d" % buf
        s.dma("sp", KC[0:64, 128:128 + NLAT], kT_own[384 + g * 64:384 + (g + 1) * 64, :], writes=[("KT", buf)], sem=sem)
        s.dma("sp", KC[0:64, 34 * 128:36 * 128], ckT[384 + g * 64:384 + (g + 1) * 64, :], writes=[("KT", buf)], sem=sem)
        s.dma("sp", VC[:, 1:33, 0:64], v_own(6 + g).rearrange("p (k e) -> p k e", e=64), writes=[("VA", buf)], sem=sem)
        s.dma("sp", VC[:, 34:36, 0:64], cvv[:, (6 + g) * 128:(7 + g) * 128].rearrange("p (k e) -> p k e", e=64), writes=[("VA", buf)], sem=sem)
        for side in range(2):
            for r in range(4):
                kcols = (NLAT - 128, NLAT) if side == 0 else (0, 128)
                s.dma("sp", candk[:, r, :], kg(r, 384 + g * 64)[:, kcols[0]:kcols[1]], reads=[("kTg", 3)], writes=[candk], sem="D_cand")
                vt = 31 if side == 0 else 0
                s.dma("sp", candv[:, r, :], vgf(r, 6 + g)[:, vt * 64:(vt + 1) * 64], reads=[("vg", 3)], writes=[candv], sem="D_cand")
            for (cand, acc, np_) in ((candk, acck, 64), (candv, accv, 128)):
                s.op("dve", lambda e: e.tensor_scalar(out=acc[:], in0=cand[:, 0, :], scalar1=selt[0:np_, 4 * side:4 * side + 1], scalar2=None,
                                                      op0=ALU.mult), reads=[cand, selt], writes=[acc])
                for r in range(1, 4):
                    s.op("dve", lambda e: e.scalar_tensor_tensor(out=acc[:], in0=cand[:, r, :], scalar=selt[0:np_, 4 * side + r:4 * side + r + 1],
                                                                 in1=acc[:], op0=ALU.mult, op1=ALU.add), reads=[cand, selt], writes=[acc])
            kc0 = 0 if side == 0 else 33 * 128
            s.op("dve", lambda e: e.tensor_copy(out=KC[0:64, kc0:kc0 + 128], in_=acck[:]), reads=[acck], writes=[("KT", buf)])
            s.op("dve", lambda e: e.tensor_copy(out=VC[:, 0 if side == 0 else 33, 0:64], in_=accv[:]), reads=[accv], writes=[("VA", buf)])

    def load_group(gi):
        kind, g = groups[gi]
        if kind == "C":
            load_kv_c(gi % 2, g)
        else:
            kk, uu = kv_spec(kind, g)
            load_kv(gi % 2, kk, uu)

    def bcast_row(dst_ps, dres, row_ap, nq, rres):
        s.op("pe", lambda e: e.matmul(dst_ps[0:64, 0:nq], lhsT=onesf[64:65, 0:64], rhs=row_ap, start=True, stop=True),
             reads=[rres, onesf], writes=[dres])

    groups = [("A", g) for g in range(2)] + [("B", h) for h in range(4)] + [("C", g) for g in range(2)]
    qblocks = [(j * 512, 512, list(range(128)) + [128, 129]) for j in range(8)]
    if do_ctx:
        qblocks.append((NLAT, NCTX, [128, 129]))

    def kv_spec(kind, g):
        if kind == "A":
            return g * 64, g
        return 128 + g * 64, 2 + g

    def q_rows(kind, g):
        return (g * 192, 3) if kind == "A" else (640 + g * 192, 3) if kind == "C" else (384 + g * 64, 1)

    cblocks = [(j * 512, 512, [4 * j + t for t in range(6)] + [34, 35]) for j in range(8)]
    if do_ctx:
        cblocks.append((NLAT, NCTX, [34, 35]))
    jobs = []
    for gi, (kind, g) in enumerate(groups):
        for bi, (q0, nq, tiles) in enumerate(cblocks if kind == "C" else qblocks):
            jb = dict(gi=gi, kind=kind, g=g, q0=q0, nq=nq, tiles=tiles, masks=None)
            if kind == "C" and nq == 512:
                pats = list(range(6))
                if bi == 0:
                    pats[0] = 6
                if bi == 7:
                    pats[5] = 7
                jb["masks"] = pats
            jobs.append(jb)

    def load_q(jn):
        jb = jobs[jn]
        r0, nh = q_rows(jb["kind"], jb["g"])
        q0, nq = jb["q0"], jb["nq"]
        if jb["kind"] in ("A", "C"):
            s.dma("sp", QB[jn % 2][0:64, 0:nh, 0:nq], qT[r0:r0 + nh * 64, q0:q0 + nq].rearrange("(h r) t -> r h t", r=64),
                  writes=[("QB", jn % 2)], sem="D_q%d" % (jn % 2))
        else:
            for k in range(2):
                s.dma("sp", QBB[jn % 2][32 * k:32 * k + 32, k, 0:nq], qT[r0 + 32 * k:r0 + 32 * k + 32, q0:q0 + nq],
                      writes=[("QB", "B", jn % 2)], sem="D_qb%d" % (jn % 2))

    steps = []
    ocnt = 0
    for jn, jb in enumerate(jobs):
        kind, nq, tiles = jb["kind"], jb["nq"], jb["tiles"]
        if kind in ("A", "C"):
            for r in range(3):
                ob = ocnt % 2
                ocnt += 1
                npair = len(tiles) // 2
                for pi in range(npair):
                    mk = None
                    if jb["masks"] is not None and pi < 3:
                        mk = (jb["masks"][2 * pi], jb["masks"][2 * pi + 1])
                    steps.append(dict(jn=jn, kind=kind, r=r, ob=ob, t=(tiles[2 * pi], tiles[2 * pi + 1]),
                                      first=(pi == 0), last=(pi == npair - 1), mk=mk))
                steps.append(dict(jn=jn, fin=True, src=steps[-1]))
        else:
            for ti, t in enumerate(tiles):
                steps.append(dict(jn=jn, kind="B", t=(t,), first=(ti == 0), last=(ti == len(tiles) - 1)))
            steps.append(dict(jn=jn, fin=True, src=steps[-1]))

    def emit_S(i):
        st = steps[i]
        if st.get("fin"):
            return
        jb = jobs[st["jn"]]
        buf, qb, nq, sb = jb["gi"] % 2, st["jn"] % 2, jb["nq"], Sb[i % 3]
        if st["kind"] in ("A", "C"):
            rd = [("KT", buf), ("KTz", buf), ("QB", qb), ("QBz", qb)]
            for k, t in enumerate(st["t"]):
                s.op("pe", lambda e: e.matmul(sb[:, k, 0:nq], lhsT=KT[buf][:, t * 128:(t + 1) * 128],
                                              rhs=QB[qb][:, st["r"], 0:nq], start=True, stop=True), reads=rd, writes=[sb])
        else:
            rd = [("KT", buf), ("KTz", buf), ("QB", "B", qb)]
            t = st["t"][0]
            for k in range(2):
                s.op("pe", lambda e: e.matmul(sb[:, k, 0:nq], lhsT=KT[buf][:, t * 128:(t + 1) * 128],
                                              rhs=QBB[qb][:, k, 0:nq], start=True, stop=True), reads=rd, writes=[sb])

    def emit_rest(i):
        st = steps[i]
        fin = st.get("fin", False)
        if fin:
            st = st["src"]
        jb = jobs[st["jn"]]
        buf, nq, sb, pt = jb["gi"] % 2, jb["nq"], Sb[i % 3], PT[i % 3]
        kind = st["kind"]
        if fin:
            emit_fin(st, jb, kind, nq, sb)
            return
        scale = 32 ** -0.5 if kind == "B" else 0.125
        s.op("act", lambda e: e.activation(out=pt[:, :, 0:nq], in_=sb[:, :, 0:nq], func=AF.Exp, scale=scale),
             reads=[], writes=[sb, pt])
        if st.get("mk") is not None:
            m0, m1 = st["mk"]
            if m1 == m0 + 1:
                s.op("dve", lambda e: e.tensor_tensor(out=pt[:], in0=pt[:], in1=cm[:, m0:m0 + 2, :], op=ALU.mult), reads=[cm], writes=[pt])
            else:
                for k, mm in enumerate((m0, m1)):
                    s.op("dve", lambda e: e.tensor_tensor(out=pt[:, k, :], in0=pt[:, k, :], in1=cm[:, mm, :], op=ALU.mult), reads=[cm], writes=[pt])
        rdv = [("VA", buf), ("VAones", buf), pt]
        if kind in ("A", "C"):
            o = Ob[st["ob"]]
            for k, t in enumerate(st["t"]):
                s.op("pe", lambda e: e.matmul(o[:, 0:nq], lhsT=VA[buf][:, t, :], rhs=pt[:, k, 0:nq],
                                              start=(st["first"] and k == 0), stop=(st["last"] and k == 1)), reads=rdv, writes=[o])
        else:
            t = st["t"][0]
            for k in range(2):
                s.op("pe", lambda e: e.matmul(Ob[k][:, 0:nq], lhsT=VA[buf][:, t, :], rhs=pt[:, k, 0:nq],
                                              start=st["first"], stop=st["last"]), reads=rdv, writes=[Ob[k]])

    def emit_fin(st, jb, kind, nq, sb):
        Mb = [sb[:, 0, :], sb[:, 1, :]]
        q0 = jb["q0"]
        if kind in ("A", "C"):
            o, ob = Ob[st["ob"]], st["ob"]
            if kind == "C":
                hh = 3 * jb["g"] + st["r"]
                s.op("dve", lambda e: e.tensor_scalar(out=rs[ob][64:65, 0:nq], in0=o[64:65, 0:nq], scalar1=sinkexp[64:65, hh:hh + 1],
                                                      scalar2=None, op0=ALU.add), reads=[sinkexp], writes=[o, rs[ob]])
                s.op("dve", lambda e: e.reciprocal(out=rs[ob][64:65, 0:nq], in_=rs[ob][64:65, 0:nq]), reads=[], writes=[rs[ob]])
            else:
                s.op("dve", lambda e: e.reciprocal(out=rs[ob][64:65, 0:nq], in_=o[64:65, 0:nq]), reads=[], writes=[o, rs[ob]])
            bcast_row(Mb[0], sb, rs[ob][64:65, 0:nq], nq, rs[ob])
            s.op("act", lambda e: e.copy(out=bcs[ob][:, 0:nq], in_=Mb[0][0:64, 0:nq]), reads=[], writes=[sb, bcs[ob]])
            s.op("dve", lambda e: e.tensor_tensor(out=OTs[ob][:, 0:nq], in0=o[0:64, 0:nq], in1=bcs[ob][:, 0:nq], op=ALU.mult),
                 reads=[bcs[ob]], writes=[o, OTs[ob]])
            row0 = ((0 if kind == "A" else 10) + jb["g"] * 3 + st["r"]) * 64
            s.dma("sp", oT[row0:row0 + 64, q0:q0 + nq], OTs[ob][:, 0:nq], reads=[OTs[ob]], writes=[("oT", row0, q0)], sem="D_oT")
        else:
            for k in range(2):
                s.op("dve", lambda e: e.reciprocal(out=rs[k][64:65, 0:nq], in_=Ob[k][64:65, 0:nq]), reads=[], writes=[Ob[k], rs[k]])
            s.op("dve", lambda e: e.tensor_scalar(out=rs[1][64:65, 0:nq], in0=rs[1][64:65, 0:nq], scalar1=neglam[64:65, 0:1],
                                                  scalar2=None, op0=ALU.mult), reads=[neglam], writes=[rs[1]])
            for k in range(2):
                bcast_row(Mb[k], sb, rs[k][64:65, 0:nq], nq, rs[k])
                s.op("act", lambda e: e.copy(out=bcs[k][:, 0:nq], in_=Mb[k][0:64, 0:nq]), reads=[], writes=[sb, bcs[k]])
            s.op("dve", lambda e: e.tensor_tensor(out=t1[:, 0:nq], in0=Ob[0][0:64, 0:nq], in1=bcs[0][:, 0:nq], op=ALU.mult),
                 reads=[bcs[0]], writes=[Ob[0], t1])
            s.op("dve", lambda e: e.tensor_tensor(out=t2[:, 0:nq], in0=Ob[1][0:64, 0:nq], in1=bcs[1][:, 0:nq], op=ALU.mult),
                 reads=[bcs[1]], writes=[Ob[1], t2])
            s.op("pool", lambda e: e.tensor_tensor(out=od[:, 0:nq], in0=t1[:, 0:nq], in1=t2[:, 0:nq], op=ALU.add), reads=[t1, t2], writes=[od])
            s.op("pool", lambda e: e.tensor_tensor(out=sq[:, 0:nq], in0=od[:, 0:nq], in1=od[:, 0:nq], op=ALU.mult), reads=[od], writes=[sq])
            s.op("pe", lambda e: e.matmul(Mb[0][0:64, 0:nq], lhsT=onesf[0:64, 0:64], rhs=sq[:, 0:nq], start=True, stop=True),
                 reads=[sq, onesf], writes=[sb])
            s.op("act", lambda e: e.activation(out=rst[:, 0:nq], in_=Mb[0][0:64, 0:nq], func=AF.Ln, scale=1.0 / 64, bias=EPS),
                 reads=[], writes=[sb, rst])
            s.op("act", lambda e: e.activation(out=rst[:, 0:nq], in_=rst[:, 0:nq], func=AF.Exp, scale=-0.5), reads=[rst], writes=[rst])
            s.op("dve", lambda e: e.scalar_tensor_tensor(out=OTs[0][:, 0:nq], in0=od[:, 0:nq], scalar=gsubs[:, 0:1], in1=rst[:, 0:nq],
                                                         op0=ALU.mult, op1=ALU.mult), reads=[od, gsubs, rst], writes=[OTs[0]])
            row0 = (6 + jb["g"]) * 64
            s.dma("sp", oT[row0:row0 + 64, q0:q0 + nq], OTs[0][:, 0:nq], reads=[OTs[0]], writes=[("oT", row0, q0)], sem="D_oT")

    load_group(0)
    load_q(0)
    emit_S(0)
    emit_S(1)
    cur_job = -1
    for i, st in enumerate(steps):
        if st["jn"] != cur_job:
            cur_job = st["jn"]
            jb = jobs[cur_job]
            if cur_job + 1 < len(jobs):
                load_q(cur_job + 1)
            if (cur_job == 0 or jobs[cur_job - 1]["gi"] != jb["gi"]) and jb["gi"] + 1 < len(groups):
                load_group(jb["gi"] + 1)
        if i + 2 < len(steps):
            emit_S(i + 2)
        emit_rest(i)

    s.pop()


def emit_ffn_mod(s, consts, do_ctx, cvec, w_ada, b_ada, g_ffn, modd):
    identf = consts[0]
    s.push()
    gff = s.sb("gff", [128, D], F32)
    s.dma("sp", gff[:], g_ffn.partition_broadcast(128), writes=[gff])
    mod = s.sb("modm", [128, 4 * D], F32)
    modres = [(mod, i) for i in range(8)]
    for stream in range(2 if do_ctx else 1):
        emit_mod(s, cvec, stream, w_ada, b_ada, 2 * D, 4 * D, mod, identf)
        s.op("dve", lambda e: e.scalar_tensor_tensor(out=mod[:, 2 * D:3 * D], in0=mod[:, 2 * D:3 * D], scalar=1.0, in1=gff[:],
                                                     op0=ALU.add, op1=ALU.mult), reads=[gff] + modres, writes=modres)
        s.dma("sp", modd[stream], mod[:], reads=modres, writes=[("modd", stream)] + modres, sem="D_modd")
    s.pop()


def emit_ffn(s, consts, last, do_ctx, xs, modd, w_out, w_ff1, w_ff3, w_ff2, g_final, oT, xout):
    identf, identb, onesf = consts
    s.push()
    wout = s.sb("wout", [128, 8, D], BF16)
    w1 = s.sb("w1", [128, 8, DFF], BF16)
    w3 = s.sb("w3", [128, 8, DFF], BF16)
    w2 = s.sb("w2", [128, NFF, D], BF16)
    wres = []
    for k in range(8):
        s.dma("pool", wout[:, k, :], w_out[k * 128:(k + 1) * 128, :], writes=[(wout, k)], sem="D_w")
        s.dma("pool", w1[:, k, :], w_ff1[k * 128:(k + 1) * 128, :], writes=[(w1, k)], sem="D_w")
        s.dma("pool", w3[:, k, :], w_ff3[k * 128:(k + 1) * 128, :], writes=[(w3, k)], sem="D_w")
        wres += [(wout, k), (w1, k), (w3, k)]
    for k in range(0, NFF, 2):
        s.dma("pool", w2[:, k:k + 2, :], w_ff2[k * 128:(k + 2) * 128, :].rearrange("(c p) n -> p c n", p=128), writes=[(w2, k)], sem="D_w")
        wres.append((w2, k))
    gfin = None
    if last:
        gfin = s.sb("gfin", [128, D], F32)
        s.dma("sp", gfin[:], g_final.partition_broadcast(128), writes=[gfin])
    mod = s.sb("modf", [128, 4 * D], F32)
    modres = [(mod, i) for i in range(8)]
    xt = [s.sb("fxt%d" % i, [128, D], F32) for i in range(2)]
    x1 = [s.sb("x1_%d" % i, [128, D], F32) for i in range(2)]
    ss = [s.sb("fss%d" % i, [128, 1], F32) for i in range(2)]
    rstd = [s.sb("frstd%d" % i, [128, 1], F32) for i in range(2)]
    hb = [s.sb("fhb%d" % i, [128, D], BF16) for i in range(2)]
    h2T = s.sb("h2T", [128, 8, 128], BF16)
    OTt = [s.sb("OTt%d" % i, [128, 8, 128], BF16) for i in range(2)]
    sa = s.sb("sa", [128, 512], F32)
    junk = sa[:].bitcast(BF16)
    uT = s.sb("uT", [128, NFF, 128], BF16)
    pya = s.ps("pya", [128, 2, 512], F32)
    py = s.ps("py", [128, 2, 512], F32)
    pa = [s.ps("pa%d" % i, [128, 512], F32) for i in range(2)]
    pb = [s.ps("pb%d" % i, [128, 512], F32) for i in range(2)]
    pT = pa[0][:].bitcast(BF16).rearrange("p (k t) -> p k t", k=8)
    grps = [list(range(c, min(c + 4, NFF))) for c in range(0, NFF, 4)]

    def stage_a(n, i):
        p = n % 2
        s.dma("sp", xt[p][:], xs[i * 128:(i + 1) * 128, :], writes=[xt[p]])
        s.dma("sp", OTt[p][:], oT.rearrange("(c r) t -> r c t", r=128)[:, :, i * 128:(i + 1) * 128], writes=[OTt[p]])
        for h in range(2):
            for c in range(8):
                s.op("pe", lambda e: e.matmul(pya[:, h, :], lhsT=OTt[p][:, c, :], rhs=wout[:, c, h * 512:(h + 1) * 512],
                                              start=(c == 0), stop=(c == 7)), reads=[OTt[p]] + wres, writes=[pya])
        s.op("dve", lambda e: e.tensor_tensor(out=x1[p][:], in0=pya[:].rearrange("p a b -> p (a b)"), in1=mod[:, 0:D], op=ALU.mult),
             reads=modres, writes=[pya, x1[p]])
        s.op("dve", lambda e: e.tensor_tensor(out=x1[p][:], in0=x1[p][:], in1=xt[p][:], op=ALU.add), reads=[xt[p]], writes=[x1[p]])
        s.op("act", lambda e: e.activation(out=junk, in_=x1[p][:], func=AF.Square, accum_out=ss[p][:]), reads=[x1[p]], writes=[sa, ss[p]])
        emit_rstd(s, ss[p][:], rstd[p][:], D, [ss[p]], [rstd[p]])
        s.op("dve", lambda e: e.scalar_tensor_tensor(out=xt[p][:], in0=x1[p][:], scalar=rstd[p][:, 0:1], in1=mod[:, 2 * D:3 * D],
                                                     op0=ALU.mult, op1=ALU.mult), reads=[x1[p], rstd[p]] + modres, writes=[xt[p]])
        s.op("pool", lambda e: e.tensor_tensor(out=hb[p][:], in0=xt[p][:], in1=mod[:, D:2 * D], op=ALU.add),
             reads=[xt[p]] + modres, writes=[hb[p]])

    def stage_b(n, i):
        p = n % 2
        for k in range(8):
            s.op("pe", lambda e: e.transpose(pT[:, k, :], hb[p][:, k * 128:(k + 1) * 128], identb[:]), reads=[hb[p], identb], writes=[pa[0]])
        s.op("act", lambda e: e.copy(out=h2T[:], in_=pT), reads=[], writes=[pa[0], h2T])
        for gi, grp in enumerate(grps):
            gp = gi % 2
            n_ = len(grp) * 128
            for (w_, pp) in ((w1, pa[gp]), (w3, pb[gp])):
                for j, c in enumerate(grp):
                    for k in range(8):
                        s.op("pe", lambda e: e.matmul(pp[:, j * 128:(j + 1) * 128], lhsT=w_[:, k, c * 128:(c + 1) * 128],
                                                      rhs=h2T[:, k, :], start=(k == 0), stop=(k == 7)), reads=[h2T] + wres, writes=[pp])
            s.op("act", lambda e: e.activation(out=sa[:, 0:n_], in_=pa[gp][:, 0:n_], func=AF.Silu), reads=[], writes=[pa[gp], sa])
            s.op("dve", lambda e: e.tensor_tensor(out=uT[:, grp[0]:grp[0] + len(grp), :].rearrange("p c t -> p (c t)"),
                                                  in0=sa[:, 0:n_], in1=pb[gp][:, 0:n_], op=ALU.mult),
                 reads=[sa], writes=[pb[gp], (uT, gi)])
        for h in range(2):
            for c in range(NFF):
                s.op("pe", lambda e: e.matmul(py[:, h, :], lhsT=uT[:, c, :], rhs=w2[:, c, h * 512:(h + 1) * 512],
                                              start=(c == 0), stop=(c == NFF - 1)),
                     reads=[(uT, g_) for g_ in range(len(grps))] + wres, writes=[py])
        s.op("dve", lambda e: e.tensor_tensor(out=xt[p][:], in0=py[:].rearrange("p a b -> p (a b)"), in1=mod[:, 3 * D:4 * D], op=ALU.mult),
             reads=modres, writes=[py, xt[p]])
        s.op("pool", lambda e: e.tensor_tensor(out=xt[p][:], in0=xt[p][:], in1=x1[p][:], op=ALU.add), reads=[x1[p]], writes=[xt[p]])
        if last:
            s.op("act", lambda e: e.activation(out=junk, in_=xt[p][:], func=AF.Square, accum_out=ss[p][:]), reads=[xt[p]], writes=[sa, ss[p]])
            emit_rstd(s, ss[p][:], rstd[p][:], D, [ss[p]], [rstd[p]])
            s.op("dve", lambda e: e.scalar_tensor_tensor(out=xt[p][:], in0=xt[p][:], scalar=rstd[p][:, 0:1], in1=gfin[:],
                                                         op0=ALU.mult, op1=ALU.mult), reads=[rstd[p], gfin], writes=[xt[p]])
        s.dma("sp", xout[i * 128:(i + 1) * 128, :], xt[p][:], reads=[xt[p]], writes=[("xout", i)], sem="D_xout")

    def run_tiles(tile_ids):
        stage_a(0, tile_ids[0])
        for n, i in enumerate(tile_ids):
            if n + 1 < len(tile_ids):
                stage_a(n + 1, tile_ids[n + 1])
            stage_b(n, i)

    for stream in range(2 if do_ctx else 1):
        s.dma("sp", mod[:], modd[stream], reads=[("modd", stream)], writes=modres)
        run_tiles(list(range(32)) if stream == 0 else [32, 33])
    s.pop()


def build_fused():
    nc = bass.Bass("TRN2", target_bir_lowering=False)
    di = lambda n, sh, dt_=F32: nc.dram_tensor(n, sh, dt_, kind="ExternalInput").ap()
    it = lambda n, sh, dt_=BF16: nc.dram_tensor(n, sh, dt_, kind="Internal").ap()
    xs_in = di("xs", [NTOK, D]); cvec = di("cvec", [2, D]); rope = di("rope", [NTOK, 192])
    cmask = di("cmask", [128, 8 * 512], BF16); sel = di("sel", [128, 8])
    w_ada = di("w_ada", [DEPTH, D, 6 * D]); b_ada = di("b_ada", [DEPTH, 6 * D]); g_attn = di("g_attn", [DEPTH, D])
    g_ffn = di("g_ffn", [DEPTH, D]); w_in = di("w_in", [DEPTH, D, 2048]); g_qk = di("g_qk", [DEPTH, 2, 64])
    lamv = di("lamv", [DEPTH, 4, 32]); sinkv = di("sinkv", [DEPTH, 6]); gsub = di("gsub", [DEPTH, 64])
    w_out = di("w_out", [DEPTH, D, D]); w_ff1 = di("w_ff1", [DEPTH, D, DFF]); w_ff3 = di("w_ff3", [DEPTH, D, DFF])
    w_ff2 = di("w_ff2", [DEPTH, DFF, D]); g_final = di("g_final", [D])
    xout = nc.dram_tensor("xout", [NLAT, D], F32, kind="ExternalOutput").ap()
    qT = it("qT_s", [1024, NTOK]); kT = it("kT_s", [512, NLAT]); vv = [it("vv_s%d" % j, [128, NLAT]) for j in range(4)]
    ckT = it("ckT_s", [512, NCTX]); cvv = it("cvv_s", [128, 1024]); oT = it("oT_s", [1024, NTOK])
    kTg = [it("kTg%d" % j, [512, NLAT]) for j in range(4)]
    vg = [it("vg%d" % j, [512, NLAT]) for j in range(4)]
    xs1 = it("xs1_s", [NTOK, D], F32)
    modd = it("modd_s", [2, 128, 4 * D], F32)
    kg = lambda r, krow0: kTg[krow0 // 128][r * 128 + krow0 % 128:r * 128 + krow0 % 128 + 64, :]
    vgf = lambda r, unit: vg[unit // 2][r * 128:(r + 1) * 128, (unit % 2) * 2048:(unit % 2 + 1) * 2048]
    s = Sched(nc)
    consts = emit_consts(s)
    xs = xs_in
    for l in range(DEPTH):
        last = l == DEPTH - 1
        emit_pre(s, consts, xs, cvec, w_ada[l], b_ada[l], g_attn[l], w_in[l], g_qk[l], rope, qT, kT, vv, ckT, cvv)
        for j in range(4):
            s.collective(kT[j * 128:(j + 1) * 128, :], kTg[j], reads=["kT"], writes=[("kTg", j)], sem="CCk%d" % j)
            s.collective(vv[j], vg[j], reads=["vv"], writes=[("vg", j)], sem="CCv%d" % j)
        s.barrier()
        vown = lambda unit: vv[unit // 2][:, (unit % 2) * 2048:(unit % 2 + 1) * 2048]
        emit_attn(s, consts, l, not last, qT, kg, vgf, kT, vown, ckT, cvv, sel, cmask, lamv[l], sinkv[l], gsub[l], oT)
        emit_ffn_mod(s, consts, not last, cvec, w_ada[l], b_ada[l], g_ffn[l], modd)
        emit_ffn(s, consts, last, not last, xs, modd, w_out[l], w_ff1[l], w_ff3[l], w_ff2[l], g_final, oT, xout if last else xs1)
        xs = xs1
    s.finish()
    return nc, s


_PROG = {}


def kernel(x, c, ctx, c_ctx, w_ada, b_ada, g_attn, g_ffn, w_in, g_q, g_k, lam_q1, lam_k1, lam_q2, lam_k2,
           g_subln, sink_logit, w_out, w_ff1, w_ff3, w_ff2, g_final):
    f = lambda a: np.ascontiguousarray(np.asarray(a, dtype=np.float32))
    x, c, ctx, c_ctx = f(x), f(c), f(ctx), f(c_ctx)
    cores = list(range(8))
    shared = dict(w_ada=f(w_ada), b_ada=f(b_ada), g_attn=f(g_attn), g_ffn=f(g_ffn), w_in=f(w_in),
                  g_qk=np.ascontiguousarray(np.stack([f(g_q), f(g_k)], axis=1)),
                  lamv=np.ascontiguousarray(np.stack([f(lam_q1), f(lam_k1), f(lam_q2), f(lam_k2)], axis=1)),
                  sinkv=f(sink_logit), gsub=f(g_subln), w_out=f(w_out), w_ff1=f(w_ff1), w_ff3=f(w_ff3), w_ff2=f(w_ff2),
                  g_final=f(g_final))
    maps = []
    for i in cores:
        b, q = i // 4, i % 4
        sel = np.zeros((128, 8), np.float32)
        if q > 0:
            sel[:, q - 1] = 1.0
        if q < 3:
            sel[:, 4 + q + 1] = 1.0
        maps.append(dict(xs=np.ascontiguousarray(np.concatenate([x[b, q * NLAT:(q + 1) * NLAT], ctx[b]], 0)),
                         cvec=np.ascontiguousarray(np.stack([c[b], c_ctx])), rope=rope_tables(i), cmask=cmask_table(i), sel=sel,
                         **shared))
    if "fused" not in _PROG:
        _PROG["fused"] = build_fused()[0]
    res = run_bass_kernel_spmd(_PROG["fused"], maps, core_ids=cores).results
    out = np.empty((2, SEQ, D), np.float32)
    for i in cores:
        out[i // 4, (i % 4) * NLAT:(i % 4 + 1) * NLAT] = np.asarray(res[i]["xout"])
    return out
```

```python
import contextlib
import math
import numpy as np
import ml_dtypes
import concourse.bass as bass
import concourse.mybir as mybir
from concourse.bass_utils import run_bass_kernel_spmd

F32 = mybir.dt.float32
BF16 = mybir.dt.bfloat16
AF = mybir.ActivationFunctionType
ALU = mybir.AluOpType
AX = mybir.AxisListType

D = 1024
SEQ = 16384
NLAT = 4096
NCTX = 256
NTOK = NLAT + NCTX
NTILE = NTOK // 128
DFF = 2816
NFF = DFF // 128
DEPTH = 2
EPS = 1e-6
AQ, AK, AV, BQ, BK, BV, CQ, CK, CV = [(0, 384), (384, 512), (512, 640), (640, 896), (896, 1152), (1152, 1408),
                                      (1408, 1792), (1792, 1920), (1920, 2048)]
WIN_ORDER = [AQ, AK, BQ, BK, CQ, CK, AV, BV, CV]


class Sched:
    def __init__(self, nc):
        self.nc = nc
        self.eng = {"pe": nc.tensor, "act": nc.scalar, "dve": nc.vector, "pool": nc.gpsimd, "sp": nc.sync}
        self.stack = contextlib.ExitStack()
        self.scopes = []
        self.sems = {}
        self.cnt = {}
        self.waited = {e: {} for e in self.eng}
        self.last_w = {}
        self.readers = {}
        for e in self.eng:
            self._sem("E_" + e)
        self.n_inst = 0
        self.n_wait = 0
        self.uid = 0
        self.defer_prefix = "\0"

    def _sem(self, name):
        if name not in self.sems:
            self.sems[name] = self.stack.enter_context(self.nc.semaphore(name))
            self.cnt[name] = 0
        return self.sems[name]

    def push(self):
        st = contextlib.ExitStack()
        self.scopes.append(st)
        return st

    def pop(self):
        self.barrier()
        self.scopes.pop().close()

    def _ctx(self):
        return self.scopes[-1] if self.scopes else self.stack

    def sb(self, name, shape, dtype):
        self.uid += 1
        return self._ctx().enter_context(self.nc.sbuf_tensor("%s_%d" % (name, self.uid), list(shape), dtype))

    def ps(self, name, shape, dtype):
        self.uid += 1
        return self._ctx().enter_context(self.nc.psum_tensor("%s_%d" % (name, self.uid), list(shape), dtype))

    @staticmethod
    def _key(r):
        if isinstance(r, tuple):
            return tuple(Sched._key(x) for x in r)
        if isinstance(r, (str, int)):
            return r
        return id(r)

    def _deps(self, reads, writes):
        d = {}
        for r in reads:
            t = self.last_w.get(self._key(r))
            if t:
                d[t[0]] = max(d.get(t[0], 0), t[1])
        for w in writes:
            k = self._key(w)
            t = self.last_w.get(k)
            if t:
                d[t[0]] = max(d.get(t[0], 0), t[1])
            for t in self.readers.get(k, ()):
                d[t[0]] = max(d.get(t[0], 0), t[1])
        return d

    def _emit_waits(self, en, deps, skip_self=False):
        e = self.eng[en]
        wd = self.waited[en]
        for sn, v in deps.items():
            if skip_self and sn == "E_" + en:
                continue
            if wd.get(sn, 0) < v:
                e.wait_ge(self.sems[sn], v)
                wd[sn] = v
                self.n_wait += 1

    def _record(self, tok, reads, writes):
        for r in reads:
            self.readers.setdefault(self._key(r), []).append(tok)
        for w in writes:
            k = self._key(w)
            self.last_w[k] = tok
            self.readers[k] = []

    def op(self, en, fn, reads=(), writes=()):
        deps = self._deps(reads, writes)
        self._emit_waits(en, deps, skip_self=(en == "pe"))
        ins = fn(self.eng[en])
        sn = "E_" + en
        self.cnt[sn] += 1
        ins.then_inc(self.sems[sn], 1)
        self._record((sn, self.cnt[sn]), reads, writes)
        self.n_inst += 1
        return ins

    def dma(self, q, out, in_, reads=(), writes=(), sem=None, **kw):
        if sem is None:
            sem = "D_%x" % (hash(self._key(writes[0])) & 0xFFFFFFF)
        self._sem(sem)
        deps = self._deps(reads, writes)
        self._emit_waits(q, deps)
        ins = self.eng[q].dma_start(out=out, in_=in_, **kw)
        self.cnt[sem] += 16
        ins.then_inc(self.sems[sem], 16)
        self._record((sem, self.cnt[sem]), reads, writes)
        self.n_inst += 1
        return ins

    def collective(self, in_ap, out_ap, reads=(), writes=(), sem="CC"):
        self._sem(sem)
        deps = self._deps(reads, writes)
        self._emit_waits("pool", deps)
        ins = self.nc.gpsimd.collective_compute("AllGather", ALU.bypass, replica_groups=[[0, 1, 2, 3], [4, 5, 6, 7]],
                                                ins=[in_ap], outs=[out_ap])
        self.cnt[sem] += 1
        ins.then_inc(self.sems[sem], 1)
        self._record((sem, self.cnt[sem]), reads, writes)
        self.n_inst += 1
        return ins

    def barrier(self):
        allv = {sn: c for sn, c in self.cnt.items() if c > 0 and not sn.startswith(self.defer_prefix)}
        for en in self.eng:
            self._emit_waits(en, allv)

    def finish(self):
        self.barrier()
        while self.scopes:
            self.scopes.pop().close()
        self.stack.close()


def emit_consts(s):
    identf = s.sb("identf", [128, 128], F32)
    identb = s.sb("identb", [128, 128], BF16)
    onesf = s.sb("onesf", [128, 128], F32)
    s.op("pool", lambda e: e.memset(identf[:], 0.0), writes=[identf])
    s.op("pool", lambda e: e.affine_select(out=identf[:], in_=identf[:], pattern=[[-1, 128]],
                                           compare_op=ALU.not_equal, fill=1.0, base=0, channel_multiplier=1),
         reads=[identf], writes=[identf])
    s.op("pool", lambda e: e.tensor_copy(out=identb[:], in_=identf[:]), reads=[identf], writes=[identb])
    s.op("pool", lambda e: e.memset(onesf[:], 1.0), writes=[onesf])
    return identf, identb, onesf


def emit_mod(s, cvec, stream, w_ada, b_ada, c0, ncols, mod, identf):
    s.push()
    cb = s.sb("cb", [128, D], F32)
    sc = s.sb("sc", [128, D], F32)
    scT = s.sb("scT", [128, 8, 128], F32)
    bb = s.sb("bb", [128, ncols], F32)
    wa = [s.sb("wa%d" % i, [128, 8, 512], F32) for i in range(2)]
    ptr = s.ps("ptr", [128, 4, 128], F32)
    pm = s.ps("pm", [128, 512], F32)
    s.dma("sp", cb[:], cvec[stream].partition_broadcast(128), writes=[cb])
    s.dma("sp", bb[:], b_ada[c0:c0 + ncols].partition_broadcast(128), writes=[bb])
    s.op("act", lambda e: e.activation(out=sc[:], in_=cb[:], func=AF.Silu), reads=[cb], writes=[sc])
    for kk in range(0, 8, 4):
        for j in range(4):
            s.op("pe", lambda e: e.transpose(ptr[:, j, :], sc[:, (kk + j) * 128:(kk + j + 1) * 128], identf[:]),
                 reads=[sc, identf], writes=[ptr])
        s.op("dve", lambda e: e.tensor_copy(out=scT[:, kk:kk + 4, :], in_=ptr[:]), reads=[], writes=[ptr, (scT, kk)])
    for cbk in range(ncols // 512):
        w = wa[cbk % 2]
        s.dma("sp", w[:], w_ada[:, c0 + cbk * 512:c0 + (cbk + 1) * 512].rearrange("(k p) n -> p k n", p=128), writes=[w])
        for k in range(8):
            s.op("pe", lambda e: e.matmul(pm[:], lhsT=scT[:, k, :], rhs=w[:, k, :], start=(k == 0), stop=(k == 7)),
                 reads=[(scT, 0), (scT, 4), w], writes=[pm])
        s.op("dve", lambda e: e.tensor_tensor(out=mod[:, cbk * 512:(cbk + 1) * 512], in0=pm[:],
                                              in1=bb[:, cbk * 512:(cbk + 1) * 512], op=ALU.add),
             reads=[bb], writes=[pm, (mod, cbk)])
    s.pop()


def emit_rstd(s, ss, rstd, n, reads, writes):
    s.op("act", lambda e: e.activation(out=rstd, in_=ss, func=AF.Ln, scale=1.0 / n, bias=EPS), reads=reads, writes=writes)
    s.op("act", lambda e: e.activation(out=rstd, in_=rstd, func=AF.Exp, scale=-0.5), reads=writes, writes=writes)


def emit_rope(s, src, src_res, C, S, tabres, w, out, out_res, t, u, src_is_psum):
    G = 512 // w
    d = w // 4
    srcg = src.rearrange("p (g c) -> p g c", g=G)
    src5 = src.rearrange("p (g t a d) -> p g t a d", g=G, t=2, a=2, d=d)
    u5 = u[:].rearrange("p (g t a d) -> p g t a d", g=G, t=2, a=2, d=d)
    S4 = S.rearrange("p (t a d) -> p t a d", t=2, a=2, d=d)
    rd = [] if src_is_psum else [src_res]
    wr = [src_res] if src_is_psum else []
    s.op("dve", lambda e: e.tensor_tensor(out=t[:].rearrange("p (g c) -> p g c", g=G), in0=srcg,
                                          in1=C.unsqueeze(1).broadcast_to([128, G, w]), op=ALU.mult),
         reads=rd + [tabres], writes=wr + [t])
    for a in range(2):
        s.op("dve", lambda e: e.tensor_tensor(out=u5[:, :, :, a, :], in0=src5[:, :, :, 1 - a, :],
                                              in1=S4[:, :, a, :].unsqueeze(1).broadcast_to([128, G, 2, d]), op=ALU.mult),
             reads=rd + [tabres], writes=wr + [(u, a)])
    s.op("pool", lambda e: e.tensor_tensor(out=out, in0=t[:], in1=u[:], op=ALU.add),
         reads=[t, (u, 0), (u, 1)], writes=[out_res])


def emit_pre(s, consts, xs, cvec, w_ada, b_ada, g_attn, w_in, g_qk, rope, qT, kT, vv, ckT, cvv):
    identf, identb, onesf = consts
    s.push()
    mods = []
    for stream in range(2):
        m = s.sb("mod%d" % stream, [128, 2 * D], F32)
        emit_mod(s, cvec, stream, w_ada, b_ada, 0, 2 * D, m, identf)
        mods.append(m)
    gat = s.sb("gat", [128, D], F32)
    s.dma("sp", gat[:], g_attn.partition_broadcast(128), writes=[gat])
    for m in mods:
        s.op("dve", lambda e: e.scalar_tensor_tensor(out=m[:, D:2 * D], in0=m[:, D:2 * D], scalar=1.0, in1=gat[:],
                                                     op0=ALU.add, op1=ALU.mult),
             reads=[gat, (m, 2), (m, 3)], writes=[(m, 2), (m, 3)])
    win = s.sb("win", [128, 8, 2048], BF16)
    c = 0
    for (a, b) in WIN_ORDER:
        s.dma("pool", win[:, :, c:c + (b - a)], w_in[:, a:b].rearrange("(k p) n -> p k n", p=128), writes=[(win, c)], sem="D_win")
        c += b - a
    win_res = [(win, cc) for cc in np.cumsum([0] + [b - a for (a, b) in WIN_ORDER[:-1]]).tolist()]
    gqk = s.sb("gqk", [128, 512], F32)
    for h in range(8):
        s.dma("sp", gqk[:, h * 64:(h + 1) * 64], g_qk[0 if h < 6 else 1].partition_broadcast(128), writes=[(gqk, h)], sem="D_gqk")
    gqk_res = [(gqk, h) for h in range(8)]
    ktacc = s.sb("ktacc", [128, 4, NLAT], BF16)
    vacc = s.sb("vacc", [128, 8, 32, 64], BF16)
    ktc = s.sb("ktc", [128, 4, NCTX], BF16)
    vc = s.sb("vc", [128, 8, 2, 64], BF16)
    xt = [s.sb("xt%d" % i, [128, D], F32) for i in range(2)]
    tb = [s.sb("tb%d" % i, [128, 192], F32) for i in range(2)]
    junk = s.sb("junk", [128, D], BF16)
    ss = [s.sb("ss%d" % i, [128, 1], F32) for i in range(2)]
    rstd = [s.sb("rstd%d" % i, [128, 1], F32) for i in range(2)]
    hb = [s.sb("hb%d" % i, [128, D], BF16) for i in range(2)]
    hT = [s.sb("hT%d" % i, [128, 8, 128], BF16) for i in range(2)]
    sqa = s.sb("sqa", [128, 512], F32)
    ssa = s.sb("ssa", [128, 8], F32)
    ra = s.sb("ra", [128, 8], F32)
    qa = s.sb("qa", [128, 512], F32)
    tt = s.sb("tt", [128, 512], F32)
    uu = s.sb("uu", [128, 512], F32)
    qkb = [s.sb("qkb%d" % i, [128, 1536], BF16) for i in range(2)]
    qst = [s.sb("qst%d" % i, [128, 8, 512], BF16) for i in range(2)]
    pT = s.ps("pT", [128, 8, 128], BF16)
    pq = s.ps("pq", [128, 4, 512], F32)
    ptq = [s.ps("ptq%d" % i, [128, 8, 128], BF16) for i in range(2)]

    for i in range(NTILE):
        p = i % 2
        isctx = i >= 32
        m = mods[1] if isctx else mods[0]
        s.dma("sp", xt[p][:], xs[i * 128:(i + 1) * 128, :], writes=[xt[p]])
        s.dma("sp", tb[p][:], rope[i * 128:(i + 1) * 128, :], writes=[tb[p]])
        s.op("act", lambda e: e.activation(out=junk[:], in_=xt[p][:], func=AF.Square, accum_out=ss[p][:]),
             reads=[xt[p]], writes=[junk, ss[p]])
        emit_rstd(s, ss[p][:], rstd[p][:], D, [ss[p]], [rstd[p]])
        s.op("dve", lambda e: e.scalar_tensor_tensor(out=xt[p][:], in0=xt[p][:], scalar=rstd[p][:, 0:1], in1=m[:, D:2 * D],
                                                     op0=ALU.mult, op1=ALU.mult),
             reads=[rstd[p], (m, 2), (m, 3)], writes=[xt[p]])
        s.op("pool", lambda e: e.tensor_tensor(out=hb[p][:], in0=xt[p][:], in1=m[:, 0:D], op=ALU.add),
             reads=[xt[p], (m, 0), (m, 1)], writes=[hb[p]])
        for k in range(8):
            s.op("pe", lambda e: e.transpose(pT[:, k, :], hb[p][:, k * 128:(k + 1) * 128], identb[:]),
                 reads=[hb[p], identb], writes=[pT])
        s.op("act", lambda e: e.copy(out=hT[p][:], in_=pT[:]), reads=[], writes=[pT, hT[p]])
        for cb in range(4):
            for k in range(8):
                s.op("pe", lambda e: e.matmul(pq[:, cb, :], lhsT=hT[p][:, k, :], rhs=win[:, k, cb * 512:(cb + 1) * 512],
                                              start=(k == 0), stop=(k == 7)),
                     reads=[hT[p]] + win_res, writes=[(pq, cb)])
        vdst = vc[:, :, i - 32, :] if isctx else vacc[:, :, i, :]
        s.op("act", lambda e: e.copy(out=vdst, in_=pq[:, 3, :].rearrange("p (u e) -> p u e", u=8)),
             reads=[], writes=[(pq, 3), ("vacc", i)])
        s.op("act", lambda e: e.activation(out=sqa[:], in_=pq[:, 0, :], func=AF.Square), reads=[], writes=[(pq, 0), sqa])
        s.op("dve", lambda e: e.reduce_sum(out=ssa[:], in_=sqa[:].rearrange("p (h c) -> p h c", h=8), axis=AX.X),
             reads=[sqa], writes=[ssa])
        emit_rstd(s, ssa[:], ra[:], 64, [ssa], [ra])
        s.op("dve", lambda e: e.tensor_tensor(out=qa[:].rearrange("p (h c) -> p h c", h=8),
                                              in0=pq[:, 0, :].rearrange("p (h c) -> p h c", h=8),
                                              in1=ra[:].unsqueeze(2).broadcast_to([128, 8, 64]), op=ALU.mult),
             reads=[ra], writes=[(pq, 0), qa])
        s.op("pool", lambda e: e.tensor_tensor(out=qa[:], in0=qa[:], in1=gqk[:], op=ALU.mult),
             reads=gqk_res, writes=[qa])
        emit_rope(s, qa[:], qa, tb[p][:, 0:64], tb[p][:, 64:128], tb[p], 64, qkb[p][:, 0:512], (qkb[p], 0), tt, uu, False)
        emit_rope(s, pq[:, 1, :], (pq, 1), tb[p][:, 128:160], tb[p][:, 160:192], tb[p], 32, qkb[p][:, 512:1024], (qkb[p], 1), tt, uu, True)
        emit_rope(s, pq[:, 2, :], (pq, 2), tb[p][:, 0:64], tb[p][:, 64:128], tb[p], 64, qkb[p][:, 1024:1536], (qkb[p], 2), tt, uu, True)
        gi = i // 4
        gp = gi % 2
        tl = i % 4
        for c12 in range(12):
            pt_, j = ptq[c12 // 8], c12 % 8
            s.op("pe", lambda e: e.transpose(pt_[:, j, :], qkb[p][:, c12 * 128:(c12 + 1) * 128], identb[:]),
                 reads=[(qkb[p], c12 // 4), identb], writes=[pt_])
        kdst = (lambda j0, n: ktc[:, j0:j0 + n, (i - 32) * 128:(i - 31) * 128]) if isctx else \
               (lambda j0, n: ktacc[:, j0:j0 + n, i * 128:(i + 1) * 128])
        qd = lambda j0, n: qst[gp][:, j0:j0 + n, tl * 128:(tl + 1) * 128]
        s.op("act", lambda e: e.copy(out=qd(0, 3), in_=ptq[0][:, 0:3, :]), reads=[], writes=[ptq[0], (qst[gp], tl, 0)])
        s.op("dve", lambda e: e.tensor_copy(out=kdst(0, 1), in_=ptq[0][:, 3:4, :]), reads=[], writes=[ptq[0], ("ktacc", i, 0)])
        s.op("act", lambda e: e.copy(out=qd(3, 2), in_=ptq[0][:, 4:6, :]), reads=[], writes=[ptq[0], (qst[gp], tl, 1)])
        s.op("dve", lambda e: e.tensor_copy(out=kdst(1, 2), in_=ptq[0][:, 6:8, :]), reads=[], writes=[ptq[0], ("ktacc", i, 1)])
        s.op("act", lambda e: e.copy(out=qd(5, 3), in_=ptq[1][:, 0:3, :]), reads=[], writes=[ptq[1], (qst[gp], tl, 2)])
        s.op("dve", lambda e: e.tensor_copy(out=kdst(3, 1), in_=ptq[1][:, 3:4, :]), reads=[], writes=[ptq[1], ("ktacc", i, 2)])
        ntl = 4 if gi < 8 else 2
        if tl == ntl - 1:
            t0 = gi * 512
            s.dma("sp", qT.rearrange("(c r) t -> r c t", r=128)[:, :, t0:t0 + ntl * 128], qst[gp][:, :, 0:ntl * 128],
                  reads=[(qst[gp], a, b) for a in range(ntl) for b in range(3)],
                  writes=[("qT", gi)] + [(qst[gp], a, b) for a in range(ntl) for b in range(3)], sem="D_qT")
    allk = [("ktacc", i, j) for i in range(NTILE) for j in range(3)]
    allv = [("vacc", i) for i in range(NTILE)]
    s.dma("sp", kT.rearrange("(c r) t -> r c t", r=128), ktacc[:], reads=allk, writes=["kT"], sem="D_kvout")
    s.dma("sp", ckT.rearrange("(c r) t -> r c t", r=128), ktc[:], reads=allk, writes=["ckT"], sem="D_kvout")
    for j in range(4):
        s.dma("sp", vv[j], vacc[:, 2 * j:2 * j + 2, :, :].rearrange("p u k e -> p (u k e)"), reads=allv, writes=["vv"], sem="D_kvout")
    s.dma("sp", cvv, vc[:].rearrange("p u k e -> p (u k e)"), reads=allv, writes=["cvv"], sem="D_kvout")
    s.pop()


def rope_tables(core):
    t = (core % 4) * NLAT + np.arange(NLAT)
    rows = (t // 64).astype(np.float32)
    cols = (t % 64).astype(np.float32)

    def tab(half):
        fr = (10000.0 ** (-np.arange(half, dtype=np.float32) / half)).astype(np.float32)
        ar = rows[:, None] * fr[None, :]
        ac = cols[:, None] * fr[None, :]
        cr, sr, cc, sc = np.cos(ar), np.sin(ar), np.cos(ac), np.sin(ac)
        C = np.concatenate([cr, cr, cc, cc], axis=1)
        S = np.concatenate([-sr, sr, -sc, sc], axis=1)
        return C.astype(np.float32), S.astype(np.float32)

    C64, S64 = tab(16)
    C32, S32 = tab(8)
    lat = np.concatenate([C64, S64, C32, S32], axis=1)
    ctx = np.zeros((NCTX, 192), np.float32)
    ctx[:, 0:64] = 1.0
    ctx[:, 128:160] = 1.0
    return np.ascontiguousarray(np.concatenate([lat, ctx], axis=0))


def cmask_table(core):
    j = np.arange(128)[:, None]
    i = np.arange(128)[None, :]
    mp = (j >= i).astype(np.float32)
    mn = (j <= i).astype(np.float32)
    one = np.ones((128, 128), np.float32)
    pats = np.zeros((8, 128, 512), np.float32)
    for t in range(6):
        for b in range(4):
            blk = mp if t == b else one if t == b + 1 else mn if t == b + 2 else None
            if blk is not None:
                pats[t, :, b * 128:(b + 1) * 128] = blk
    q = core % 4
    pats[6] = pats[0] if q > 0 else 0.0
    pats[7] = pats[5] if q < 3 else 0.0
    return np.ascontiguousarray(pats.transpose(1, 0, 2).reshape(128, 8 * 512)).astype(ml_dtypes.bfloat16)


def emit_attn(s, consts, layer, do_ctx, qT, kg, vgf, kT_own, v_own, ckT, cvv, sel, cmask, lamv, sinkv, gsub, oT):
    identf, identb, onesf = consts
    lam_init = 0.8 - 0.6 * math.exp(-0.3 * layer)
    s.push()
    lq = s.sb("lq", [128, 4, 32], F32)
    for i in range(4):
        s.dma("sp", lq[:, i, :], lamv[i].partition_broadcast(128), writes=[(lq, i)], sem="D_small")
    lp = s.sb("lp", [128, 2, 32], F32)
    ld = s.sb("ld", [128, 2], F32)
    neglam = s.sb("neglam", [128, 1], F32)
    s.op("dve", lambda e: e.tensor_tensor(out=lp[:], in0=lq[:, 0:4:2, :], in1=lq[:, 1:4:2, :], op=ALU.mult),
         reads=[(lq, i) for i in range(4)], writes=[lp])
    s.op("dve", lambda e: e.reduce_sum(out=ld[:], in_=lp[:], axis=AX.X), reads=[lp], writes=[ld])
    s.op("act", lambda e: e.activation(out=ld[:], in_=ld[:], func=AF.Exp), reads=[ld], writes=[ld])
    s.op("dve", lambda e: e.tensor_tensor(out=neglam[:], in0=ld[:, 1:2], in1=ld[:, 0:1], op=ALU.subtract), reads=[ld], writes=[neglam])
    s.op("dve", lambda e: e.tensor_scalar_add(out=neglam[:], in0=neglam[:], scalar1=-lam_init), reads=[neglam], writes=[neglam])
    sinkexp = s.sb("sinkexp", [128, 6], F32)
    s.dma("sp", sinkexp[:], sinkv.partition_broadcast(128), writes=[sinkexp], sem="D_small")
    s.op("act", lambda e: e.activation(out=sinkexp[:], in_=sinkexp[:], func=AF.Exp), reads=[sinkexp], writes=[sinkexp])
    gsubs = s.sb("gsubs", [64, 1], F32)
    s.dma("sp", gsubs[:], gsub.rearrange("(p o) -> p o", o=1), writes=[gsubs], sem="D_small")
    s.op("dve", lambda e: e.tensor_scalar_mul(out=gsubs[:], in0=gsubs[:], scalar1=1.0 - lam_init), reads=[gsubs], writes=[gsubs])
    cm = s.sb("cm", [128, 8, 512], BF16)
    s.dma("sp", cm[:], cmask.rearrange("p (m q) -> p m q", m=8), writes=[cm], sem="D_small")

    NKT = 130
    KT = [s.sb("KT%d" % i, [128, NKT * 128], BF16) for i in range(2)]
    VA = [s.sb("VA%d" % i, [128, NKT, 128], BF16) for i in range(2)]
    QB = [s.sb("QB%d" % i, [128, 3, 512], BF16) for i in range(2)]
    QBB = [s.sb("QBB%d" % i, [128, 2, 512], BF16) for i in range(2)]
    for i in range(2):
        s.op("pool", lambda e: e.memset(VA[i][:, :, 64:128], 1.0), writes=[("VAones", i)])
        s.op("pool", lambda e: e.memset(KT[i][64:128, :], 0.0), writes=[("KTz", i)])
        s.op("pool", lambda e: e.memset(QB[i][64:128, :, :], 0.0), writes=[("QBz", i)])
        s.op("pool", lambda e: e.memset(QBB[i][:], 0.0), writes=[("QB", "B", i)])
    PT = [s.sb("PT%d" % i, [128, 2, 512], BF16) for i in range(3)]
    rs = [s.sb("rs%d" % i, [128, 512], F32) for i in range(2)]
    bcs = [s.sb("bcs%d" % i, [64, 512], F32) for i in range(2)]
    t1 = s.sb("t1", [64, 512], F32)
    t2 = s.sb("t2", [64, 512], F32)
    od = s.sb("od", [64, 512], F32)
    sq = s.sb("sq", [64, 512], F32)
    rst = s.sb("rst", [64, 512], F32)
    OTs = [s.sb("OTs%d" % i, [64, 512], BF16) for i in range(2)]
    Sb = [s.ps("Sb%d" % i, [128, 2, 512], F32) for i in range(3)]
    Ob = [s.ps("Ob%d" % i, [128, 512], F32) for i in range(2)]

    def load_kv(buf, krow0, unit):
        for r in range(4):
            s.dma("sp", KT[buf][0:64, r * NLAT:(r + 1) * NLAT], kg(r, krow0), reads=[("kTg", krow0 // 128)], writes=[("KT", buf)],
                  sem="D_kv%d" % buf)
            s.dma("sp", VA[buf][:, r * 32:(r + 1) * 32, 0:64],
                  vgf(r, unit).rearrange("p (k e) -> p k e", e=64), reads=[("vg", unit // 2)], writes=[("VA", buf)], sem="D_kv%d" % buf)
        s.dma("sp", KT[buf][0:64, 4 * NLAT:4 * NLAT + NCTX], ckT[krow0:krow0 + 64, :], writes=[("KT", buf)], sem="D_kv%d" % buf)
        s.dma("sp", VA[buf][:, 128:130, 0:64], cvv[:, unit * 128:(unit + 1) * 128].rearrange("p (k e) -> p k e", e=64),
              writes=[("VA", buf)], sem="D_kv%d" % buf)

    candk = s.sb("candk", [64, 4, 128], BF16)
    candv = s.sb("candv", [128, 4, 64], BF16)
    acck = s.sb("acck", [64, 128], F32)
    accv = s.sb("accv", [128, 64], F32)
    selt = s.sb("selt", [128, 8], F32)
    s.dma("sp", selt[:], sel, writes=[selt], sem="D_small")

    def load_kv_c(buf, g):
        KC, VC = KT[buf], VA[buf]
        sem = "D_kv%d" % buf
        s.dma("sp", KC[0:64, 128:128 + NLAT], kT_own[384 + g * 64:384 + (g + 1) * 64, :], writes=[("KT", buf)], sem=sem)
        s.dma("sp", KC[0:64, 34 * 128:36 * 128], ckT[384 + g * 64:384 + (g + 1) * 64, :], writes=[("KT", buf)], sem=sem)
        s.dma("sp", VC[:, 1:33, 0:64], v_own(6 + g).rearrange("p (k e) -> p k e", e=64), writes=[("VA", buf)], sem=sem)
        s.dma("sp", VC[:, 34:36, 0:64], cvv[:, (6 + g) * 128:(7 + g) * 128].rearrange("p (k e) -> p k e", e=64), writes=[("VA", buf)], sem=sem)
        for side in range(2):
            for r in range(4):
                kcols = (NLAT - 128, NLAT) if side == 0 else (0, 128)
                s.dma("sp", candk[:, r, :], kg(r, 384 + g * 64)[:, kcols[0]:kcols[1]], reads=[("kTg", 3)], writes=[candk], sem="D_cand")
                vt = 31 if side == 0 else 0
                s.dma("sp", candv[:, r, :], vgf(r, 6 + g)[:, vt * 64:(vt + 1) * 64], reads=[("vg", 3)], writes=[candv], sem="D_cand")
            for (cand, acc, np_) in ((candk, acck, 64), (candv, accv, 128)):
                s.op("dve", lambda e: e.tensor_scalar(out=acc[:], in0=cand[:, 0, :], scalar1=selt[0:np_, 4 * side:4 * side + 1], scalar2=None,
                                                      op0=ALU.mult), reads=[cand, selt], writes=[acc])
                for r in range(1, 4):
                    s.op("dve", lambda e: e.scalar_tensor_tensor(out=acc[:], in0=cand[:, r, :], scalar=selt[0:np_, 4 * side + r:4 * side + r + 1],
                                                                 in1=acc[:], op0=ALU.mult, op1=ALU.add), reads=[cand, selt], writes=[acc])
            kc0 = 0 if side == 0 else 33 * 128
            s.op("dve", lambda e: e.tensor_copy(out=KC[0:64, kc0:kc0 + 128], in_=acck[:]), reads=[acck], writes=[("KT", buf)])
            s.op("dve", lambda e: e.tensor_copy(out=VC[:, 0 if side == 0 else 33, 0:64], in_=accv[:]), reads=[accv], writes=[("VA", buf)])

    def load_group(gi):
        kind, g = groups[gi]
        if kind == "C":
            load_kv_c(gi % 2, g)
        else:
            kk, uu = kv_spec(kind, g)
            load_kv(gi % 2, kk, uu)

    def bcast_row(dst_ps, dres, row_ap, nq, rres):
        s.op("pe", lambda e: e.matmul(dst_ps[0:64, 0:nq], lhsT=onesf[64:65, 0:64], rhs=row_ap, start=True, stop=True),
             reads=[rres, onesf], writes=[dres])

    groups = [("A", g) for g in range(2)] + [("B", h) for h in range(4)] + [("C", g) for g in range(2)]
    qblocks = [(j * 512, 512, list(range(128)) + [128, 129]) for j in range(8)]
    if do_ctx:
        qblocks.append((NLAT, NCTX, [128, 129]))

    def kv_spec(kind, g):
        if kind == "A":
            return g * 64, g
        return 128 + g * 64, 2 + g

    def q_rows(kind, g):
        return (g * 192, 3) if kind == "A" else (640 + g * 192, 3) if kind == "C" else (384 + g * 64, 1)

    cblocks = [(j * 512, 512, [4 * j + t for t in range(6)] + [34, 35]) for j in range(8)]
    if do_ctx:
        cblocks.append((NLAT, NCTX, [34, 35]))
    jobs = []
    for gi, (kind, g) in enumerate(groups):
        for bi, (q0, nq, tiles) in enumerate(cblocks if kind == "C" else qblocks):
            jb = dict(gi=gi, kind=kind, g=g, q0=q0, nq=nq, tiles=tiles, masks=None)
            if kind == "C" and nq == 512:
                pats = list(range(6))
                if bi == 0:
                    pats[0] = 6
                if bi == 7:
                    pats[5] = 7
                jb["masks"] = pats
            jobs.append(jb)

    def load_q(jn):
        jb = jobs[jn]
        r0, nh = q_rows(jb["kind"], jb["g"])
        q0, nq = jb["q0"], jb["nq"]
        if jb["kind"] in ("A", "C"):
            s.dma("sp", QB[jn % 2][0:64, 0:nh, 0:nq], qT[r0:r0 + nh * 64, q0:q0 + nq].rearrange("(h r) t -> r h t", r=64),
                  writes=[("QB", jn % 2)], sem="D_q%d" % (jn % 2))
        else:
            for k in range(2):
                s.dma("sp", QBB[jn % 2][32 * k:32 * k + 32, k, 0:nq], qT[r0 + 32 * k:r0 + 32 * k + 32, q0:q0 + nq],
                      writes=[("QB", "B", jn % 2)], sem="D_qb%d" % (jn % 2))

    steps = []
    ocnt = 0
    for jn, jb in enumerate(jobs):
        kind, nq, tiles = jb["kind"], jb["nq"], jb["tiles"]
        if kind in ("A", "C"):
            for r in range(3):
                ob = ocnt % 2
                ocnt += 1
                npair = len(tiles) // 2
                for pi in range(npair):
                    mk = None
                    if jb["masks"] is not None and pi < 3:
                        mk = (jb["masks"][2 * pi], jb["masks"][2 * pi + 1])
                    steps.append(dict(jn=jn, kind=kind, r=r, ob=ob, t=(tiles[2 * pi], tiles[2 * pi + 1]),
                                      first=(pi == 0), last=(pi == npair - 1), mk=mk))
                steps.append(dict(jn=jn, fin=True, src=steps[-1]))
        else:
            for ti, t in enumerate(tiles):
                steps.append(dict(jn=jn, kind="B", t=(t,), first=(ti == 0), last=(ti == len(tiles) - 1)))
            steps.append(dict(jn=jn, fin=True, src=steps[-1]))

    def emit_S(i):
        st = steps[i]
        if st.get("fin"):
            return
        jb = jobs[st["jn"]]
        buf, qb, nq, sb = jb["gi"] % 2, st["jn"] % 2, jb["nq"], Sb[i % 3]
        if st["kind"] in ("A", "C"):
            rd = [("KT", buf), ("KTz", buf), ("QB", qb), ("QBz", qb)]
            for k, t in enumerate(st["t"]):
                s.op("pe", lambda e: e.matmul(sb[:, k, 0:nq], lhsT=KT[buf][:, t * 128:(t + 1) * 128],
                                              rhs=QB[qb][:, st["r"], 0:nq], start=True, stop=True), reads=rd, writes=[sb])
        else:
            rd = [("KT", buf), ("KTz", buf), ("QB", "B", qb)]
            t = st["t"][0]
            for k in range(2):
                s.op("pe", lambda e: e.matmul(sb[:, k, 0:nq], lhsT=KT[buf][:, t * 128:(t + 1) * 128],
                                              rhs=QBB[qb][:, k, 0:nq], start=True, stop=True), reads=rd, writes=[sb])

    def emit_rest(i):
        st = steps[i]
        fin = st.get("fin", False)
        if fin:
            st = st["src"]
        jb = jobs[st["jn"]]
        buf, nq, sb, pt = jb["gi"] % 2, jb["nq"], Sb[i % 3], PT[i % 3]
        kind = st["kind"]
        if fin:
            emit_fin(st, jb, kind, nq, sb)
            return
        scale = 32 ** -0.5 if kind == "B" else 0.125
        s.op("act", lambda e: e.activation(out=pt[:, :, 0:nq], in_=sb[:, :, 0:nq], func=AF.Exp, scale=scale),
             reads=[], writes=[sb, pt])
        if st.get("mk") is not None:
            m0, m1 = st["mk"]
            if m1 == m0 + 1:
                s.op("dve", lambda e: e.tensor_tensor(out=pt[:], in0=pt[:], in1=cm[:, m0:m0 + 2, :], op=ALU.mult), reads=[cm], writes=[pt])
            else:
                for k, mm in enumerate((m0, m1)):
                    s.op("dve", lambda e: e.tensor_tensor(out=pt[:, k, :], in0=pt[:, k, :], in1=cm[:, mm, :], op=ALU.mult), reads=[cm], writes=[pt])
        rdv = [("VA", buf), ("VAones", buf), pt]
        if kind in ("A", "C"):
            o = Ob[st["ob"]]
            for k, t in enumerate(st["t"]):
                s.op("pe", lambda e: e.matmul(o[:, 0:nq], lhsT=VA[buf][:, t, :], rhs=pt[:, k, 0:nq],
                                              start=(st["first"] and k == 0), stop=(st["last"] and k == 1)), reads=rdv, writes=[o])
        else:
            t = st["t"][0]
            for k in range(2):
                s.op("pe", lambda e: e.matmul(Ob[k][:, 0:nq], lhsT=VA[buf][:, t, :], rhs=pt[:, k, 0:nq],
                                              start=st["first"], stop=st["last"]), reads=rdv, writes=[Ob[k]])

    def emit_fin(st, jb, kind, nq, sb):
        Mb = [sb[:, 0, :], sb[:, 1, :]]
        q0 = jb["q0"]
        if kind in ("A", "C"):
            o, ob = Ob[st["ob"]], st["ob"]
            if kind == "C":
                hh = 3 * jb["g"] + st["r"]
                s.op("dve", lambda e: e.tensor_scalar(out=rs[ob][64:65, 0:nq], in0=o[64:65, 0:nq], scalar1=sinkexp[64:65, hh:hh + 1],
                                                      scalar2=None, op0=ALU.add), reads=[sinkexp], writes=[o, rs[ob]])
                s.op("dve", lambda e: e.reciprocal(out=rs[ob][64:65, 0:nq], in_=rs[ob][64:65, 0:nq]), reads=[], writes=[rs[ob]])
            else:
                s.op("dve", lambda e: e.reciprocal(out=rs[ob][64:65, 0:nq], in_=o[64:65, 0:nq]), reads=[], writes=[o, rs[ob]])
            bcast_row(Mb[0], sb, rs[ob][64:65, 0:nq], nq, rs[ob])
            s.op("act", lambda e: e.copy(out=bcs[ob][:, 0:nq], in_=Mb[0][0:64, 0:nq]), reads=[], writes=[sb, bcs[ob]])
            s.op("dve", lambda e: e.tensor_tensor(out=OTs[ob][:, 0:nq], in0=o[0:64, 0:nq], in1=bcs[ob][:, 0:nq], op=ALU.mult),
                 reads=[bcs[ob]], writes=[o, OTs[ob]])
            row0 = ((0 if kind == "A" else 10) + jb["g"] * 3 + st["r"]) * 64
            s.dma("sp", oT[row0:row0 + 64, q0:q0 + nq], OTs[ob][:, 0:nq], reads=[OTs[ob]], writes=[("oT", row0, q0)], sem="D_oT")
        else:
            for k in range(2):
                s.op("dve", lambda e: e.reciprocal(out=rs[k][64:65, 0:nq], in_=Ob[k][64:65, 0:nq]), reads=[], writes=[Ob[k], rs[k]])
            s.op("dve", lambda e: e.tensor_scalar(out=rs[1][64:65, 0:nq], in0=rs[1][64:65, 0:nq], scalar1=neglam[64:65, 0:1],
                                                  scalar2=None, op0=ALU.mult), reads=[neglam], writes=[rs[1]])
            for k in range(2):
                bcast_row(Mb[k], sb, rs[k][64:65, 0:nq], nq, rs[k])
                s.op("act", lambda e: e.copy(out=bcs[k][:, 0:nq], in_=Mb[k][0:64, 0:nq]), reads=[], writes=[sb, bcs[k]])
            s.op("dve", lambda e: e.tensor_tensor(out=t1[:, 0:nq], in0=Ob[0][0:64, 0:nq], in1=bcs[0][:, 0:nq], op=ALU.mult),
                 reads=[bcs[0]], writes=[Ob[0], t1])
            s.op("dve", lambda e: e.tensor_tensor(out=t2[:, 0:nq], in0=Ob[1][0:64, 0:nq], in1=bcs[1][:, 0:nq], op=ALU.mult),
                 reads=[bcs[1]], writes=[Ob[1], t2])
            s.op("pool", lambda e: e.tensor_tensor(out=od[:, 0:nq], in0=t1[:, 0:nq], in1=t2[:, 0:nq], op=ALU.add), reads=[t1, t2], writes=[od])
            s.op("pool", lambda e: e.tensor_tensor(out=sq[:, 0:nq], in0=od[:, 0:nq], in1=od[:, 0:nq], op=ALU.mult), reads=[od], writes=[sq])
            s.op("pe", lambda e: e.matmul(Mb[0][0:64, 0:nq], lhsT=onesf[0:64, 0:64], rhs=sq[:, 0:nq], start=True, stop=True),
                 reads=[sq, onesf], writes=[sb])
            s.op("act", lambda e: e.activation(out=rst[:, 0:nq], in_=Mb[0][0:64, 0:nq], func=AF.Ln, scale=1.0 / 64, bias=EPS),
                 reads=[], writes=[sb, rst])
            s.op("act", lambda e: e.activation(out=rst[:, 0:nq], in_=rst[:, 0:nq], func=AF.Exp, scale=-0.5), reads=[rst], writes=[rst])
            s.op("dve", lambda e: e.scalar_tensor_tensor(out=OTs[0][:, 0:nq], in0=od[:, 0:nq], scalar=gsubs[:, 0:1], in1=rst[:, 0:nq],
                                                         op0=ALU.mult, op1=ALU.mult), reads=[od, gsubs, rst], writes=[OTs[0]])
            row0 = (6 + jb["g"]) * 64
            s.dma("sp", oT[row0:row0 + 64, q0:q0 + nq], OTs[0][:, 0:nq], reads=[OTs[0]], writes=[("oT", row0, q0)], sem="D_oT")

    load_group(0)
    load_q(0)
    emit_S(0)
    emit_S(1)
    cur_job = -1
    for i, st in enumerate(steps):
        if st["jn"] != cur_job:
            cur_job = st["jn"]
            jb = jobs[cur_job]
            if cur_job + 1 < len(jobs):
                load_q(cur_job + 1)
            if (cur_job == 0 or jobs[cur_job - 1]["gi"] != jb["gi"]) and jb["gi"] + 1 < len(groups):
                load_group(jb["gi"] + 1)
        if i + 2 < len(steps):
            emit_S(i + 2)
        emit_rest(i)

    s.pop()


def emit_ffn_mod(s, consts, do_ctx, cvec, w_ada, b_ada, g_ffn, modd):
    identf = consts[0]
    s.push()
    gff = s.sb("gff", [128, D], F32)
    s.dma("sp", gff[:], g_ffn.partition_broadcast(128), writes=[gff])
    mod = s.sb("modm", [128, 4 * D], F32)
    modres = [(mod, i) for i in range(8)]
    for stream in range(2 if do_ctx else 1):
        emit_mod(s, cvec, stream, w_ada, b_ada, 2 * D, 4 * D, mod, identf)
        s.op("dve", lambda e: e.scalar_tensor_tensor(out=mod[:, 2 * D:3 * D], in0=mod[:, 2 * D:3 * D], scalar=1.0, in1=gff[:],
                                                     op0=ALU.add, op1=ALU.mult), reads=[gff] + modres, writes=modres)
        s.dma("sp", modd[stream], mod[:], reads=modres, writes=[("modd", stream)] + modres, sem="D_modd")
    s.pop()


def emit_ffn(s, consts, last, do_ctx, xs, modd, w_out, w_ff1, w_ff3, w_ff2, g_final, oT, xout):
    identf, identb, onesf = consts
    s.push()
    wout = s.sb("wout", [128, 8, D], BF16)
    w1 = s.sb("w1", [128, 8, DFF], BF16)
    w3 = s.sb("w3", [128, 8, DFF], BF16)
    w2 = s.sb("w2", [128, NFF, D], BF16)
    wres = []
    for k in range(8):
        s.dma("sp", wout[:, k, :], w_out[k * 128:(k + 1) * 128, :], reads=[("wcast", 0)], writes=[(wout, k)], sem="D_w")
        s.dma("sp", w1[:, k, :], w_ff1[k * 128:(k + 1) * 128, :], reads=[("wcast", 1)], writes=[(w1, k)], sem="D_w")
        s.dma("sp", w3[:, k, :], w_ff3[k * 128:(k + 1) * 128, :], reads=[("wcast", 2)], writes=[(w3, k)], sem="D_w")
        wres += [(wout, k), (w1, k), (w3, k)]
    for k in range(0, NFF, 2):
        s.dma("sp", w2[:, k:k + 2, :], w_ff2[k * 128:(k + 2) * 128, :].rearrange("(c p) n -> p c n", p=128), reads=[("wcast", 3)],
              writes=[(w2, k)], sem="D_w")
        wres.append((w2, k))
    gfin = None
    if last:
        gfin = s.sb("gfin", [128, D], F32)
        s.dma("sp", gfin[:], g_final.partition_broadcast(128), writes=[gfin])
    mod = s.sb("modf", [128, 4 * D], F32)
    modres = [(mod, i) for i in range(8)]
    xt = [s.sb("fxt%d" % i, [128, D], F32) for i in range(2)]
    x1 = [s.sb("x1_%d" % i, [128, D], F32) for i in range(2)]
    ss = [s.sb("fss%d" % i, [128, 1], F32) for i in range(2)]
    rstd = [s.sb("frstd%d" % i, [128, 1], F32) for i in range(2)]
    hb = [s.sb("fhb%d" % i, [128, D], BF16) for i in range(2)]
    h2T = s.sb("h2T", [128, 8, 128], BF16)
    OTt = [s.sb("OTt%d" % i, [128, 8, 128], BF16) for i in range(2)]
    sa = s.sb("sa", [128, 512], F32)
    junk = sa[:].bitcast(BF16)
    uT = s.sb("uT", [128, NFF, 128], BF16)
    pya = s.ps("pya", [128, 2, 512], F32)
    py = s.ps("py", [128, 2, 512], F32)
    pa = [s.ps("pa%d" % i, [128, 512], F32) for i in range(2)]
    pb = [s.ps("pb%d" % i, [128, 512], F32) for i in range(2)]
    pT = pa[0][:].bitcast(BF16).rearrange("p (k t) -> p k t", k=8)
    grps = [list(range(c, min(c + 4, NFF))) for c in range(0, NFF, 4)]

    def stage_a(n, i):
        p = n % 2
        s.dma("sp", xt[p][:], xs[i * 128:(i + 1) * 128, :], writes=[xt[p]])
        s.dma("sp", OTt[p][:], oT.rearrange("(c r) t -> r c t", r=128)[:, :, i * 128:(i + 1) * 128], writes=[OTt[p]])
        for h in range(2):
            for c in range(8):
                s.op("pe", lambda e: e.matmul(pya[:, h, :], lhsT=OTt[p][:, c, :], rhs=wout[:, c, h * 512:(h + 1) * 512],
                                              start=(c == 0), stop=(c == 7)), reads=[OTt[p]] + wres, writes=[pya])
        s.op("dve", lambda e: e.tensor_tensor(out=x1[p][:], in0=pya[:].rearrange("p a b -> p (a b)"), in1=mod[:, 0:D], op=ALU.mult),
             reads=modres, writes=[pya, x1[p]])
        s.op("dve", lambda e: e.tensor_tensor(out=x1[p][:], in0=x1[p][:], in1=xt[p][:], op=ALU.add), reads=[xt[p]], writes=[x1[p]])
        s.op("act", lambda e: e.activation(out=junk, in_=x1[p][:], func=AF.Square, accum_out=ss[p][:]), reads=[x1[p]], writes=[sa, ss[p]])
        emit_rstd(s, ss[p][:], rstd[p][:], D, [ss[p]], [rstd[p]])
        s.op("dve", lambda e: e.scalar_tensor_tensor(out=xt[p][:], in0=x1[p][:], scalar=rstd[p][:, 0:1], in1=mod[:, 2 * D:3 * D],
                                                     op0=ALU.mult, op1=ALU.mult), reads=[x1[p], rstd[p]] + modres, writes=[xt[p]])
        s.op("pool", lambda e: e.tensor_tensor(out=hb[p][:], in0=xt[p][:], in1=mod[:, D:2 * D], op=ALU.add),
             reads=[xt[p]] + modres, writes=[hb[p]])

    def stage_b(n, i):
        p = n % 2
        for k in range(8):
            s.op("pe", lambda e: e.transpose(pT[:, k, :], hb[p][:, k * 128:(k + 1) * 128], identb[:]), reads=[hb[p], identb], writes=[pa[0]])
        s.op("act", lambda e: e.copy(out=h2T[:], in_=pT), reads=[], writes=[pa[0], h2T])
        for gi, grp in enumerate(grps):
            gp = gi % 2
            n_ = len(grp) * 128
            for (w_, pp) in ((w1, pa[gp]), (w3, pb[gp])):
                for j, c in enumerate(grp):
                    for k in range(8):
                        s.op("pe", lambda e: e.matmul(pp[:, j * 128:(j + 1) * 128], lhsT=w_[:, k, c * 128:(c + 1) * 128],
                                                      rhs=h2T[:, k, :], start=(k == 0), stop=(k == 7)), reads=[h2T] + wres, writes=[pp])
            s.op("act", lambda e: e.activation(out=sa[:, 0:n_], in_=pa[gp][:, 0:n_], func=AF.Silu), reads=[], writes=[pa[gp], sa])
            s.op("dve", lambda e: e.tensor_tensor(out=uT[:, grp[0]:grp[0] + len(grp), :].rearrange("p c t -> p (c t)"),
                                                  in0=sa[:, 0:n_], in1=pb[gp][:, 0:n_], op=ALU.mult),
                 reads=[sa], writes=[pb[gp], (uT, gi)])
        for h in range(2):
            for c in range(NFF):
                s.op("pe", lambda e: e.matmul(py[:, h, :], lhsT=uT[:, c, :], rhs=w2[:, c, h * 512:(h + 1) * 512],
                                              start=(c == 0), stop=(c == NFF - 1)),
                     reads=[(uT, g_) for g_ in range(len(grps))] + wres, writes=[py])
        s.op("dve", lambda e: e.tensor_tensor(out=xt[p][:], in0=py[:].rearrange("p a b -> p (a b)"), in1=mod[:, 3 * D:4 * D], op=ALU.mult),
             reads=modres, writes=[py, xt[p]])
        s.op("pool", lambda e: e.tensor_tensor(out=xt[p][:], in0=xt[p][:], in1=x1[p][:], op=ALU.add), reads=[x1[p]], writes=[xt[p]])
        if last:
            s.op("act", lambda e: e.activation(out=junk, in_=xt[p][:], func=AF.Square, accum_out=ss[p][:]), reads=[xt[p]], writes=[sa, ss[p]])
            emit_rstd(s, ss[p][:], rstd[p][:], D, [ss[p]], [rstd[p]])
            s.op("dve", lambda e: e.scalar_tensor_tensor(out=xt[p][:], in0=xt[p][:], scalar=rstd[p][:, 0:1], in1=gfin[:],
                                                         op0=ALU.mult, op1=ALU.mult), reads=[rstd[p], gfin], writes=[xt[p]])
        s.dma("sp", xout[i * 128:(i + 1) * 128, :], xt[p][:], reads=[xt[p]], writes=[("xout", i)], sem="D_xout")

    def run_tiles(tile_ids):
        stage_a(0, tile_ids[0])
        for n, i in enumerate(tile_ids):
            if n + 1 < len(tile_ids):
                stage_a(n + 1, tile_ids[n + 1])
            stage_b(n, i)

    for stream in range(2 if do_ctx else 1):
        s.dma("sp", mod[:], modd[stream], reads=[("modd", stream)], writes=modres)
        run_tiles(list(range(32)) if stream == 0 else [32, 33])
    s.pop()


def build_fused():
    nc = bass.Bass("TRN2", target_bir_lowering=False)
    di = lambda n, sh, dt_=F32: nc.dram_tensor(n, sh, dt_, kind="ExternalInput").ap()
    it = lambda n, sh, dt_=BF16: nc.dram_tensor(n, sh, dt_, kind="Internal").ap()
    xs_in = di("xs", [NTOK, D]); cvec = di("cvec", [2, D]); rope = di("rope", [NTOK, 192])
    cmask = di("cmask", [128, 8 * 512], BF16); sel = di("sel", [128, 8])
    w_ada = di("w_ada", [DEPTH, D, 6 * D]); b_ada = di("b_ada", [DEPTH, 6 * D]); g_attn = di("g_attn", [DEPTH, D])
    g_ffn = di("g_ffn", [DEPTH, D]); w_in = di("w_in", [DEPTH, D, 2048]); g_qk = di("g_qk", [DEPTH, 2, 64])
    lamv = di("lamv", [DEPTH, 4, 32]); sinkv = di("sinkv", [DEPTH, 6]); gsub = di("gsub", [DEPTH, 64])
    w_out = di("w_out", [DEPTH, D, D]); w_ff1 = di("w_ff1", [DEPTH, D, DFF]); w_ff3 = di("w_ff3", [DEPTH, D, DFF])
    w_ff2 = di("w_ff2", [DEPTH, DFF, D]); g_final = di("g_final", [D])
    xout = nc.dram_tensor("xout", [NLAT, D], F32, kind="ExternalOutput").ap()
    qT = it("qT_s", [1024, NTOK]); kT = it("kT_s", [512, NLAT]); vv = [it("vv_s%d" % j, [128, NLAT]) for j in range(4)]
    ckT = it("ckT_s", [512, NCTX]); cvv = it("cvv_s", [128, 1024]); oT = it("oT_s", [1024, NTOK])
    kTg = [it("kTg%d" % j, [512, NLAT]) for j in range(4)]
    vg = [it("vg%d" % j, [512, NLAT]) for j in range(4)]
    xs1 = it("xs1_s", [NTOK, D], F32)
    modd = it("modd_s", [2, 128, 4 * D], F32)
    wob = it("wob_s", [D, D]); w1b = it("w1b_s", [D, DFF]); w3b = it("w3b_s", [D, DFF]); w2b = it("w2b_s", [DFF, D])
    kg = lambda r, krow0: kTg[krow0 // 128][r * 128 + krow0 % 128:r * 128 + krow0 % 128 + 64, :]
    vgf = lambda r, unit: vg[unit // 2][r * 128:(r + 1) * 128, (unit % 2) * 2048:(unit % 2 + 1) * 2048]
    s = Sched(nc)
    consts = emit_consts(s)
    xs = xs_in
    for l in range(DEPTH):
        last = l == DEPTH - 1
        emit_pre(s, consts, xs, cvec, w_ada[l], b_ada[l], g_attn[l], w_in[l], g_qk[l], rope, qT, kT, vv, ckT, cvv)
        for j in range(4):
            s.collective(kT[j * 128:(j + 1) * 128, :], kTg[j], reads=["kT"], writes=[("kTg", j)], sem="CCk%d" % j)
            s.collective(vv[j], vg[j], reads=["vv"], writes=[("vg", j)], sem="CCv%d" % j)
        s.defer_prefix = "CC"
        emit_ffn_mod(s, consts, not last, cvec, w_ada[l], b_ada[l], g_ffn[l], modd)
        s.defer_prefix = "\0"
        s.barrier()
        for wi, (dst, src) in enumerate(((wob, w_out[l]), (w1b, w_ff1[l]), (w3b, w_ff3[l]), (w2b, w_ff2[l]))):
            s.dma("pool", dst, src, writes=[("wcast", wi)], sem="D_wcast%d" % wi)
        vown = lambda unit: vv[unit // 2][:, (unit % 2) * 2048:(unit % 2 + 1) * 2048]
        emit_attn(s, consts, l, not last, qT, kg, vgf, kT, vown, ckT, cvv, sel, cmask, lamv[l], sinkv[l], gsub[l], oT)
        emit_ffn(s, consts, last, not last, xs, modd, wob, w1b, w3b, w2b, g_final, oT, xout if last else xs1)
        xs = xs1
    s.finish()
    return nc, s


_PROG = {}


def kernel(x, c, ctx, c_ctx, w_ada, b_ada, g_attn, g_ffn, w_in, g_q, g_k, lam_q1, lam_k1, lam_q2, lam_k2,
           g_subln, sink_logit, w_out, w_ff1, w_ff3, w_ff2, g_final):
    f = lambda a: np.ascontiguousarray(np.asarray(a, dtype=np.float32))
    x, c, ctx, c_ctx = f(x), f(c), f(ctx), f(c_ctx)
    cores = list(range(8))
    shared = dict(w_ada=f(w_ada), b_ada=f(b_ada), g_attn=f(g_attn), g_ffn=f(g_ffn), w_in=f(w_in),
                  g_qk=np.ascontiguousarray(np.stack([f(g_q), f(g_k)], axis=1)),
                  lamv=np.ascontiguousarray(np.stack([f(lam_q1), f(lam_k1), f(lam_q2), f(lam_k2)], axis=1)),
                  sinkv=f(sink_logit), gsub=f(g_subln), w_out=f(w_out), w_ff1=f(w_ff1), w_ff3=f(w_ff3), w_ff2=f(w_ff2),
                  g_final=f(g_final))
    maps = []
    for i in cores:
        b, q = i // 4, i % 4
        sel = np.zeros((128, 8), np.float32)
        if q > 0:
            sel[:, q - 1] = 1.0
        if q < 3:
            sel[:, 4 + q + 1] = 1.0
        maps.append(dict(xs=np.ascontiguousarray(np.concatenate([x[b, q * NLAT:(q + 1) * NLAT], ctx[b]], 0)),
                         cvec=np.ascontiguousarray(np.stack([c[b], c_ctx])), rope=rope_tables(i), cmask=cmask_table(i), sel=sel,
                         **shared))
    if "fused" not in _PROG:
        _PROG["fused"] = build_fused()[0]
    res = run_bass_kernel_spmd(_PROG["fused"], maps, core_ids=cores).results
    out = np.empty((2, SEQ, D), np.float32)
    for i in cores:
        out[i // 4, (i % 4) * NLAT:(i % 4 + 1) * NLAT] = np.asarray(res[i]["xout"])
    return out
```

```python
import contextlib
import math
import numpy as np
import ml_dtypes
import concourse.bass as bass
import concourse.mybir as mybir
from concourse.bass_utils import run_bass_kernel_spmd

F32 = mybir.dt.float32
BF16 = mybir.dt.bfloat16
AF = mybir.ActivationFunctionType
ALU = mybir.AluOpType
AX = mybir.AxisListType

D = 1024
SEQ = 16384
NLAT = 4096
NCTX = 256
NTOK = NLAT + NCTX
NTILE = NTOK // 128
DFF = 2816
NFF = DFF // 128
DEPTH = 2
EPS = 1e-6
AQ, AK, AV, BQ, BK, BV, CQ, CK, CV = [(0, 384), (384, 512), (512, 640), (640, 896), (896, 1152), (1152, 1408),
                                      (1408, 1792), (1792, 1920), (1920, 2048)]
WIN_ORDER = [AQ, AK, BQ, BK, CQ, CK, AV, BV, CV]


class Sched:
    def __init__(self, nc):
        self.nc = nc
        self.eng = {"pe": nc.tensor, "act": nc.scalar, "dve": nc.vector, "pool": nc.gpsimd, "sp": nc.sync}
        self.stack = contextlib.ExitStack()
        self.scopes = []
        self.sems = {}
        self.cnt = {}
        self.waited = {e: {} for e in self.eng}
        self.last_w = {}
        self.readers = {}
        for e in self.eng:
            self._sem("E_" + e)
        self.n_inst = 0
        self.n_wait = 0
        self.uid = 0
        self.defer_prefix = "\0"

    def _sem(self, name):
        if name not in self.sems:
            self.sems[name] = self.stack.enter_context(self.nc.semaphore(name))
            self.cnt[name] = 0
        return self.sems[name]

    def push(self):
        st = contextlib.ExitStack()
        self.scopes.append(st)
        return st

    def pop(self):
        self.barrier()
        self.scopes.pop().close()

    def _ctx(self):
        return self.scopes[-1] if self.scopes else self.stack

    def sb(self, name, shape, dtype):
        self.uid += 1
        return self._ctx().enter_context(self.nc.sbuf_tensor("%s_%d" % (name, self.uid), list(shape), dtype))

    def ps(self, name, shape, dtype):
        self.uid += 1
        return self._ctx().enter_context(self.nc.psum_tensor("%s_%d" % (name, self.uid), list(shape), dtype))

    @staticmethod
    def _key(r):
        if isinstance(r, tuple):
            return tuple(Sched._key(x) for x in r)
        if isinstance(r, (str, int)):
            return r
        return id(r)

    def _deps(self, reads, writes):
        d = {}
        for r in reads:
            t = self.last_w.get(self._key(r))
            if t:
                d[t[0]] = max(d.get(t[0], 0), t[1])
        for w in writes:
            k = self._key(w)
            t = self.last_w.get(k)
            if t:
                d[t[0]] = max(d.get(t[0], 0), t[1])
            for t in self.readers.get(k, ()):
                d[t[0]] = max(d.get(t[0], 0), t[1])
        return d

    def _emit_waits(self, en, deps, skip_self=False):
        e = self.eng[en]
        wd = self.waited[en]
        for sn, v in deps.items():
            if skip_self and sn == "E_" + en:
                continue
            if wd.get(sn, 0) < v:
                e.wait_ge(self.sems[sn], v)
                wd[sn] = v
                self.n_wait += 1

    def _record(self, tok, reads, writes):
        for r in reads:
            self.readers.setdefault(self._key(r), []).append(tok)
        for w in writes:
            k = self._key(w)
            self.last_w[k] = tok
            self.readers[k] = []

    def op(self, en, fn, reads=(), writes=()):
        deps = self._deps(reads, writes)
        self._emit_waits(en, deps, skip_self=(en == "pe"))
        ins = fn(self.eng[en])
        sn = "E_" + en
        self.cnt[sn] += 1
        ins.then_inc(self.sems[sn], 1)
        self._record((sn, self.cnt[sn]), reads, writes)
        self.n_inst += 1
        return ins

    def dma(self, q, out, in_, reads=(), writes=(), sem=None, **kw):
        if sem is None:
            sem = "D_%x" % (hash(self._key(writes[0])) & 0xFFFFFFF)
        self._sem(sem)
        deps = self._deps(reads, writes)
        self._emit_waits(q, deps)
        ins = self.eng[q].dma_start(out=out, in_=in_, **kw)
        self.cnt[sem] += 16
        ins.then_inc(self.sems[sem], 16)
        self._record((sem, self.cnt[sem]), reads, writes)
        self.n_inst += 1
        return ins

    def collective(self, in_ap, out_ap, reads=(), writes=(), sem="CC"):
        self._sem(sem)
        deps = self._deps(reads, writes)
        self._emit_waits("pool", deps)
        ins = self.nc.gpsimd.collective_compute("AllGather", ALU.bypass, replica_groups=[[0, 1, 2, 3], [4, 5, 6, 7]],
                                                ins=[in_ap], outs=[out_ap])
        self.cnt[sem] += 1
        ins.then_inc(self.sems[sem], 1)
        self._record((sem, self.cnt[sem]), reads, writes)
        self.n_inst += 1
        return ins

    def barrier(self):
        allv = {sn: c for sn, c in self.cnt.items() if c > 0 and not sn.startswith(self.defer_prefix)}
        for en in self.eng:
            self._emit_waits(en, allv)

    def finish(self):
        self.barrier()
        while self.scopes:
            self.scopes.pop().close()
        self.stack.close()


def emit_consts(s):
    identf = s.sb("identf", [128, 128], F32)
    identb = s.sb("identb", [128, 128], BF16)
    onesf = s.sb("onesf", [128, 128], F32)
    s.op("pool", lambda e: e.memset(identf[:], 0.0), writes=[identf])
    s.op("pool", lambda e: e.affine_select(out=identf[:], in_=identf[:], pattern=[[-1, 128]],
                                           compare_op=ALU.not_equal, fill=1.0, base=0, channel_multiplier=1),
         reads=[identf], writes=[identf])
    s.op("pool", lambda e: e.tensor_copy(out=identb[:], in_=identf[:]), reads=[identf], writes=[identb])
    s.op("pool", lambda e: e.memset(onesf[:], 1.0), writes=[onesf])
    return identf, identb, onesf


def emit_mod(s, cvec, stream, w_ada, b_ada, c0, ncols, mod, identf):
    s.push()
    cb = s.sb("cb", [128, D], F32)
    sc = s.sb("sc", [128, D], F32)
    scT = s.sb("scT", [128, 8, 128], F32)
    bb = s.sb("bb", [128, ncols], F32)
    wa = [s.sb("wa%d" % i, [128, 8, 512], F32) for i in range(2)]
    ptr = s.ps("ptr", [128, 4, 128], F32)
    pm = s.ps("pm", [128, 512], F32)
    s.dma("sp", cb[:], cvec[stream].partition_broadcast(128), writes=[cb])
    s.dma("sp", bb[:], b_ada[c0:c0 + ncols].partition_broadcast(128), writes=[bb])
    s.op("act", lambda e: e.activation(out=sc[:], in_=cb[:], func=AF.Silu), reads=[cb], writes=[sc])
    for kk in range(0, 8, 4):
        for j in range(4):
            s.op("pe", lambda e: e.transpose(ptr[:, j, :], sc[:, (kk + j) * 128:(kk + j + 1) * 128], identf[:]),
                 reads=[sc, identf], writes=[ptr])
        s.op("dve", lambda e: e.tensor_copy(out=scT[:, kk:kk + 4, :], in_=ptr[:]), reads=[], writes=[ptr, (scT, kk)])
    for cbk in range(ncols // 512):
        w = wa[cbk % 2]
        s.dma("sp", w[:], w_ada[:, c0 + cbk * 512:c0 + (cbk + 1) * 512].rearrange("(k p) n -> p k n", p=128), writes=[w])
        for k in range(8):
            s.op("pe", lambda e: e.matmul(pm[:], lhsT=scT[:, k, :], rhs=w[:, k, :], start=(k == 0), stop=(k == 7)),
                 reads=[(scT, 0), (scT, 4), w], writes=[pm])
        s.op("dve", lambda e: e.tensor_tensor(out=mod[:, cbk * 512:(cbk + 1) * 512], in0=pm[:],
                                              in1=bb[:, cbk * 512:(cbk + 1) * 512], op=ALU.add),
             reads=[bb], writes=[pm, (mod, cbk)])
    s.pop()


def emit_rstd(s, ss, rstd, n, reads, writes):
    s.op("act", lambda e: e.activation(out=rstd, in_=ss, func=AF.Ln, scale=1.0 / n, bias=EPS), reads=reads, writes=writes)
    s.op("act", lambda e: e.activation(out=rstd, in_=rstd, func=AF.Exp, scale=-0.5), reads=writes, writes=writes)


def emit_rope(s, src, src_res, C, S, tabres, w, out, out_res, t, u, src_is_psum):
    G = 512 // w
    d = w // 4
    srcg = src.rearrange("p (g c) -> p g c", g=G)
    src5 = src.rearrange("p (g t a d) -> p g t a d", g=G, t=2, a=2, d=d)
    u5 = u[:].rearrange("p (g t a d) -> p g t a d", g=G, t=2, a=2, d=d)
    S4 = S.rearrange("p (t a d) -> p t a d", t=2, a=2, d=d)
    rd = [] if src_is_psum else [src_res]
    wr = [src_res] if src_is_psum else []
    s.op("dve", lambda e: e.tensor_tensor(out=t[:].rearrange("p (g c) -> p g c", g=G), in0=srcg,
                                          in1=C.unsqueeze(1).broadcast_to([128, G, w]), op=ALU.mult),
         reads=rd + [tabres], writes=wr + [t])
    for a in range(2):
        s.op("dve", lambda e: e.tensor_tensor(out=u5[:, :, :, a, :], in0=src5[:, :, :, 1 - a, :],
                                              in1=S4[:, :, a, :].unsqueeze(1).broadcast_to([128, G, 2, d]), op=ALU.mult),
             reads=rd + [tabres], writes=wr + [(u, a)])
    s.op("pool", lambda e: e.tensor_tensor(out=out, in0=t[:], in1=u[:], op=ALU.add),
         reads=[t, (u, 0), (u, 1)], writes=[out_res])


def emit_pre(s, consts, xs, cvec, w_ada, b_ada, g_attn, w_in, g_qk, rope, qT, kT, vv, ckT, cvv):
    identf, identb, onesf = consts
    s.push()
    mods = []
    for stream in range(2):
        m = s.sb("mod%d" % stream, [128, 2 * D], F32)
        emit_mod(s, cvec, stream, w_ada, b_ada, 0, 2 * D, m, identf)
        mods.append(m)
    gat = s.sb("gat", [128, D], F32)
    s.dma("sp", gat[:], g_attn.partition_broadcast(128), writes=[gat])
    for m in mods:
        s.op("dve", lambda e: e.scalar_tensor_tensor(out=m[:, D:2 * D], in0=m[:, D:2 * D], scalar=1.0, in1=gat[:],
                                                     op0=ALU.add, op1=ALU.mult),
             reads=[gat, (m, 2), (m, 3)], writes=[(m, 2), (m, 3)])
    win = s.sb("win", [128, 8, 2048], BF16)
    c = 0
    for (a, b) in WIN_ORDER:
        s.dma("pool", win[:, :, c:c + (b - a)], w_in[:, a:b].rearrange("(k p) n -> p k n", p=128), writes=[(win, c)], sem="D_win")
        c += b - a
    win_res = [(win, cc) for cc in np.cumsum([0] + [b - a for (a, b) in WIN_ORDER[:-1]]).tolist()]
    gqk = s.sb("gqk", [128, 512], F32)
    for h in range(8):
        s.dma("sp", gqk[:, h * 64:(h + 1) * 64], g_qk[0 if h < 6 else 1].partition_broadcast(128), writes=[(gqk, h)], sem="D_gqk")
    gqk_res = [(gqk, h) for h in range(8)]
    ktacc = s.sb("ktacc", [128, 4, NLAT], BF16)
    vacc = s.sb("vacc", [128, 8, 32, 64], BF16)
    ktc = s.sb("ktc", [128, 4, NCTX], BF16)
    vc = s.sb("vc", [128, 8, 2, 64], BF16)
    xt = [s.sb("xt%d" % i, [128, D], F32) for i in range(2)]
    tb = [s.sb("tb%d" % i, [128, 192], F32) for i in range(2)]
    junk = s.sb("junk", [128, D], BF16)
    ss = [s.sb("ss%d" % i, [128, 1], F32) for i in range(2)]
    rstd = [s.sb("rstd%d" % i, [128, 1], F32) for i in range(2)]
    hb = [s.sb("hb%d" % i, [128, D], BF16) for i in range(2)]
    hT = [s.sb("hT%d" % i, [128, 8, 128], BF16) for i in range(2)]
    sqa = s.sb("sqa", [128, 512], F32)
    ssa = s.sb("ssa", [128, 8], F32)
    ra = s.sb("ra", [128, 8], F32)
    qa = s.sb("qa", [128, 512], F32)
    tt = [s.sb("tt%d" % i, [128, 512], F32) for i in range(2)]
    uu = [s.sb("uu%d" % i, [128, 512], F32) for i in range(2)]
    qkb = [s.sb("qkb%d" % i, [128, 1536], BF16) for i in range(2)]
    stg = [s.sb("stg%d" % i, [128, 4, 512], F32) for i in range(2)]
    qst = [s.sb("qst%d" % i, [128, 8, 512], BF16) for i in range(2)]
    pT = s.ps("pT", [128, 8, 128], BF16)
    pq = s.ps("pq", [128, 4, 512], F32)
    ptq = [s.ps("ptq%d" % i, [128, 8, 128], BF16) for i in range(2)]

    for i in range(NTILE):
        p = i % 2
        isctx = i >= 32
        m = mods[1] if isctx else mods[0]
        s.dma("sp", xt[p][:], xs[i * 128:(i + 1) * 128, :], writes=[xt[p]])
        s.dma("sp", tb[p][:], rope[i * 128:(i + 1) * 128, :], writes=[tb[p]])
        s.op("act", lambda e: e.activation(out=junk[:], in_=xt[p][:], func=AF.Square, accum_out=ss[p][:]),
             reads=[xt[p]], writes=[junk, ss[p]])
        emit_rstd(s, ss[p][:], rstd[p][:], D, [ss[p]], [rstd[p]])
        s.op("dve", lambda e: e.scalar_tensor_tensor(out=xt[p][:], in0=xt[p][:], scalar=rstd[p][:, 0:1], in1=m[:, D:2 * D],
                                                     op0=ALU.mult, op1=ALU.mult),
             reads=[rstd[p], (m, 2), (m, 3)], writes=[xt[p]])
        s.op("pool", lambda e: e.tensor_tensor(out=hb[p][:], in0=xt[p][:], in1=m[:, 0:D], op=ALU.add),
             reads=[xt[p], (m, 0), (m, 1)], writes=[hb[p]])
        for k in range(8):
            s.op("pe", lambda e: e.transpose(pT[:, k, :], hb[p][:, k * 128:(k + 1) * 128], identb[:]),
                 reads=[hb[p], identb], writes=[pT])
        s.op("act", lambda e: e.copy(out=hT[p][:], in_=pT[:]), reads=[], writes=[pT, hT[p]])
        for cb in range(4):
            for k in range(8):
                s.op("pe", lambda e: e.matmul(pq[:, cb, :], lhsT=hT[p][:, k, :], rhs=win[:, k, cb * 512:(cb + 1) * 512],
                                              start=(k == 0), stop=(k == 7)),
                     reads=[hT[p]] + win_res, writes=[(pq, cb)])
        s.op("act", lambda e: e.copy(out=stg[p][:, 0:2, :], in_=pq[:, 0:2, :]), reads=[], writes=[(pq, 0), (pq, 1), (stg[p], 0), (stg[p], 1)])
        s.op("dve", lambda e: e.tensor_copy(out=stg[p][:, 2:4, :], in_=pq[:, 2:4, :]), reads=[], writes=[(pq, 2), (pq, 3), (stg[p], 2), (stg[p], 3)])
        vdst = vc[:, :, i - 32, :] if isctx else vacc[:, :, i, :]
        s.op("act", lambda e: e.copy(out=vdst, in_=stg[p][:, 3, :].rearrange("p (u e) -> p u e", u=8)),
             reads=[(stg[p], 3)], writes=[("vacc", i)])
        s.op("act", lambda e: e.activation(out=sqa[:], in_=stg[p][:, 0, :], func=AF.Square), reads=[(stg[p], 0)], writes=[sqa])
        s.op("dve", lambda e: e.reduce_sum(out=ssa[:], in_=sqa[:].rearrange("p (h c) -> p h c", h=8), axis=AX.X),
             reads=[sqa], writes=[ssa])
        emit_rstd(s, ssa[:], ra[:], 64, [ssa], [ra])
        s.op("dve", lambda e: e.tensor_tensor(out=qa[:].rearrange("p (h c) -> p h c", h=8),
                                              in0=stg[p][:, 0, :].rearrange("p (h c) -> p h c", h=8),
                                              in1=ra[:].unsqueeze(2).broadcast_to([128, 8, 64]), op=ALU.mult),
             reads=[ra, (stg[p], 0)], writes=[qa])
        s.op("dve", lambda e: e.tensor_tensor(out=qa[:], in0=qa[:], in1=gqk[:], op=ALU.mult),
             reads=gqk_res, writes=[qa])
        emit_rope(s, qa[:], qa, tb[p][:, 0:64], tb[p][:, 64:128], tb[p], 64, qkb[p][:, 0:512], (qkb[p], 0), tt[0], uu[0], False)
        emit_rope(s, stg[p][:, 1, :], (stg[p], 1), tb[p][:, 128:160], tb[p][:, 160:192], tb[p], 32, qkb[p][:, 512:1024], (qkb[p], 1), tt[1], uu[1], False)
        emit_rope(s, stg[p][:, 2, :], (stg[p], 2), tb[p][:, 0:64], tb[p][:, 64:128], tb[p], 64, qkb[p][:, 1024:1536], (qkb[p], 2), tt[0], uu[0], False)
        gi = i // 4
        gp = gi % 2
        tl = i % 4
        for c12 in range(12):
            pt_, j = ptq[c12 // 8], c12 % 8
            s.op("pe", lambda e: e.transpose(pt_[:, j, :], qkb[p][:, c12 * 128:(c12 + 1) * 128], identb[:]),
                 reads=[(qkb[p], c12 // 4), identb], writes=[pt_])
        kdst = (lambda j0, n: ktc[:, j0:j0 + n, (i - 32) * 128:(i - 31) * 128]) if isctx else \
               (lambda j0, n: ktacc[:, j0:j0 + n, i * 128:(i + 1) * 128])
        qd = lambda j0, n: qst[gp][:, j0:j0 + n, tl * 128:(tl + 1) * 128]
        s.op("act", lambda e: e.copy(out=qd(0, 3), in_=ptq[0][:, 0:3, :]), reads=[], writes=[ptq[0], (qst[gp], tl, 0)])
        s.op("dve", lambda e: e.tensor_copy(out=kdst(0, 1), in_=ptq[0][:, 3:4, :]), reads=[], writes=[ptq[0], ("ktacc", i, 0)])
        s.op("act", lambda e: e.copy(out=qd(3, 2), in_=ptq[0][:, 4:6, :]), reads=[], writes=[ptq[0], (qst[gp], tl, 1)])
        s.op("dve", lambda e: e.tensor_copy(out=kdst(1, 2), in_=ptq[0][:, 6:8, :]), reads=[], writes=[ptq[0], ("ktacc", i, 1)])
        s.op("act", lambda e: e.copy(out=qd(5, 3), in_=ptq[1][:, 0:3, :]), reads=[], writes=[ptq[1], (qst[gp], tl, 2)])
        s.op("dve", lambda e: e.tensor_copy(out=kdst(3, 1), in_=ptq[1][:, 3:4, :]), reads=[], writes=[ptq[1], ("ktacc", i, 2)])
        ntl = 4 if gi < 8 else 2
        if tl == ntl - 1:
            t0 = gi * 512
            s.dma("sp", qT.rearrange("(c r) t -> r c t", r=128)[:, :, t0:t0 + ntl * 128], qst[gp][:, :, 0:ntl * 128],
                  reads=[(qst[gp], a, b) for a in range(ntl) for b in range(3)],
                  writes=[("qT", gi)] + [(qst[gp], a, b) for a in range(ntl) for b in range(3)], sem="D_qT")
    allk = [("ktacc", i, j) for i in range(NTILE) for j in range(3)]
    allv = [("vacc", i) for i in range(NTILE)]
    s.dma("sp", kT.rearrange("(c r) t -> r c t", r=128), ktacc[:], reads=allk, writes=["kT"], sem="D_kvout")
    s.dma("sp", ckT.rearrange("(c r) t -> r c t", r=128), ktc[:], reads=allk, writes=["ckT"], sem="D_kvout")
    for j in range(4):
        s.dma("sp", vv[j], vacc[:, 2 * j:2 * j + 2, :, :].rearrange("p u k e -> p (u k e)"), reads=allv, writes=["vv"], sem="D_kvout")
    s.dma("sp", cvv, vc[:].rearrange("p u k e -> p (u k e)"), reads=allv, writes=["cvv"], sem="D_kvout")
    s.pop()


def rope_tables(core):
    t = (core % 4) * NLAT + np.arange(NLAT)
    rows = (t // 64).astype(np.float32)
    cols = (t % 64).astype(np.float32)

    def tab(half):
        fr = (10000.0 ** (-np.arange(half, dtype=np.float32) / half)).astype(np.float32)
        ar = rows[:, None] * fr[None, :]
        ac = cols[:, None] * fr[None, :]
        cr, sr, cc, sc = np.cos(ar), np.sin(ar), np.cos(ac), np.sin(ac)
        C = np.concatenate([cr, cr, cc, cc], axis=1)
        S = np.concatenate([-sr, sr, -sc, sc], axis=1)
        return C.astype(np.float32), S.astype(np.float32)

    C64, S64 = tab(16)
    C32, S32 = tab(8)
    lat = np.concatenate([C64, S64, C32, S32], axis=1)
    ctx = np.zeros((NCTX, 192), np.float32)
    ctx[:, 0:64] = 1.0
    ctx[:, 128:160] = 1.0
    return np.ascontiguousarray(np.concatenate([lat, ctx], axis=0))


def cmask_table(core):
    j = np.arange(128)[:, None]
    i = np.arange(128)[None, :]
    mp = (j >= i).astype(np.float32)
    mn = (j <= i).astype(np.float32)
    one = np.ones((128, 128), np.float32)
    pats = np.zeros((8, 128, 512), np.float32)
    for t in range(6):
        for b in range(4):
            blk = mp if t == b else one if t == b + 1 else mn if t == b + 2 else None
            if blk is not None:
                pats[t, :, b * 128:(b + 1) * 128] = blk
    q = core % 4
    pats[6] = pats[0] if q > 0 else 0.0
    pats[7] = pats[5] if q < 3 else 0.0
    return np.ascontiguousarray(pats.transpose(1, 0, 2).reshape(128, 8 * 512)).astype(ml_dtypes.bfloat16)


def emit_attn(s, consts, layer, do_ctx, qT, kg, vgf, kT_own, v_own, ckT, cvv, sel, cmask, lamv, sinkv, gsub, oT):
    identf, identb, onesf = consts
    lam_init = 0.8 - 0.6 * math.exp(-0.3 * layer)
    s.push()
    lq = s.sb("lq", [128, 4, 32], F32)
    for i in range(4):
        s.dma("sp", lq[:, i, :], lamv[i].partition_broadcast(128), writes=[(lq, i)], sem="D_small")
    lp = s.sb("lp", [128, 2, 32], F32)
    ld = s.sb("ld", [128, 2], F32)
    neglam = s.sb("neglam", [128, 1], F32)
    s.op("dve", lambda e: e.tensor_tensor(out=lp[:], in0=lq[:, 0:4:2, :], in1=lq[:, 1:4:2, :], op=ALU.mult),
         reads=[(lq, i) for i in range(4)], writes=[lp])
    s.op("dve", lambda e: e.reduce_sum(out=ld[:], in_=lp[:], axis=AX.X), reads=[lp], writes=[ld])
    s.op("act", lambda e: e.activation(out=ld[:], in_=ld[:], func=AF.Exp), reads=[ld], writes=[ld])
    s.op("dve", lambda e: e.tensor_tensor(out=neglam[:], in0=ld[:, 1:2], in1=ld[:, 0:1], op=ALU.subtract), reads=[ld], writes=[neglam])
    s.op("dve", lambda e: e.tensor_scalar_add(out=neglam[:], in0=neglam[:], scalar1=-lam_init), reads=[neglam], writes=[neglam])
    sinkexp = s.sb("sinkexp", [128, 6], F32)
    s.dma("sp", sinkexp[:], sinkv.partition_broadcast(128), writes=[sinkexp], sem="D_small")
    s.op("act", lambda e: e.activation(out=sinkexp[:], in_=sinkexp[:], func=AF.Exp), reads=[sinkexp], writes=[sinkexp])
    gsubs = s.sb("gsubs", [64, 1], F32)
    s.dma("sp", gsubs[:], gsub.rearrange("(p o) -> p o", o=1), writes=[gsubs], sem="D_small")
    s.op("dve", lambda e: e.tensor_scalar_mul(out=gsubs[:], in0=gsubs[:], scalar1=1.0 - lam_init), reads=[gsubs], writes=[gsubs])
    cm = s.sb("cm", [128, 8, 512], BF16)
    s.dma("sp", cm[:], cmask.rearrange("p (m q) -> p m q", m=8), writes=[cm], sem="D_small")

    NKT = 130
    KT = [s.sb("KT%d" % i, [128, NKT * 128], BF16) for i in range(2)]
    VA = [s.sb("VA%d" % i, [128, NKT, 128], BF16) for i in range(2)]
    QB = [s.sb("QB%d" % i, [128, 3, 512], BF16) for i in range(2)]
    QBB = [s.sb("QBB%d" % i, [128, 2, 512], BF16) for i in range(2)]
    for i in range(2):
        s.op("pool", lambda e: e.memset(VA[i][:, :, 64:128], 1.0), writes=[("VAones", i)])
        s.op("pool", lambda e: e.memset(KT[i][64:128, :], 0.0), writes=[("KTz", i)])
        s.op("pool", lambda e: e.memset(QB[i][64:128, :, :], 0.0), writes=[("QBz", i)])
        s.op("pool", lambda e: e.memset(QBB[i][:], 0.0), writes=[("QB", "B", i)])
    PT = [s.sb("PT%d" % i, [128, 2, 512], BF16) for i in range(3)]
    rs = [s.sb("rs%d" % i, [128, 512], F32) for i in range(2)]
    bcs = [s.sb("bcs%d" % i, [64, 512], F32) for i in range(2)]
    t1 = s.sb("t1", [64, 512], F32)
    t2 = s.sb("t2", [64, 512], F32)
    od = s.sb("od", [64, 512], F32)
    sq = s.sb("sq", [64, 512], F32)
    rst = s.sb("rst", [64, 512], F32)
    OTs = [s.sb("OTs%d" % i, [64, 512], BF16) for i in range(2)]
    Sb = [s.ps("Sb%d" % i, [128, 2, 512], F32) for i in range(3)]
    Ob = [s.ps("Ob%d" % i, [128, 512], F32) for i in range(2)]

    def load_kv(buf, krow0, unit):
        for r in range(4):
            s.dma("sp", KT[buf][0:64, r * NLAT:(r + 1) * NLAT], kg(r, krow0), reads=[("kTg", krow0 // 128)], writes=[("KT", buf)],
                  sem="D_kv%d" % buf)
            s.dma("sp", VA[buf][:, r * 32:(r + 1) * 32, 0:64],
                  vgf(r, unit).rearrange("p (k e) -> p k e", e=64), reads=[("vg", unit // 2)], writes=[("VA", buf)], sem="D_kv%d" % buf)
        s.dma("sp", KT[buf][0:64, 4 * NLAT:4 * NLAT + NCTX], ckT[krow0:krow0 + 64, :], writes=[("KT", buf)], sem="D_kv%d" % buf)
        s.dma("sp", VA[buf][:, 128:130, 0:64], cvv[:, unit * 128:(unit + 1) * 128].rearrange("p (k e) -> p k e", e=64),
              writes=[("VA", buf)], sem="D_kv%d" % buf)

    candk = s.sb("candk", [64, 4, 128], BF16)
    candv = s.sb("candv", [128, 4, 64], BF16)
    acck = s.sb("acck", [64, 128], F32)
    accv = s.sb("accv", [128, 64], F32)
    selt = s.sb("selt", [128, 8], F32)
    s.dma("sp", selt[:], sel, writes=[selt], sem="D_small")

    def load_kv_c(buf, g):
        KC, VC = KT[buf], VA[buf]
        sem = "D_kv%d" % buf
        s.dma("sp", KC[0:64, 128:128 + NLAT], kT_own[384 + g * 64:384 + (g + 1) * 64, :], writes=[("KT", buf)], sem=sem)
        s.dma("sp", KC[0:64, 34 * 128:36 * 128], ckT[384 + g * 64:384 + (g + 1) * 64, :], writes=[("KT", buf)], sem=sem)
        s.dma("sp", VC[:, 1:33, 0:64], v_own(6 + g).rearrange("p (k e) -> p k e", e=64), writes=[("VA", buf)], sem=sem)
        s.dma("sp", VC[:, 34:36, 0:64], cvv[:, (6 + g) * 128:(7 + g) * 128].rearrange("p (k e) -> p k e", e=64), writes=[("VA", buf)], sem=sem)
        for side in range(2):
            for r in range(4):
                kcols = (NLAT - 128, NLAT) if side == 0 else (0, 128)
                s.dma("sp", candk[:, r, :], kg(r, 384 + g * 64)[:, kcols[0]:kcols[1]], reads=[("kTg", 3)], writes=[candk], sem="D_cand")
                vt = 31 if side == 0 else 0
                s.dma("sp", candv[:, r, :], vgf(r, 6 + g)[:, vt * 64:(vt + 1) * 64], reads=[("vg", 3)], writes=[candv], sem="D_cand")
            for (cand, acc, np_) in ((candk, acck, 64), (candv, accv, 128)):
                s.op("dve", lambda e: e.tensor_scalar(out=acc[:], in0=cand[:, 0, :], scalar1=selt[0:np_, 4 * side:4 * side + 1], scalar2=None,
                                                      op0=ALU.mult), reads=[cand, selt], writes=[acc])
                for r in range(1, 4):
                    s.op("dve", lambda e: e.scalar_tensor_tensor(out=acc[:], in0=cand[:, r, :], scalar=selt[0:np_, 4 * side + r:4 * side + r + 1],
                                                                 in1=acc[:], op0=ALU.mult, op1=ALU.add), reads=[cand, selt], writes=[acc])
            kc0 = 0 if side == 0 else 33 * 128
            s.op("dve", lambda e: e.tensor_copy(out=KC[0:64, kc0:kc0 + 128], in_=acck[:]), reads=[acck], writes=[("KT", buf)])
            s.op("dve", lambda e: e.tensor_copy(out=VC[:, 0 if side == 0 else 33, 0:64], in_=accv[:]), reads=[accv], writes=[("VA", buf)])

    def load_group(gi):
        kind, g = groups[gi]
        if kind == "C":
            load_kv_c(gi % 2, g)
        else:
            kk, uu = kv_spec(kind, g)
            load_kv(gi % 2, kk, uu)

    def bcast_row(dst_ps, dres, row_ap, nq, rres):
        s.op("pe", lambda e: e.matmul(dst_ps[0:64, 0:nq], lhsT=onesf[64:65, 0:64], rhs=row_ap, start=True, stop=True),
             reads=[rres, onesf], writes=[dres])

    groups = [("A", g) for g in range(2)] + [("B", h) for h in range(4)] + [("C", g) for g in range(2)]
    qblocks = [(j * 512, 512, list(range(128)) + [128, 129]) for j in range(8)]
    if do_ctx:
        qblocks.append((NLAT, NCTX, [128, 129]))

    def kv_spec(kind, g):
        if kind == "A":
            return g * 64, g
        return 128 + g * 64, 2 + g

    def q_rows(kind, g):
        return (g * 192, 3) if kind == "A" else (640 + g * 192, 3) if kind == "C" else (384 + g * 64, 1)

    cblocks = [(j * 512, 512, [4 * j + t for t in range(6)] + [34, 35]) for j in range(8)]
    if do_ctx:
        cblocks.append((NLAT, NCTX, [34, 35]))
    jobs = []
    for gi, (kind, g) in enumerate(groups):
        for bi, (q0, nq, tiles) in enumerate(cblocks if kind == "C" else qblocks):
            jb = dict(gi=gi, kind=kind, g=g, q0=q0, nq=nq, tiles=tiles, masks=None)
            if kind == "C" and nq == 512:
                pats = list(range(6))
                if bi == 0:
                    pats[0] = 6
                if bi == 7:
                    pats[5] = 7
                jb["masks"] = pats
            jobs.append(jb)

    def load_q(jn):
        jb = jobs[jn]
        r0, nh = q_rows(jb["kind"], jb["g"])
        q0, nq = jb["q0"], jb["nq"]
        if jb["kind"] in ("A", "C"):
            s.dma("sp", QB[jn % 2][0:64, 0:nh, 0:nq], qT[r0:r0 + nh * 64, q0:q0 + nq].rearrange("(h r) t -> r h t", r=64),
                  writes=[("QB", jn % 2)], sem="D_q%d" % (jn % 2))
        else:
            for k in range(2):
                s.dma("sp", QBB[jn % 2][32 * k:32 * k + 32, k, 0:nq], qT[r0 + 32 * k:r0 + 32 * k + 32, q0:q0 + nq],
                      writes=[("QB", "B", jn % 2)], sem="D_qb%d" % (jn % 2))

    steps = []
    ocnt = 0
    for jn, jb in enumerate(jobs):
        kind, nq, tiles = jb["kind"], jb["nq"], jb["tiles"]
        if kind in ("A", "C"):
            for r in range(3):
                ob = ocnt % 2
                ocnt += 1
                npair = len(tiles) // 2
                for pi in range(npair):
                    mk = None
                    if jb["masks"] is not None and pi < 3:
                        mk = (jb["masks"][2 * pi], jb["masks"][2 * pi + 1])
                    steps.append(dict(jn=jn, kind=kind, r=r, ob=ob, t=(tiles[2 * pi], tiles[2 * pi + 1]),
                                      first=(pi == 0), last=(pi == npair - 1), mk=mk))
                steps.append(dict(jn=jn, fin=True, src=steps[-1]))
        else:
            for ti, t in enumerate(tiles):
                steps.append(dict(jn=jn, kind="B", t=(t,), first=(ti == 0), last=(ti == len(tiles) - 1)))
            steps.append(dict(jn=jn, fin=True, src=steps[-1]))

    def emit_S(i):
        st = steps[i]
        if st.get("fin"):
            return
        jb = jobs[st["jn"]]
        buf, qb, nq, sb = jb["gi"] % 2, st["jn"] % 2, jb["nq"], Sb[i % 3]
        if st["kind"] in ("A", "C"):
            rd = [("KT", buf), ("KTz", buf), ("QB", qb), ("QBz", qb)]
            for k, t in enumerate(st["t"]):
                s.op("pe", lambda e: e.matmul(sb[:, k, 0:nq], lhsT=KT[buf][:, t * 128:(t + 1) * 128],
                                              rhs=QB[qb][:, st["r"], 0:nq], start=True, stop=True), reads=rd, writes=[sb])
        else:
            rd = [("KT", buf), ("KTz", buf), ("QB", "B", qb)]
            t = st["t"][0]
            for k in range(2):
                s.op("pe", lambda e: e.matmul(sb[:, k, 0:nq], lhsT=KT[buf][:, t * 128:(t + 1) * 128],
                                              rhs=QBB[qb][:, k, 0:nq], start=True, stop=True), reads=rd, writes=[sb])

    def emit_rest(i):
        st = steps[i]
        fin = st.get("fin", False)
        if fin:
            st = st["src"]
        jb = jobs[st["jn"]]
        buf, nq, sb, pt = jb["gi"] % 2, jb["nq"], Sb[i % 3], PT[i % 3]
        kind = st["kind"]
        if fin:
            emit_fin(st, jb, kind, nq, sb)
            return
        scale = 32 ** -0.5 if kind == "B" else 0.125
        s.op("act", lambda e: e.activation(out=pt[:, :, 0:nq], in_=sb[:, :, 0:nq], func=AF.Exp, scale=scale),
             reads=[], writes=[sb, pt])
        if st.get("mk") is not None:
            m0, m1 = st["mk"]
            if m1 == m0 + 1:
                s.op("dve", lambda e: e.tensor_tensor(out=pt[:], in0=pt[:], in1=cm[:, m0:m0 + 2, :], op=ALU.mult), reads=[cm], writes=[pt])
            else:
                for k, mm in enumerate((m0, m1)):
                    s.op("dve", lambda e: e.tensor_tensor(out=pt[:, k, :], in0=pt[:, k, :], in1=cm[:, mm, :], op=ALU.mult), reads=[cm], writes=[pt])
        rdv = [("VA", buf), ("VAones", buf), pt]
        if kind in ("A", "C"):
            o = Ob[st["ob"]]
            for k, t in enumerate(st["t"]):
                s.op("pe", lambda e: e.matmul(o[:, 0:nq], lhsT=VA[buf][:, t, :], rhs=pt[:, k, 0:nq],
                                              start=(st["first"] and k == 0), stop=(st["last"] and k == 1)), reads=rdv, writes=[o])
        else:
            t = st["t"][0]
            for k in range(2):
                s.op("pe", lambda e: e.matmul(Ob[k][:, 0:nq], lhsT=VA[buf][:, t, :], rhs=pt[:, k, 0:nq],
                                              start=st["first"], stop=st["last"]), reads=rdv, writes=[Ob[k]])

    def emit_fin(st, jb, kind, nq, sb):
        Mb = [sb[:, 0, :], sb[:, 1, :]]
        q0 = jb["q0"]
        if kind in ("A", "C"):
            o, ob = Ob[st["ob"]], st["ob"]
            if kind == "C":
                hh = 3 * jb["g"] + st["r"]
                s.op("dve", lambda e: e.tensor_scalar(out=rs[ob][64:65, 0:nq], in0=o[64:65, 0:nq], scalar1=sinkexp[64:65, hh:hh + 1],
                                                      scalar2=None, op0=ALU.add), reads=[sinkexp], writes=[o, rs[ob]])
                s.op("dve", lambda e: e.reciprocal(out=rs[ob][64:65, 0:nq], in_=rs[ob][64:65, 0:nq]), reads=[], writes=[rs[ob]])
            else:
                s.op("dve", lambda e: e.reciprocal(out=rs[ob][64:65, 0:nq], in_=o[64:65, 0:nq]), reads=[], writes=[o, rs[ob]])
            bcast_row(Mb[0], sb, rs[ob][64:65, 0:nq], nq, rs[ob])
            s.op("act", lambda e: e.copy(out=bcs[ob][:, 0:nq], in_=Mb[0][0:64, 0:nq]), reads=[], writes=[sb, bcs[ob]])
            s.op("dve", lambda e: e.tensor_tensor(out=OTs[ob][:, 0:nq], in0=o[0:64, 0:nq], in1=bcs[ob][:, 0:nq], op=ALU.mult),
                 reads=[bcs[ob]], writes=[o, OTs[ob]])
            row0 = ((0 if kind == "A" else 10) + jb["g"] * 3 + st["r"]) * 64
            s.dma("sp", oT[row0:row0 + 64, q0:q0 + nq], OTs[ob][:, 0:nq], reads=[OTs[ob]], writes=[("oT", row0, q0)], sem="D_oT")
        else:
            for k in range(2):
                s.op("dve", lambda e: e.reciprocal(out=rs[k][64:65, 0:nq], in_=Ob[k][64:65, 0:nq]), reads=[], writes=[Ob[k], rs[k]])
            s.op("dve", lambda e: e.tensor_scalar(out=rs[1][64:65, 0:nq], in0=rs[1][64:65, 0:nq], scalar1=neglam[64:65, 0:1],
                                                  scalar2=None, op0=ALU.mult), reads=[neglam], writes=[rs[1]])
            for k in range(2):
                bcast_row(Mb[k], sb, rs[k][64:65, 0:nq], nq, rs[k])
                s.op("act", lambda e: e.copy(out=bcs[k][:, 0:nq], in_=Mb[k][0:64, 0:nq]), reads=[], writes=[sb, bcs[k]])
            s.op("dve", lambda e: e.tensor_tensor(out=t1[:, 0:nq], in0=Ob[0][0:64, 0:nq], in1=bcs[0][:, 0:nq], op=ALU.mult),
                 reads=[bcs[0]], writes=[Ob[0], t1])
            s.op("dve", lambda e: e.tensor_tensor(out=t2[:, 0:nq], in0=Ob[1][0:64, 0:nq], in1=bcs[1][:, 0:nq], op=ALU.mult),
                 reads=[bcs[1]], writes=[Ob[1], t2])
            s.op("pool", lambda e: e.tensor_tensor(out=od[:, 0:nq], in0=t1[:, 0:nq], in1=t2[:, 0:nq], op=ALU.add), reads=[t1, t2], writes=[od])
            s.op("pool", lambda e: e.tensor_tensor(out=sq[:, 0:nq], in0=od[:, 0:nq], in1=od[:, 0:nq], op=ALU.mult), reads=[od], writes=[sq])
            s.op("pe", lambda e: e.matmul(Mb[0][0:64, 0:nq], lhsT=onesf[0:64, 0:64], rhs=sq[:, 0:nq], start=True, stop=True),
                 reads=[sq, onesf], writes=[sb])
            s.op("act", lambda e: e.activation(out=rst[:, 0:nq], in_=Mb[0][0:64, 0:nq], func=AF.Ln, scale=1.0 / 64, bias=EPS),
                 reads=[], writes=[sb, rst])
            s.op("act", lambda e: e.activation(out=rst[:, 0:nq], in_=rst[:, 0:nq], func=AF.Exp, scale=-0.5), reads=[rst], writes=[rst])
            s.op("dve", lambda e: e.scalar_tensor_tensor(out=OTs[0][:, 0:nq], in0=od[:, 0:nq], scalar=gsubs[:, 0:1], in1=rst[:, 0:nq],
                                                         op0=ALU.mult, op1=ALU.mult), reads=[od, gsubs, rst], writes=[OTs[0]])
            row0 = (6 + jb["g"]) * 64
            s.dma("sp", oT[row0:row0 + 64, q0:q0 + nq], OTs[0][:, 0:nq], reads=[OTs[0]], writes=[("oT", row0, q0)], sem="D_oT")

    load_group(0)
    load_q(0)
    emit_S(0)
    emit_S(1)
    cur_job = -1
    for i, st in enumerate(steps):
        if st["jn"] != cur_job:
            cur_job = st["jn"]
            jb = jobs[cur_job]
            if cur_job + 1 < len(jobs):
                load_q(cur_job + 1)
            if (cur_job == 0 or jobs[cur_job - 1]["gi"] != jb["gi"]) and jb["gi"] + 1 < len(groups):
                load_group(jb["gi"] + 1)
        if i + 2 < len(steps):
            emit_S(i + 2)
        emit_rest(i)

    s.pop()


def emit_ffn_mod(s, consts, do_ctx, cvec, w_ada, b_ada, g_ffn, modd):
    identf = consts[0]
    s.push()
    gff = s.sb("gff", [128, D], F32)
    s.dma("sp", gff[:], g_ffn.partition_broadcast(128), writes=[gff])
    mod = s.sb("modm", [128, 4 * D], F32)
    modres = [(mod, i) for i in range(8)]
    for stream in range(2 if do_ctx else 1):
        emit_mod(s, cvec, stream, w_ada, b_ada, 2 * D, 4 * D, mod, identf)
        s.op("dve", lambda e: e.scalar_tensor_tensor(out=mod[:, 2 * D:3 * D], in0=mod[:, 2 * D:3 * D], scalar=1.0, in1=gff[:],
                                                     op0=ALU.add, op1=ALU.mult), reads=[gff] + modres, writes=modres)
        s.dma("sp", modd[stream], mod[:], reads=modres, writes=[("modd", stream)] + modres, sem="D_modd")
    s.pop()


def emit_ffn(s, consts, last, do_ctx, xs, modd, w_out, w_ff1, w_ff3, w_ff2, g_final, oT, xout):
    identf, identb, onesf = consts
    s.push()
    wout = s.sb("wout", [128, 8, D], BF16)
    w1 = s.sb("w1", [128, 8, DFF], BF16)
    w3 = s.sb("w3", [128, 8, DFF], BF16)
    w2 = s.sb("w2", [128, NFF, D], BF16)
    wres = []
    for k in range(8):
        s.dma("sp", wout[:, k, :], w_out[k * 128:(k + 1) * 128, :], reads=[("wcast", 0)], writes=[(wout, k)], sem="D_w")
        s.dma("sp", w1[:, k, :], w_ff1[k * 128:(k + 1) * 128, :], reads=[("wcast", 1)], writes=[(w1, k)], sem="D_w")
        s.dma("sp", w3[:, k, :], w_ff3[k * 128:(k + 1) * 128, :], reads=[("wcast", 2)], writes=[(w3, k)], sem="D_w")
        wres += [(wout, k), (w1, k), (w3, k)]
    for k in range(0, NFF, 2):
        s.dma("sp", w2[:, k:k + 2, :], w_ff2[k * 128:(k + 2) * 128, :].rearrange("(c p) n -> p c n", p=128), reads=[("wcast", 3)],
              writes=[(w2, k)], sem="D_w")
        wres.append((w2, k))
    gfin = None
    if last:
        gfin = s.sb("gfin", [128, D], F32)
        s.dma("sp", gfin[:], g_final.partition_broadcast(128), writes=[gfin])
    mod = s.sb("modf", [128, 4 * D], F32)
    modres = [(mod, i) for i in range(8)]
    xt = [s.sb("fxt%d" % i, [128, D], F32) for i in range(2)]
    x1 = [s.sb("x1_%d" % i, [128, D], F32) for i in range(2)]
    ss = [s.sb("fss%d" % i, [128, 1], F32) for i in range(2)]
    rstd = [s.sb("frstd%d" % i, [128, 1], F32) for i in range(2)]
    hb = [s.sb("fhb%d" % i, [128, D], BF16) for i in range(2)]
    h2T = s.sb("h2T", [128, 8, 128], BF16)
    OTt = [s.sb("OTt%d" % i, [128, 8, 128], BF16) for i in range(2)]
    sa = s.sb("sa", [128, 512], F32)
    junk = sa[:].bitcast(BF16)
    uT = s.sb("uT", [128, NFF, 128], BF16)
    pya = s.ps("pya", [128, 2, 512], F32)
    py = s.ps("py", [128, 2, 512], F32)
    pa = [s.ps("pa%d" % i, [128, 512], F32) for i in range(2)]
    pb = [s.ps("pb%d" % i, [128, 512], F32) for i in range(2)]
    pT = pa[0][:].bitcast(BF16).rearrange("p (k t) -> p k t", k=8)
    grps = [list(range(c, min(c + 4, NFF))) for c in range(0, NFF, 4)]

    def stage_a(n, i):
        p = n % 2
        s.dma("sp", xt[p][:], xs[i * 128:(i + 1) * 128, :], writes=[xt[p]])
        s.dma("sp", OTt[p][:], oT.rearrange("(c r) t -> r c t", r=128)[:, :, i * 128:(i + 1) * 128], writes=[OTt[p]])
        for h in range(2):
            for c in range(8):
                s.op("pe", lambda e: e.matmul(pya[:, h, :], lhsT=OTt[p][:, c, :], rhs=wout[:, c, h * 512:(h + 1) * 512],
                                              start=(c == 0), stop=(c == 7)), reads=[OTt[p]] + wres, writes=[pya])
        s.op("dve", lambda e: e.tensor_tensor(out=x1[p][:], in0=pya[:].rearrange("p a b -> p (a b)"), in1=mod[:, 0:D], op=ALU.mult),
             reads=modres, writes=[pya, x1[p]])
        s.op("dve", lambda e: e.tensor_tensor(out=x1[p][:], in0=x1[p][:], in1=xt[p][:], op=ALU.add), reads=[xt[p]], writes=[x1[p]])
        s.op("act", lambda e: e.activation(out=junk, in_=x1[p][:], func=AF.Square, accum_out=ss[p][:]), reads=[x1[p]], writes=[sa, ss[p]])
        emit_rstd(s, ss[p][:], rstd[p][:], D, [ss[p]], [rstd[p]])
        s.op("dve", lambda e: e.scalar_tensor_tensor(out=xt[p][:], in0=x1[p][:], scalar=rstd[p][:, 0:1], in1=mod[:, 2 * D:3 * D],
                                                     op0=ALU.mult, op1=ALU.mult), reads=[x1[p], rstd[p]] + modres, writes=[xt[p]])
        s.op("pool", lambda e: e.tensor_tensor(out=hb[p][:], in0=xt[p][:], in1=mod[:, D:2 * D], op=ALU.add),
             reads=[xt[p]] + modres, writes=[hb[p]])

    def stage_b(n, i):
        p = n % 2
        for k in range(8):
            s.op("pe", lambda e: e.transpose(pT[:, k, :], hb[p][:, k * 128:(k + 1) * 128], identb[:]), reads=[hb[p], identb], writes=[pa[0]])
        s.op("act", lambda e: e.copy(out=h2T[:], in_=pT), reads=[], writes=[pa[0], h2T])
        for gi, grp in enumerate(grps):
            gp = gi % 2
            n_ = len(grp) * 128
            for (w_, pp) in ((w1, pa[gp]), (w3, pb[gp])):
                for j, c in enumerate(grp):
                    for k in range(8):
                        s.op("pe", lambda e: e.matmul(pp[:, j * 128:(j + 1) * 128], lhsT=w_[:, k, c * 128:(c + 1) * 128],
                                                      rhs=h2T[:, k, :], start=(k == 0), stop=(k == 7)), reads=[h2T] + wres, writes=[pp])
            s.op("act", lambda e: e.activation(out=sa[:, 0:n_], in_=pa[gp][:, 0:n_], func=AF.Silu), reads=[], writes=[pa[gp], sa])
            s.op("dve", lambda e: e.tensor_tensor(out=uT[:, grp[0]:grp[0] + len(grp), :].rearrange("p c t -> p (c t)"),
                                                  in0=sa[:, 0:n_], in1=pb[gp][:, 0:n_], op=ALU.mult),
                 reads=[sa], writes=[pb[gp], (uT, gi)])
        for h in range(2):
            for c in range(NFF):
                s.op("pe", lambda e: e.matmul(py[:, h, :], lhsT=uT[:, c, :], rhs=w2[:, c, h * 512:(h + 1) * 512],
                                              start=(c == 0), stop=(c == NFF - 1)),
                     reads=[(uT, g_) for g_ in range(len(grps))] + wres, writes=[py])
        s.op("dve", lambda e: e.tensor_tensor(out=xt[p][:], in0=py[:].rearrange("p a b -> p (a b)"), in1=mod[:, 3 * D:4 * D], op=ALU.mult),
             reads=modres, writes=[py, xt[p]])
        s.op("pool", lambda e: e.tensor_tensor(out=xt[p][:], in0=xt[p][:], in1=x1[p][:], op=ALU.add), reads=[x1[p]], writes=[xt[p]])
        if last:
            s.op("act", lambda e: e.activation(out=junk, in_=xt[p][:], func=AF.Square, accum_out=ss[p][:]), reads=[xt[p]], writes=[sa, ss[p]])
            emit_rstd(s, ss[p][:], rstd[p][:], D, [ss[p]], [rstd[p]])
            s.op("dve", lambda e: e.scalar_tensor_tensor(out=xt[p][:], in0=xt[p][:], scalar=rstd[p][:, 0:1], in1=gfin[:],
                                                         op0=ALU.mult, op1=ALU.mult), reads=[rstd[p], gfin], writes=[xt[p]])
        s.dma("sp", xout[i * 128:(i + 1) * 128, :], xt[p][:], reads=[xt[p]], writes=[("xout", i)], sem="D_xout")

    def run_tiles(tile_ids):
        stage_a(0, tile_ids[0])
        for n, i in enumerate(tile_ids):
            if n + 1 < len(tile_ids):
                stage_a(n + 1, tile_ids[n + 1])
            stage_b(n, i)

    for stream in range(2 if do_ctx else 1):
        s.dma("sp", mod[:], modd[stream], reads=[("modd", stream)], writes=modres)
        run_tiles(list(range(32)) if stream == 0 else [32, 33])
    s.pop()


def build_fused():
    nc = bass.Bass("TRN2", target_bir_lowering=False)
    di = lambda n, sh, dt_=F32: nc.dram_tensor(n, sh, dt_, kind="ExternalInput").ap()
    it = lambda n, sh, dt_=BF16: nc.dram_tensor(n, sh, dt_, kind="Internal").ap()
    xs_in = di("xs", [NTOK, D]); cvec = di("cvec", [2, D]); rope = di("rope", [NTOK, 192])
    cmask = di("cmask", [128, 8 * 512], BF16); sel = di("sel", [128, 8])
    w_ada = di("w_ada", [DEPTH, D, 6 * D]); b_ada = di("b_ada", [DEPTH, 6 * D]); g_attn = di("g_attn", [DEPTH, D])
    g_ffn = di("g_ffn", [DEPTH, D]); w_in = di("w_in", [DEPTH, D, 2048]); g_qk = di("g_qk", [DEPTH, 2, 64])
    lamv = di("lamv", [DEPTH, 4, 32]); sinkv = di("sinkv", [DEPTH, 6]); gsub = di("gsub", [DEPTH, 64])
    w_out = di("w_out", [DEPTH, D, D]); w_ff1 = di("w_ff1", [DEPTH, D, DFF]); w_ff3 = di("w_ff3", [DEPTH, D, DFF])
    w_ff2 = di("w_ff2", [DEPTH, DFF, D]); g_final = di("g_final", [D])
    xout = nc.dram_tensor("xout", [NLAT, D], F32, kind="ExternalOutput").ap()
    qT = it("qT_s", [1024, NTOK]); kT = it("kT_s", [512, NLAT]); vv = [it("vv_s%d" % j, [128, NLAT]) for j in range(4)]
    ckT = it("ckT_s", [512, NCTX]); cvv = it("cvv_s", [128, 1024]); oT = it("oT_s", [1024, NTOK])
    kTg = [it("kTg%d" % j, [512, NLAT]) for j in range(4)]
    vg = [it("vg%d" % j, [512, NLAT]) for j in range(4)]
    xs1 = it("xs1_s", [NTOK, D], F32)
    modd = it("modd_s", [2, 128, 4 * D], F32)
    wob = it("wob_s", [D, D]); w1b = it("w1b_s", [D, DFF]); w3b = it("w3b_s", [D, DFF]); w2b = it("w2b_s", [DFF, D])
    kg = lambda r, krow0: kTg[krow0 // 128][r * 128 + krow0 % 128:r * 128 + krow0 % 128 + 64, :]
    vgf = lambda r, unit: vg[unit // 2][r * 128:(r + 1) * 128, (unit % 2) * 2048:(unit % 2 + 1) * 2048]
    s = Sched(nc)
    consts = emit_consts(s)
    xs = xs_in
    for l in range(DEPTH):
        last = l == DEPTH - 1
        emit_pre(s, consts, xs, cvec, w_ada[l], b_ada[l], g_attn[l], w_in[l], g_qk[l], rope, qT, kT, vv, ckT, cvv)
        for j in range(4):
            s.collective(kT[j * 128:(j + 1) * 128, :], kTg[j], reads=["kT"], writes=[("kTg", j)], sem="CCk%d" % j)
            s.collective(vv[j], vg[j], reads=["vv"], writes=[("vg", j)], sem="CCv%d" % j)
        s.defer_prefix = "CC"
        emit_ffn_mod(s, consts, not last, cvec, w_ada[l], b_ada[l], g_ffn[l], modd)
        s.defer_prefix = "\0"
        s.barrier()
        for wi, (dst, src) in enumerate(((wob, w_out[l]), (w1b, w_ff1[l]), (w3b, w_ff3[l]), (w2b, w_ff2[l]))):
            s.dma("pool", dst, src, writes=[("wcast", wi)], sem="D_wcast%d" % wi)
        vown = lambda unit: vv[unit // 2][:, (unit % 2) * 2048:(unit % 2 + 1) * 2048]
        emit_attn(s, consts, l, not last, qT, kg, vgf, kT, vown, ckT, cvv, sel, cmask, lamv[l], sinkv[l], gsub[l], oT)
        emit_ffn(s, consts, last, not last, xs, modd, wob, w1b, w3b, w2b, g_final, oT, xout if last else xs1)
        xs = xs1
    s.finish()
    return nc, s


_PROG = {}


def kernel(x, c, ctx, c_ctx, w_ada, b_ada, g_attn, g_ffn, w_in, g_q, g_k, lam_q1, lam_k1, lam_q2, lam_k2,
           g_subln, sink_logit, w_out, w_ff1, w_ff3, w_ff2, g_final):
    f = lambda a: np.ascontiguousarray(np.asarray(a, dtype=np.float32))
    x, c, ctx, c_ctx = f(x), f(c), f(ctx), f(c_ctx)
    cores = list(range(8))
    shared = dict(w_ada=f(w_ada), b_ada=f(b_ada), g_attn=f(g_attn), g_ffn=f(g_ffn), w_in=f(w_in),
                  g_qk=np.ascontiguousarray(np.stack([f(g_q), f(g_k)], axis=1)),
                  lamv=np.ascontiguousarray(np.stack([f(lam_q1), f(lam_k1), f(lam_q2), f(lam_k2)], axis=1)),
                  sinkv=f(sink_logit), gsub=f(g_subln), w_out=f(w_out), w_ff1=f(w_ff1), w_ff3=f(w_ff3), w_ff2=f(w_ff2),
                  g_final=f(g_final))
    maps = []
    for i in cores:
        b, q = i // 4, i % 4
        sel = np.zeros((128, 8), np.float32)
        if q > 0:
            sel[:, q - 1] = 1.0
        if q < 3:
            sel[:, 4 + q + 1] = 1.0
        maps.append(dict(xs=np.ascontiguousarray(np.concatenate([x[b, q * NLAT:(q + 1) * NLAT], ctx[b]], 0)),
                         cvec=np.ascontiguousarray(np.stack([c[b], c_ctx])), rope=rope_tables(i), cmask=cmask_table(i), sel=sel,
                         **shared))
    if "fused" not in _PROG:
        _PROG["fused"] = build_fused()[0]
    res = run_bass_kernel_spmd(_PROG["fused"], maps, core_ids=cores).results
    out = np.empty((2, SEQ, D), np.float32)
    for i in cores:
        out[i // 4, (i % 4) * NLAT:(i % 4 + 1) * NLAT] = np.asarray(res[i]["xout"])
    return out
```
